# Optimizing a Trainium2 kernel written in Bass

```python
import jax
import jax.numpy as jnp
from jax import lax
import numpy as np

D_MODEL = 1024
BATCH = 2
SEQ = 8192
DEPTH = 2

GRID_W = 64
CTX_LEN = 256
N_EVEN = (DEPTH + 1) // 2
N_ODD = DEPTH // 2
EPS = 1e-6
N_MOD = 9
D_FF = 2816
FFN_RESIDUAL = 0.5

MLA_HEADS = 8
MLA_Q_RANK = 384
MLA_KV_RANK = 256
MLA_NOPE = 64
MLA_ROPE = 32
MLA_V = 64
MLA_QK = MLA_NOPE + MLA_ROPE
ROPE_AXIS_DIM = MLA_ROPE // 2
ROPE_BASE = 10000.0
Q_BLOCK = 128

MLSTM_HEADS = 4
MLSTM_DH = 128
MLSTM_WIDTH = MLSTM_HEADS * MLSTM_DH
MLSTM_CHUNK = 64

MIX_WIDTH = MLA_HEADS * MLA_V + MLSTM_WIDTH
EVEN_SPLITS = (MLA_Q_RANK, MLA_KV_RANK, MLA_ROPE, MLSTM_WIDTH, MLSTM_WIDTH, MLSTM_WIDTH, MLSTM_WIDTH, 4 * MLSTM_HEADS)
IN_EVEN = MLA_Q_RANK + MLA_KV_RANK + MLA_ROPE + 4 * MLSTM_WIDTH + 4 * MLSTM_HEADS

RNN_WIDTH = 1024
RNN_BLOCKS = 8
RNN_BLOCK_DIM = RNN_WIDTH // RNN_BLOCKS
CONV_W = 4
CONV_PAD_L = CONV_W // 2
CONV_PAD_R = CONV_W - 1 - CONV_PAD_L
LRU_C = 8.0

kernel_name = 'hybrid_mla_mlstm_rglru_prefix_block'


def rms_norm(x, g):
    xf = x.astype(jnp.float32)
    y = xf * lax.rsqrt(jnp.mean(xf * xf, axis=-1, keepdims=True) + EPS)
    return (y * g.astype(jnp.float32)).astype(x.dtype)


def ada_params(cond, w, b):
    m = jax.nn.silu(cond) @ w + b
    return jnp.split(m, N_MOD, axis=-1)


def modulate(h, shift, scale):
    return h * (1.0 + scale) + shift


def swiglu(h, w_gate, w_up, w_down):
    return (jax.nn.silu(h @ w_gate) * (h @ w_up)) @ w_down


def split_cols(p, sizes):
    idx, acc = [], 0
    for s in sizes[:-1]:
        acc += s
        idx.append(acc)
    return jnp.split(p, idx, axis=-1)


def axial_rope_tables(n_rows):
    rows = jnp.repeat(jnp.arange(n_rows), GRID_W)
    cols = jnp.tile(jnp.arange(GRID_W), n_rows)
    inv = ROPE_BASE ** (-jnp.arange(0, ROPE_AXIS_DIM, 2, dtype=jnp.float32) / ROPE_AXIS_DIM)
    ang = jnp.stack([rows, cols], axis=-1).astype(jnp.float32)[..., None] * inv
    return jnp.cos(ang), jnp.sin(ang)


def apply_axial_rope(x, cos, sin):
    shp = x.shape
    xr = x.astype(jnp.float32).reshape(shp[:-1] + (2, 2, ROPE_AXIS_DIM // 2))
    x1, x2 = xr[..., 0, :], xr[..., 1, :]
    out = jnp.stack([x1 * cos - x2 * sin, x2 * cos + x1 * sin], axis=-2)
    return out.reshape(shp).astype(x.dtype)


def mla_qkv(c_q, c_kv, k_rope, cq_g, w_uq, ckv_g, w_ukv, q_g, k_g, cos, sin):
    B, T, _ = c_q.shape
    q = (rms_norm(c_q, cq_g) @ w_uq).reshape(B, T, MLA_HEADS, MLA_QK)
    kv = (rms_norm(c_kv, ckv_g) @ w_ukv).reshape(B, T, MLA_HEADS, MLA_NOPE + MLA_V)
    q_nope = rms_norm(q[..., :MLA_NOPE], q_g[:MLA_NOPE])
    q_rope = rms_norm(q[..., MLA_NOPE:], q_g[MLA_NOPE:])
    k_nope = rms_norm(kv[..., :MLA_NOPE], k_g[:MLA_NOPE])
    k_rope = rms_norm(k_rope, k_g[MLA_NOPE:])
    v = kv[..., MLA_NOPE:]
    if cos is not None:
        q_rope = apply_axial_rope(q_rope, cos[:, None], sin[:, None])
        k_rope = apply_axial_rope(k_rope, cos, sin)
    k_rope = jnp.broadcast_to(k_rope[:, :, None, :], (B, T, MLA_HEADS, MLA_ROPE))
    q = jnp.concatenate([q_nope, q_rope], axis=-1).transpose(0, 2, 1, 3)
    k = jnp.concatenate([k_nope, k_rope], axis=-1).transpose(0, 2, 1, 3)
    return q, k, v.transpose(0, 2, 1, 3)


def attend(q, k, v):
    s = jnp.einsum('bhqd,bhkd->bhqk', q, k, preferred_element_type=jnp.float32) * (MLA_QK ** -0.5)
    p = jax.nn.softmax(s, axis=-1).astype(v.dtype)
    return jnp.einsum('bhqk,bhkd->bhqd', p, v)


def blocked_attention(q, k, v):
    B, H, T, dq = q.shape
    qb = jnp.moveaxis(q.reshape(B, H, T // Q_BLOCK, Q_BLOCK, dq), 2, 0)
    out = lax.map(lambda qi: attend(qi, k, v), qb)
    return jnp.moveaxis(out, 0, 2).reshape(B, H, T, -1)


def heads_to_tokens(a):
    B, H, T, d = a.shape
    return a.transpose(0, 2, 1, 3).reshape(B, T, H * d)


def mlstm_chunkwise(q, k, v, log_i, log_f, state0):
    B, H, T, dh = q.shape
    L = MLSTM_CHUNK
    nc = T // L

    def chunks(a):
        return jnp.moveaxis(a.reshape((B, H, nc, L) + a.shape[3:]), 2, 0)

    lower = jnp.tril(jnp.ones((L, L), dtype=bool))

    def step(carry, inp):
        C, n, m = carry
        qc, kc, vc, ic, fc = inp
        b = jnp.cumsum(fc, axis=-1)
        d = jnp.where(lower, b[..., :, None] - b[..., None, :] + ic[..., None, :], -jnp.inf)
        m_t = jnp.maximum(b + m[..., None], jnp.max(d, axis=-1))
        w_intra = jnp.exp(d - m_t[..., None])
        w_inter = jnp.exp(b + m[..., None] - m_t)
        s = jnp.einsum('bhtd,bhsd->bhts', qc, kc) * w_intra
        num = jnp.einsum('bhts,bhsd->bhtd', s, vc) + w_inter[..., None] * jnp.einsum('bhvk,bhtk->bhtv', C, qc)
        den = jnp.sum(s, axis=-1) + w_inter * jnp.einsum('bhk,bhtk->bht', n, qc)
        h = num / jnp.maximum(jnp.abs(den), jnp.exp(-m_t))[..., None]
        b_end = b[..., -1]
        g = b_end[..., None] - b + ic
        m_new = jnp.maximum(b_end + m, jnp.max(g, axis=-1))
        w_s = jnp.exp(g - m_new[..., None])
        decay = jnp.exp(b_end + m - m_new)
        C_new = decay[..., None, None] * C + jnp.einsum('bhs,bhsv,bhsk->bhvk', w_s, vc, kc)
        n_new = decay[..., None] * n + jnp.einsum('bhs,bhsk->bhk', w_s, kc)
        return (C_new, n_new, m_new), h

    state, hs = lax.scan(step, state0, (chunks(q), chunks(k), chunks(v), chunks(log_i), chunks(log_f)))
    return jnp.moveaxis(hs, 0, 2).reshape(B, H, T, dh), state


def mlstm_mixer(qx, kx, vx, ox, gx, qc, kc, vc, oc, gc, gate_b, out_g, need_ctx):
    def heads(a):
        B, T, _ = a.shape
        return a.astype(jnp.float32).reshape(B, T, MLSTM_HEADS, MLSTM_DH).transpose(0, 2, 1, 3)

    def gates(g):
        B, T, _ = g.shape
        g = (g.astype(jnp.float32) + gate_b.astype(jnp.float32)).reshape(B, T, 4, MLSTM_HEADS).transpose(2, 0, 3, 1)
        return g[0], jax.nn.log_sigmoid(g[1]), g[2], jax.nn.log_sigmoid(g[3])

    def rev(a):
        return jnp.flip(a, axis=2)

    def readout(h, o):
        B, H, T, d = h.shape
        hn = rms_norm(h.transpose(0, 2, 1, 3), out_g.reshape(MLSTM_HEADS, MLSTM_DH))
        gate = jax.nn.sigmoid(o.astype(jnp.float32)).reshape(B, T, H, d)
        return (hn * gate).reshape(B, T, H * d).astype(o.dtype)

    k_scale = MLSTM_DH ** -0.5
    Qx, Kx, Vx = heads(qx), heads(kx) * k_scale, heads(vx)
    Qc, Kc, Vc = heads(qc), heads(kc) * k_scale, heads(vc)
    ifx, lffx, ibx, lfbx = gates(gx)
    ifc, lffc, ibc, lfbc = gates(gc)
    B = qx.shape[0]
    zero = (jnp.zeros((B, MLSTM_HEADS, MLSTM_DH, MLSTM_DH), jnp.float32),
            jnp.zeros((B, MLSTM_HEADS, MLSTM_DH), jnp.float32),
            jnp.zeros((B, MLSTM_HEADS), jnp.float32))
    hcf, st_f = mlstm_chunkwise(Qc, Kc, Vc, ifc, lffc, zero)
    hcb, st_b = mlstm_chunkwise(rev(Qc), rev(Kc), rev(Vc), rev(ibc), rev(lfbc), zero)
    hxf, _ = mlstm_chunkwise(Qx, Kx, Vx, ifx, lffx, st_f)
    hxb, _ = mlstm_chunkwise(rev(Qx), rev(Kx), rev(Vx), rev(ibx), rev(lfbx), st_b)
    out_x = readout(hxf + rev(hxb), ox)
    out_c = readout(hcf + rev(hcb), oc) if need_ctx else None
    return out_x, out_c


def even_mixer(hx, hc, w_in, w_out, cq_g, w_uq, ckv_g, w_ukv, q_g, k_g, gate_b, out_g, cos, sin, need_ctx):
    cq_x, ckv_x, kr_x, mq_x, mk_x, mv_x, mo_x, mg_x = split_cols(hx @ w_in, EVEN_SPLITS)
    cq_c, ckv_c, kr_c, mq_c, mk_c, mv_c, mo_c, mg_c = split_cols(hc @ w_in, EVEN_SPLITS)
    q_x, k_x, v_x = mla_qkv(cq_x, ckv_x, kr_x, cq_g, w_uq, ckv_g, w_ukv, q_g, k_g, cos, sin)
    q_c, k_c, v_c = mla_qkv(cq_c, ckv_c, kr_c, cq_g, w_uq, ckv_g, w_ukv, q_g, k_g, None, None)
    k_all = jnp.concatenate([k_c, k_x], axis=2)
    v_all = jnp.concatenate([v_c, v_x], axis=2)
    att_x = heads_to_tokens(blocked_attention(q_x, k_all, v_all))
    ml_x, ml_c = mlstm_mixer(mq_x, mk_x, mv_x, mo_x, mg_x, mq_c, mk_c, mv_c, mo_c, mg_c, gate_b, out_g, need_ctx)
    y_x = jnp.concatenate([att_x, ml_x], axis=-1) @ w_out
    y_c = None
    if need_ctx:
        att_c = heads_to_tokens(attend(q_c, k_c, v_c))
        y_c = jnp.concatenate([att_c, ml_c], axis=-1) @ w_out
    return y_x, y_c


def short_conv(u, w, b):
    out = lax.conv_general_dilated(u, w[:, None, :].astype(u.dtype), window_strides=(1,),
                                   padding=[(CONV_PAD_L, CONV_PAD_R)],
                                   dimension_numbers=('NWC', 'WIO', 'NWC'),
                                   feature_group_count=u.shape[-1])
    return out + b


def _lin_combine(e1, e2):
    a1, b1 = e1
    a2, b2 = e2
    return a1 * a2, a2 * b1 + b2


def rglru_scan(u, w_a, b_a, w_x, b_x, lam, h0):
    B, T, R = u.shape
    uf = u.astype(jnp.float32)
    ub = uf.reshape(B, T, RNN_BLOCKS, RNN_BLOCK_DIM)
    r = jax.nn.sigmoid(jnp.einsum('btnd,nde->btne', ub, w_a.astype(jnp.float32)).reshape(B, T, R) + b_a.astype(jnp.float32))
    i = jax.nn.sigmoid(jnp.einsum('btnd,nde->btne', ub, w_x.astype(jnp.float32)).reshape(B, T, R) + b_x.astype(jnp.float32))
    log_a = -LRU_C * r * jax.nn.softplus(-lam.astype(jnp.float32))
    a = jnp.exp(log_a)
    inp = jnp.sqrt(-jnp.expm1(2.0 * log_a)) * (i * uf)
    a_cum, h_part = lax.associative_scan(_lin_combine, (a, inp), axis=1)
    h = a_cum * h0[:, None, :] + h_part
    return h, h[:, -1]


def odd_mixer(hx, hc, w_in, conv_w, conv_b, w_a, b_a, w_x, b_x, lam, w_out, need_ctx):
    gate_x, xr_x = jnp.split(hx @ w_in, 2, axis=-1)
    if need_ctx:
        gate_c, xr_c = jnp.split(hc @ w_in, 2, axis=-1)
    else:
        xr_c = hc @ w_in[:, RNN_WIDTH:]
    xc_x = short_conv(xr_x, conv_w, conv_b)
    xc_c = short_conv(xr_c, conv_w, conv_b)
    h0 = jnp.zeros((hx.shape[0], RNN_WIDTH), jnp.float32)
    hf_c, s_f = rglru_scan(xc_c, w_a[0], b_a[0], w_x[0], b_x[0], lam[0], h0)
    hb_c, s_b = rglru_scan(jnp.flip(xc_c, 1), w_a[1], b_a[1], w_x[1], b_x[1], lam[1], h0)
    hf_x, _ = rglru_scan(xc_x, w_a[0], b_a[0], w_x[0], b_x[0], lam[0], s_f)
    hb_x, _ = rglru_scan(jnp.flip(xc_x, 1), w_a[1], b_a[1], w_x[1], b_x[1], lam[1], s_b)
    y_x = ((hf_x + jnp.flip(hb_x, 1)).astype(hx.dtype) * jax.nn.gelu(gate_x)) @ w_out
    y_c = None
    if need_ctx:
        y_c = ((hf_c + jnp.flip(hb_c, 1)).astype(hc.dtype) * jax.nn.gelu(gate_c)) @ w_out
    return y_x, y_c


def setup_inputs(seed: int = 0) -> dict:
    key = jax.random.key(seed)
    ks = iter(jax.random.split(key, 32))

    def nrm(shape, scale):
        return scale * jax.random.normal(next(ks), shape, jnp.float32)

    x = nrm((BATCH, SEQ, D_MODEL), 1.0)
    c = nrm((BATCH, D_MODEL), 1.0)
    ctx = nrm((BATCH, CTX_LEN, D_MODEL), 1.0)
    c_ctx = nrm((D_MODEL,), 1.0)
    mod_w = nrm((DEPTH, D_MODEL, N_MOD * D_MODEL), 0.5 * D_MODEL ** -0.5)
    mod_b = nrm((DEPTH, N_MOD * D_MODEL), 0.02)
    norm_g = 1.0 + nrm((DEPTH, 3, D_MODEL), 0.02)
    ffn_w_gate = nrm((DEPTH, 2, D_MODEL, D_FF), D_MODEL ** -0.5)
    ffn_w_up = nrm((DEPTH, 2, D_MODEL, D_FF), D_MODEL ** -0.5)
    ffn_w_down = nrm((DEPTH, 2, D_FF, D_MODEL), D_FF ** -0.5)
    even_w_in = nrm((N_EVEN, D_MODEL, IN_EVEN), D_MODEL ** -0.5)
    even_w_out = nrm((N_EVEN, MIX_WIDTH, D_MODEL), MIX_WIDTH ** -0.5)
    mla_cq_g = 1.0 + nrm((N_EVEN, MLA_Q_RANK), 0.02)
    mla_w_uq = nrm((N_EVEN, MLA_Q_RANK, MLA_HEADS * MLA_QK), MLA_Q_RANK ** -0.5)
    mla_ckv_g = 1.0 + nrm((N_EVEN, MLA_KV_RANK), 0.02)
    mla_w_ukv = nrm((N_EVEN, MLA_KV_RANK, MLA_HEADS * (MLA_NOPE + MLA_V)), MLA_KV_RANK ** -0.5)
    mla_q_g = 1.0 + nrm((N_EVEN, MLA_QK), 0.02)
    mla_k_g = 1.0 + nrm((N_EVEN, MLA_QK), 0.02)
    i_bias = -1.0 + nrm((N_EVEN, 2, MLSTM_HEADS), 0.1)
    f_bias = jnp.linspace(3.0, 6.0, MLSTM_HEADS, dtype=jnp.float32) + nrm((N_EVEN, 2, MLSTM_HEADS), 0.1)
    mlstm_gate_b = jnp.stack([i_bias, f_bias], axis=2).reshape(N_EVEN, 4 * MLSTM_HEADS)
    mlstm_out_g = 1.0 + nrm((N_EVEN, MLSTM_WIDTH), 0.02)
    odd_w_in = nrm((N_ODD, D_MODEL, 2 * RNN_WIDTH), D_MODEL ** -0.5)
    odd_conv_w = nrm((N_ODD, CONV_W, RNN_WIDTH), CONV_W ** -0.5)
    odd_conv_b = nrm((N_ODD, RNN_WIDTH), 0.02)
    lru_w_a = nrm((N_ODD, 2, RNN_BLOCKS, RNN_BLOCK_DIM, RNN_BLOCK_DIM), RNN_BLOCK_DIM ** -0.5)
    lru_b_a = nrm((N_ODD, 2, RNN_WIDTH), 0.02)
    lru_w_x = nrm((N_ODD, 2, RNN_BLOCKS, RNN_BLOCK_DIM, RNN_BLOCK_DIM), RNN_BLOCK_DIM ** -0.5)
    lru_b_x = nrm((N_ODD, 2, RNN_WIDTH), 0.02)
    u = jax.random.uniform(next(ks), (N_ODD, 2, RNN_WIDTH), jnp.float32, 0.9, 0.999)
    a0 = u ** (1.0 / LRU_C)
    lru_lam = jnp.log(a0) - jnp.log1p(-a0)
    odd_w_out = nrm((N_ODD, RNN_WIDTH, D_MODEL), RNN_WIDTH ** -0.5)
    return {'x': x, 'c': c, 'ctx': ctx, 'c_ctx': c_ctx,
            'mod_w': mod_w, 'mod_b': mod_b, 'norm_g': norm_g,
            'ffn_w_gate': ffn_w_gate, 'ffn_w_up': ffn_w_up, 'ffn_w_down': ffn_w_down,
            'even_w_in': even_w_in, 'even_w_out': even_w_out,
            'mla_cq_g': mla_cq_g, 'mla_w_uq': mla_w_uq, 'mla_ckv_g': mla_ckv_g, 'mla_w_ukv': mla_w_ukv,
            'mla_q_g': mla_q_g, 'mla_k_g': mla_k_g,
            'mlstm_gate_b': mlstm_gate_b, 'mlstm_out_g': mlstm_out_g,
            'odd_w_in': odd_w_in, 'odd_conv_w': odd_conv_w, 'odd_conv_b': odd_conv_b,
            'lru_w_a': lru_w_a, 'lru_b_a': lru_b_a, 'lru_w_x': lru_w_x, 'lru_b_x': lru_b_x,
            'lru_lam': lru_lam, 'odd_w_out': odd_w_out}


def reference(x, c, ctx, c_ctx, mod_w, mod_b, norm_g, ffn_w_gate, ffn_w_up, ffn_w_down,
              even_w_in, even_w_out, mla_cq_g, mla_w_uq, mla_ckv_g, mla_w_ukv, mla_q_g, mla_k_g,
              mlstm_gate_b, mlstm_out_g, odd_w_in, odd_conv_w, odd_conv_b,
              lru_w_a, lru_b_a, lru_w_x, lru_b_x, lru_lam, odd_w_out):
    n_rows = x.shape[1] // GRID_W
    cos, sin = axial_rope_tables(n_rows)
    for layer in range(DEPTH):
        need_ctx = layer < DEPTH - 1
        j = layer // 2
        mx = [m[:, None, :] for m in ada_params(c, mod_w[layer], mod_b[layer])]
        mc = ada_params(c_ctx, mod_w[layer], mod_b[layer])
        x = x + FFN_RESIDUAL * mx[2] * swiglu(modulate(rms_norm(x, norm_g[layer, 0]), mx[0], mx[1]),
                                            ffn_w_gate[layer, 0], ffn_w_up[layer, 0], ffn_w_down[layer, 0])
        ctx = ctx + FFN_RESIDUAL * mc[2] * swiglu(modulate(rms_norm(ctx, norm_g[layer, 0]), mc[0], mc[1]),
                                                ffn_w_gate[layer, 0], ffn_w_up[layer, 0], ffn_w_down[layer, 0])
        hx = modulate(rms_norm(x, norm_g[layer, 1]), mx[3], mx[4])
        hc = modulate(rms_norm(ctx, norm_g[layer, 1]), mc[3], mc[4])
        if layer % 2 == 0:
            y_x, y_c = even_mixer(hx, hc, even_w_in[j], even_w_out[j], mla_cq_g[j], mla_w_uq[j], mla_ckv_g[j],
                                  mla_w_ukv[j], mla_q_g[j], mla_k_g[j], mlstm_gate_b[j], mlstm_out_g[j],
                                  cos, sin, need_ctx)
        else:
            y_x, y_c = odd_mixer(hx, hc, odd_w_in[j], odd_conv_w[j], odd_conv_b[j], lru_w_a[j], lru_b_a[j],
                                 lru_w_x[j], lru_b_x[j], lru_lam[j], odd_w_out[j], need_ctx)
        x = x + mx[5] * y_x
        x = x + FFN_RESIDUAL * mx[8] * swiglu(modulate(rms_norm(x, norm_g[layer, 2]), mx[6], mx[7]),
                                            ffn_w_gate[layer, 1], ffn_w_up[layer, 1], ffn_w_down[layer, 1])
        if need_ctx:
            ctx = ctx + mc[5] * y_c
            ctx = ctx + FFN_RESIDUAL * mc[8] * swiglu(modulate(rms_norm(ctx, norm_g[layer, 2]), mc[6], mc[7]),
                                                    ffn_w_gate[layer, 1], ffn_w_up[layer, 1], ffn_w_down[layer, 1])
    return x
```

```python
import contextlib
import numpy as np
import concourse.bass as bass
import concourse.mybir as mybir
from concourse.bass_utils import run_bass_kernel_spmd

F32 = mybir.dt.float32
BF16 = mybir.dt.bfloat16
AF = mybir.ActivationFunctionType
ALU = mybir.AluOpType
AX = mybir.AxisListType


class Buf:
    __slots__ = ("name", "writer", "readers", "excl", "nowaw")

    def __init__(self, name="", excl=False, nowaw=False):
        self.name = name
        self.nowaw = nowaw
        self.writer = None
        self.readers = []
        self.excl = excl


class Sem:
    __slots__ = ("h", "total", "dma", "name")

    def __init__(self, h, dma, name):
        self.h = h
        self.total = 0
        self.dma = dma
        self.name = name


class T:
    __slots__ = ("ap", "bufs")

    def __init__(self, ap, bufs):
        self.ap = ap
        self.bufs = bufs if isinstance(bufs, (list, tuple)) else [bufs]

    def __getitem__(self, idx):
        return T(self.ap[idx], self.bufs)

    def with_ap(self, ap):
        return T(ap, self.bufs)


ENGS = ("pe", "act", "dve", "pool", "sp")


class Prog:
    def __init__(self, nc, stack):
        self.nc = nc
        self.stack = stack
        self.sem_stack = stack
        self.ops = {e: [] for e in ENGS}
        self.esem = {e: self.new_sem("e_" + e, False) for e in ENGS}
        self.dsems = {}
        self.seen = {e: {} for e in ENGS}
        self.nops = 0

    def new_sem(self, name, dma):
        h = self.sem_stack.enter_context(self.nc.semaphore(name))
        return Sem(h, dma, name)

    def dsem(self, name):
        if name not in self.dsems:
            self.dsems[name] = self.new_sem("d_" + name, True)
        return self.dsems[name]

    def _record(self, eng, fn, reads, writes, sem=None):
        waits = {}

        def need(tok, kind):
            if tok is None:
                return
            s, v, peng = tok
            if not s.dma and peng == eng:
                if kind != "raw":
                    return
                if eng == "pe":
                    return
            if s.dma:
                v = max(v, s.total)
            if self.seen[eng].get(s, 0) >= v:
                return
            if waits.get(s, 0) < v:
                waits[s] = v

        rb = []
        xb = []
        for t in reads:
            if t is None or isinstance(t, (int, float)):
                continue
            for b in t.bufs:
                need(b.writer, "raw")
                if b.excl:
                    for r in b.readers:
                        need(r, "war")
                    xb.append(b)
                else:
                    rb.append(b)
        wb = []
        for t in writes:
            if t is None:
                continue
            for b in t.bufs:
                wb.append(b)
                if not b.nowaw:
                    need(b.writer, "waw")
                for r in b.readers:
                    need(r, "war")
        for s, v in waits.items():
            self.seen[eng][s] = v
        if sem is None:
            s = self.esem[eng]
            s.total += 1
        else:
            s = sem
            s.total += 16
        tok = (s, s.total, eng)
        for b in rb:
            b.readers.append(tok)
        for b in wb:
            b.writer = tok
            b.readers = []
        for b in xb:
            if b.writer is not tok:
                b.readers.append(tok)
                b.writer = (tok[0], tok[1], tok[2]) if False else b.writer
                b.writer = tok
                b.readers = []
        self.ops[eng].append((list(waits.items()), fn, s, 16 if s.dma else 1))
        self.nops += 1

    def barrier(self):
        sems = list(self.esem.values()) + list(self.dsems.values())
        for e in ENGS:
            waits = []
            for s in sems:
                if s.total > self.seen[e].get(s, 0) and not (s is self.esem[e]):
                    waits.append((s, s.total))
                    self.seen[e][s] = s.total
            if waits:
                self.ops[e].append((waits, None, None, 0))

    def emit(self):
        nc = self.nc
        ops = self.ops

        def run(eng, lst):
            for waits, fn, s, inc in lst:
                for ws, wv in waits:
                    eng.wait_ge(ws.h, wv)
                if fn is not None:
                    fn(eng).then_inc(s.h, inc)

        with nc.Block() as block:
            @block.tensor
            def _(e):
                run(e, ops["pe"])

            @block.scalar
            def _(e):
                run(e, ops["act"])

            @block.vector
            def _(e):
                run(e, ops["dve"])

            @block.gpsimd
            def _(e):
                run(e, ops["pool"])

            @block.sync
            def _(e):
                run(e, ops["sp"])

        self.ops = {e: [] for e in ENGS}

    def dma(self, q, out, in_, sem="ld", **kw):
        s = self.dsem(sem)
        self._record(q, lambda e: e.dma_start(out=out.ap, in_=in_.ap, **kw), [in_], [out], sem=s)

    def mm(self, out, lhsT, rhs, start=True, stop=True):
        self._record("pe", lambda e: e.matmul(out.ap, lhsT.ap, rhs.ap, start=start, stop=stop),
                     [lhsT, rhs], [out])

    def transpose(self, out, in_, ident):
        self._record("pe", lambda e: e.transpose(out.ap, in_.ap, ident.ap), [in_, ident], [out])

    def act(self, out, in_, func, bias=None, scale=1.0, accum=None, eng="act"):
        kw = {}
        if bias is not None:
            kw["bias"] = bias.ap if isinstance(bias, T) else bias
        kw["scale"] = scale.ap if isinstance(scale, T) else scale
        if accum is not None:
            kw["accum_out"] = accum.ap
        self._record("act", lambda e: e.activation(out.ap, in_.ap, func, **kw),
                     [in_, bias if isinstance(bias, T) else None, scale if isinstance(scale, T) else None],
                     [out, accum])

    def tt(self, eng, out, in0, in1, op):
        self._record(eng, lambda e: e.tensor_tensor(out.ap, in0.ap, in1.ap, op), [in0, in1], [out])

    def ts(self, eng, out, in0, s1, s2, op0, op1=None):
        a1 = s1.ap if isinstance(s1, T) else s1
        a2 = s2.ap if isinstance(s2, T) else s2
        if op1 is None:
            f = lambda e: e.tensor_scalar(out.ap, in0.ap, a1, None, op0)
        else:
            f = lambda e: e.tensor_scalar(out.ap, in0.ap, a1, a2, op0, op1)
        self._record(eng, f, [in0, s1 if isinstance(s1, T) else None, s2 if isinstance(s2, T) else None], [out])

    def stt(self, eng, out, in0, scalar, in1, op0, op1):
        a = scalar.ap if isinstance(scalar, T) else scalar
        self._record(eng, lambda e: e.scalar_tensor_tensor(out.ap, in0.ap, a, in1.ap, op0, op1),
                     [in0, in1, scalar if isinstance(scalar, T) else None], [out])

    def copy(self, eng, out, in_):
        if eng == "act":
            self._record(eng, lambda e: e.copy(out.ap, in_.ap), [in_], [out])
        else:
            self._record(eng, lambda e: e.tensor_copy(out.ap, in_.ap), [in_], [out])

    def memset(self, eng, out, val):
        self._record(eng, lambda e: e.memset(out.ap, val), [], [out])

    def recip(self, out, in_):
        self._record("dve", lambda e: e.reciprocal(out.ap, in_.ap), [in_], [out])

    def scan(self, out, d0, d1, init, op0, op1):
        a = init.ap if isinstance(init, T) else init
        self._record("dve", lambda e: e.tensor_tensor_scan(out.ap, d0.ap, d1.ap, a, op0, op1),
                     [d0, d1, init if isinstance(init, T) else None], [out])


D = 1024
DFF = 2816
NFF = 22
EPS = 1e-6


class Ctx:
    pass


WDRAM = Buf("wdram")


_UID = [0]


def uname(name):
    _UID[0] += 1
    return f"{name}_u{_UID[0]}"


def sbt(P, name, shape, dt, nb=None):
    h = P.stack.enter_context(P.nc.sbuf_tensor(uname("s_" + name), shape, dt))
    return T(h[tuple(slice(None) for _ in shape)], Buf(name))


def setup_common(P, NTL, NTC):
    nc = P.nc
    C = Ctx()
    C.NTL, C.NTC = NTL, NTC
    NT = NTL + NTC
    C.NT = NT
    C.blocks = [(s, 512, False) for s in range(0, NTL, 512)] + ([(NTL, NTC, True)] if NTC else [])
    xh = P.stack.enter_context(nc.sbuf_tensor(uname("X"), [128, 8, NT], F32))
    C.xh = xh
    hh = P.stack.enter_context(nc.sbuf_tensor(uname("hT"), [128, 8, NT], BF16))
    C.X = [[T(xh[:, k, s:s + n], Buf(f"X{bi}_{k}")) for k in range(8)] for bi, (s, n, c) in enumerate(C.blocks)]
    C.H = [[T(hh[:, k, s:s + n], Buf(f"H{bi}_{k}")) for k in range(8)] for bi, (s, n, c) in enumerate(C.blocks)]
    C.ps = []
    for i in range(8):
        ph = P.stack.enter_context(nc.psum_tensor(uname(f"ps{i}"), [128, 512], F32))
        C.ps.append(T(ph[:, :], Buf(f"ps{i}", excl=True)))
    C.ones_bf = sbt(P, "ones_bf", [128, 128], BF16)
    P.memset("pool", C.ones_bf, 1.0)
    C.NSLOT = 6
    C.SLOTSZ = 4096
    wh = P.stack.enter_context(nc.sbuf_tensor(uname("wslots"), [128, C.NSLOT, C.SLOTSZ], BF16))
    C.wh = wh
    C.wslots = [T(wh[:, i, :], Buf(f"wslot{i}")) for i in range(C.NSLOT)]
    C.wi = 0
    C.wq = []
    C.wloaded = []
    C.sq = [sbt(P, f"sq{i}", [128, 512], BF16) for i in range(2)]
    C.rstd = [sbt(P, f"rstd{i}", [128, 512], F32) for i in range(2)]
    C.tmpf = [sbt(P, f"tmpf{i}", [128, 512], F32) for i in range(3)]
    C.sg = [sbt(P, f"sg{i}", [128, 512], F32) for i in range(2)]
    C.actb = [[sbt(P, f"act{i}_{j}", [128, 512], BF16) for j in range(4)] for i in range(2)]
    C.cnt = 0
    return C


def wq_push(C, dram_ap, K, cols):
    C.wq.append((dram_ap, K, cols))


def wq_issue(P, C):
    if not C.wq:
        return
    dram_ap, K, cols = C.wq.pop(0)
    si = C.wi % C.NSLOT
    C.wi += 1
    slot = C.wslots[si]
    view = slot.with_ap(slot.ap[:, 0:K * cols].rearrange("p (k c) -> p k c", k=K))
    P.dma("pool", view, T(dram_ap, WDRAM), sem=f"w{si}")
    C.wloaded.append(view)


def wq_prefetch(P, C, n):
    for _ in range(n):
        wq_issue(P, C)


def wq_get(P, C):
    return C.wloaded.pop(0)


def norm_mod(P, C, bi, A, B):
    s, n, isc = C.blocks[bi]
    i = C.cnt
    C.cnt += 1
    pss = C.ps[6]
    sq = C.sq
    for k in range(8):
        q = sq[k % 2]
        P.act(q[:, :n], C.X[bi][k], AF.Square)
        P.mm(pss[:, :n], C.ones_bf, q[:, :n], start=(k == 0), stop=(k == 7))
    r = C.rstd[i % 2]
    P.ts("dve", r[:, :n], pss[:, :n], 1.0 / D, EPS, ALU.mult, ALU.add)
    P.act(r[:, :n], r[:, :n], AF.Sqrt)
    P.recip(r[:, :n], r[:, :n])
    for k in range(8):
        t = C.tmpf[k % 3]
        P.tt("dve", t[:, :n], C.X[bi][k], r[:, :n], ALU.mult)
        P.act(C.H[bi][k], t[:, :n], AF.Identity, bias=B[:, k:k + 1], scale=A[:, k:k + 1])


def ffn(P, C, wg_d, wu_d, wd_d, mods, groups=(4, 4, 4, 4, 4, 2)):
    for bi in range(len(C.blocks)):
        isc = C.blocks[bi][2]
        norm_mod(P, C, bi, mods[isc][0], mods[isc][1])
    j0 = 0
    for g in groups:
        wq_push(C, wg_d[:, :, j0 * 128:(j0 + g) * 128], 8, g * 128)
        wq_push(C, wu_d[:, :, j0 * 128:(j0 + g) * 128], 8, g * 128)
        wq_push(C, wd_d[:, j0:j0 + g, :], g, 1024)
        j0 += g
    items = []
    for g in groups:
        for bi in range(len(C.blocks)):
            items.append((g, bi))
    state = {}

    def gateup(it, idx):
        g, bi = it
        s, n, isc = C.blocks[bi]
        if bi == 0:
            state["w"] = (wq_get(P, C), wq_get(P, C), wq_get(P, C))
        wg, wu, wd = state["w"]
        acts = C.actb[idx % 2]
        for jj in range(g):
            pg = C.ps[(2 * jj) % 4]
            pu = C.ps[(2 * jj + 1) % 4]
            for k in range(8):
                P.mm(pg[:, :n], wg[:, k, jj * 128:(jj + 1) * 128], C.H[bi][k], start=(k == 0), stop=(k == 7))
            for k in range(8):
                P.mm(pu[:, :n], wu[:, k, jj * 128:(jj + 1) * 128], C.H[bi][k], start=(k == 0), stop=(k == 7))
            sg = C.sg[jj % 2]
            P.act(sg[:, :n], pg[:, :n], AF.Silu)
            P.tt("dve", acts[jj][:, :n], sg[:, :n], pu[:, :n], ALU.mult)
        return (g, bi, wd, acts)

    def down(st):
        g, bi, wd, acts = st
        s, n, isc = C.blocks[bi]
        G = mods[isc][2]
        for m in range(8):
            py = C.ps[4 + m % 2]
            for jj in range(g):
                P.mm(py[:, :n], wd[:, jj, m * 128:(m + 1) * 128], acts[jj][:, :n], start=(jj == 0), stop=(jj == g - 1))
            P.stt("dve", C.X[bi][m], py[:, :n], G[:, m:m + 1], C.X[bi][m], ALU.mult, ALU.add)

    prev = None
    nb = len(C.blocks)
    wq_prefetch(P, C, 6)
    for idx, it in enumerate(items):
        cur = gateup(it, idx)
        if prev is not None:
            down(prev)
            if prev[1] == nb - 1:
                wq_prefetch(P, C, 3)
        prev = cur
    down(prev)


import ml_dtypes

NPBF = ml_dtypes.bfloat16


class DR:
    def __init__(self, nc):
        self.nc = nc
        self.bufs = {}

    def inp(self, name, shape, dt=F32):
        if name not in self.bufs:
            ap = self.nc.dram_tensor(name, list(shape), dt, kind="ExternalInput").ap()
            self.bufs[name] = T(ap, Buf(name))
        return self.bufs[name]

    def out(self, name, shape, dt=F32):
        if name not in self.bufs:
            ap = self.nc.dram_tensor(name, list(shape), dt, kind="ExternalOutput").ap()
            self.bufs[name] = T(ap, Buf(name, nowaw=True))
        return self.bufs[name]

    def tmp(self, name, shape, dt=F32):
        if name not in self.bufs:
            ap = self.nc.dram_tensor(name, list(shape), dt).ap()
            self.bufs[name] = T(ap, Buf(name, nowaw=True))
        return self.bufs[name]


def launch(build, in_maps):
    nc = bass.Bass("TRN2", target_bir_lowering=False)
    with contextlib.ExitStack() as stack:
        P = Prog(nc, stack)
        build(P, DR(nc))
        P.barrier()
        P.emit()
    res = run_bass_kernel_spmd(nc, in_maps, core_ids=list(range(8)))
    return res.results


def fm(v):
    v = np.asarray(v, np.float32)
    return np.ascontiguousarray(v.reshape(-1, 128).T)


def build_M(P, dr):
    modw = dr.inp("modw", [2, 1024, 1152])
    modb = dr.inp("modb", [128, 18])
    cv = dr.inp("cv", [128, 24])
    outd = dr.out("mod_out", [128, 54])
    cvt = sbt(P, "cvt", [128, 8, 3], F32)
    scv = sbt(P, "scv", [128, 8, 3], F32)
    mb = sbt(P, "mb", [128, 18], F32)
    res = sbt(P, "res", [128, 18, 3], F32)
    P.dma("sp", cvt, cv.with_ap(cv.ap.rearrange("p (k v) -> p k v", v=3)), sem="c")
    P.dma("sp", mb, modb, sem="c")
    P.act(scv, cvt, AF.Silu)
    ph = P.stack.enter_context(P.nc.psum_tensor(uname("psm"), [128, 18, 3], F32))
    ps = T(ph[:, :, :], Buf("psm", excl=True))
    ws = [sbt(P, f"mw{i}", [128, 8, 384], F32) for i in range(2)]
    n = 0
    for l in range(2):
        wv = modw.ap[l].rearrange("(k p) c -> p k c", p=128)
        for pc in range(3):
            w = ws[n % 2]
            P.dma("sp", w, modw.with_ap(wv[:, :, pc * 384:(pc + 1) * 384]), sem=f"mw{n % 2}")
            n += 1
            for jj in range(3):
                j = l * 9 + pc * 3 + jj
                for k in range(8):
                    P.mm(ps[:, j, :], w[:, k, jj * 128:(jj + 1) * 128], scv[:, k, :], start=(k == 0), stop=(k == 7))
    for v in range(3):
        P.tt("dve", res[:, :, v], ps[:, :, v], mb, ALU.add)
    P.dma("sp", outd, res.with_ap(res.ap.rearrange("p j v -> p (j v)")), sem="o")


def run_M(inp):
    c, c_ctx, mod_w, mod_b = inp["c"], inp["c_ctx"], inp["mod_w"], inp["mod_b"]
    vecs = np.stack([c[0], c[1], c_ctx], 0)
    cv = np.ascontiguousarray(vecs.reshape(3, 8, 128).transpose(2, 1, 0).reshape(128, 24))
    maps = []
    for r in range(8):
        cs = slice(1152 * r, 1152 * (r + 1))
        mbr = mod_b[:, cs].reshape(2, 9, 128).transpose(2, 0, 1).reshape(128, 18)
        maps.append({"modw": np.ascontiguousarray(mod_w[:, :, cs]), "modb": np.ascontiguousarray(mbr), "cv": cv})
    res = launch(build_M, maps)
    m = np.zeros((2, 3, 9216), np.float32)
    for r in range(8):
        o = res[r]["mod_out"].reshape(128, 2, 9, 3)
        for l in range(2):
            blk = o[:, l].transpose(2, 1, 0).reshape(3, 1152)
            m[l, :, 1152 * r:1152 * (r + 1)] = blk
    return m


def mod_table(m, l, b):
    t = np.zeros((128, 9, 8, 2), np.float32)
    for xc, v in enumerate((b, 2)):
        t[:, :, :, xc] = m[l, v].reshape(9, 8, 128).transpose(2, 0, 1)
    return np.ascontiguousarray(t.reshape(128, 144))


NTL = 2048
NTC = 256
NT = NTL + NTC


def load_mods(P, dr, C, tag=""):
    modt_d = dr.inp("modt" + tag, [128, 144])
    ng_d = dr.inp("normg" + tag, [128, 24])
    mt = sbt(P, "mt" + tag, [128, 9, 8, 2], F32)
    ng = sbt(P, "ng" + tag, [128, 3, 8], F32)
    P.dma("sp", mt, modt_d.with_ap(modt_d.ap.rearrange("p (i k x) -> p i k x", i=9, k=8)), sem="c")
    P.dma("sp", ng, ng_d.with_ap(ng_d.ap.rearrange("p (i k) -> p i k", i=3)), sem="c")
    ab = sbt(P, "modAB" + tag, [128, 3, 2, 3, 8], F32)
    mods = []
    for i in range(3):
        row = {}
        for xc in range(2):
            A = ab[:, i, xc, 0, :]
            B = ab[:, i, xc, 1, :]
            G = ab[:, i, xc, 2, :]
            P.ts("dve", A, mt[:, 3 * i + 1, :, xc], 1.0, None, ALU.add)
            P.tt("dve", A, A, ng[:, i, :], ALU.mult)
            P.copy("dve", B, mt[:, 3 * i, :, xc])
            P.ts("dve", G, mt[:, 3 * i + 2, :, xc], 0.5 if i != 1 else 1.0, None, ALU.mult)
            row[bool(xc)] = (A, B, G)
        mods.append(row)
    return mods


def ffn_w(dr, tag):
    wg = dr.inp("wg" + tag, [1024, 2816])
    wu = dr.inp("wu" + tag, [1024, 2816])
    wd = dr.inp("wd" + tag, [2816, 1024])
    return (wg.ap.rearrange("(k p) c -> p k c", p=128), wu.ap.rearrange("(k p) c -> p k c", p=128),
            wd.ap.rearrange("(j p) c -> p j c", p=128))


def load_X(P, C, xin, nblk=None):
    xv = xin.ap.rearrange("(k p) t -> p k t", p=128)
    for bi, (s, n, c) in enumerate(C.blocks):
        for k in range(8):
            P.dma("sp", C.X[bi][k], xin.with_ap(xv[:, k, s:s + n]), sem="xin")


def store_X(P, C, xout):
    xv = xout.ap.rearrange("(k p) t -> p k t", p=128)
    for bi, (s, n, c) in enumerate(C.blocks):
        for k in range(8):
            P.dma("sp", xout.with_ap(xv[:, k, s:s + n]), C.X[bi][k], sem="xout")


def wload(P, name, dram_ap, shape, dt=BF16, q=None):
    t = sbt(P, name, shape, dt)
    P.dma("pool" if dt == BF16 else "sp", t, T(dram_ap, WDRAM), sem="c")
    return t


def build_A(P, dr):
    C = setup_common(P, NTL, NTC)
    xin = dr.inp("xT", [1024, NT])
    mods = load_mods(P, dr, C)
    fw = ffn_w(dr, "")
    w_in = dr.inp("w_in", [1024, 2736])
    w_uq = dr.inp("w_uq", [384, 768])
    w_ukk = dr.inp("w_ukk", [256, 768])
    w_ukv = dr.inp("w_ukv", [256, 512])
    cqg_d = dr.inp("cqg", [128, 3])
    ckvg_d = dr.inp("ckvg", [128, 2])
    c96_d = dr.inp("c96", [96, 3])
    oblk_d = dr.inp("oblk", [96, 96])
    rm_d = dr.inp("rm", [96, 96])
    e32_d = dr.inp("e32", [32, 96])
    cos_d = dr.inp("cosT", [96, NT])
    sin_d = dr.inp("sinT", [96, NT])
    xo = dr.out("xTo", [1024, NT])
    QT = dr.out("QT", [8, 96, NT], BF16)
    KT = dr.out("KT", [8, 96, NT], BF16)
    Vo = dr.out("V", [NT, 512], BF16)
    MQ = dr.out("MQ", [NT, 512], BF16)
    MK = dr.out("MK", [NT, 512], BF16)
    MV = dr.out("MV", [NT, 512], BF16)
    OT = dr.out("OT", [512, NT], F32)
    GT = dr.out("GT", [NT, 16], F32)

    load_X(P, C, xin)
    ffn(P, C, fw[0], fw[1], fw[2], mods[0])
    store_X(P, C, xo)
    for bi in range(len(C.blocks)):
        isc = C.blocks[bi][2]
        norm_mod(P, C, bi, mods[1][isc][0], mods[1][isc][1])
    P.barrier()
    wslab = C.wh
    win = T(wslab[:, :, :].rearrange("p s c -> p (s c)")[:, 0:8 * 2736].rearrange("p (k c) -> p k c", k=8),
            [b for t in C.wslots for b in t.bufs])
    winv = w_in.ap.rearrange("(k p) c -> p k c", p=128)
    for k in range(8):
        P.dma("pool", win[:, k, :], T(winv[:, k, :], WDRAM), sem="win")
    wuq = wload(P, "wuq", w_uq.ap.rearrange("(k p) c -> p k c", p=128), [128, 3, 768])
    wukk = wload(P, "wukk", w_ukk.ap.rearrange("(k p) c -> p k c", p=128), [128, 2, 768])
    wukv = wload(P, "wukv", w_ukv.ap.rearrange("(k p) c -> p k c", p=128), [128, 2, 512])
    cqg = wload(P, "cqg", cqg_d.ap, [128, 3], F32)
    ckvg = wload(P, "ckvg", ckvg_d.ap, [128, 2], F32)
    c96 = wload(P, "c96", c96_d.ap, [96, 3], F32)
    oblk = wload(P, "oblk", oblk_d.ap, [96, 96])
    rm = wload(P, "rm", rm_d.ap, [96, 96], F32)
    e32 = wload(P, "e32", e32_d.ap, [32, 96])
    xh = C.xh
    XS = xh[:, :, :].rearrange("p k t -> p (k t)")
    off = [0]

    def xs(name, parts, n):
        a = XS[0:parts, off[0]:off[0] + n]
        off[0] += n
        return T(a, Buf(name))

    cq = [xs(f"cq{k}", 128, 512) for k in range(3)]
    ckv = [xs(f"ckv{k}", 128, 512) for k in range(2)]
    cosb = xs("cosb", 96, 512)
    sinb = xs("sinb", 96, 512)
    qn = [xs(f"qn{i}", 96, 512) for i in range(2)]
    t1 = [xs(f"t1{i}", 96, 512) for i in range(2)]
    t2 = [xs(f"t2{i}", 96, 512) for i in range(2)]
    rr = [xs(f"rr{i}", 128, 512) for i in range(2)]
    otile = [xs(f"ot{i}", 128, 512) for i in range(2)]
    gts = [xs(f"gts{i}", 128, 16) for i in range(2)]
    cqn = [sbt(P, f"cqn{k}", [128, 512], BF16) for k in range(3)]
    ckvn = [sbt(P, f"ckvn{k}", [128, 512], BF16) for k in range(2)]
    krr = sbt(P, "krr", [32, 512], BF16)
    sqb = [sbt(P, f"sqb{i}", [128, 512], BF16) for i in range(2)]
    qo = [sbt(P, f"qo{i}", [96, 512], BF16) for i in range(2)]
    tok = [sbt(P, f"tok{i}", [128, 512], BF16) for i in range(3)]
    ps = C.ps
    cnt = [0]

    def nxt():
        cnt[0] += 1
        return cnt[0]

    def rstd_from(pss, n, parts, invn, out):
        P.ts("dve", out[:parts, :n], pss[:parts, :n], invn, EPS, ALU.mult, ALU.add)
        P.act(out[:parts, :n], out[:parts, :n], AF.Sqrt)
        P.recip(out[:parts, :n], out[:parts, :n])

    def lowrank_norm(raw, nk, gains, outs, n):
        pss = ps[6]
        for k in range(nk):
            q = sqb[k % 2]
            P.act(q[:, :n], raw[k][:, :n], AF.Square)
            P.mm(pss[:, :n], C.ones_bf, q[:, :n], start=(k == 0), stop=(k == nk - 1))
        r = rr[nxt() % 2]
        rstd_from(pss, n, 128, 1.0 / (nk * 128), r)
        for k in range(nk):
            t = C.tmpf[k % 3]
            P.tt("dve", t[:, :n], raw[k][:, :n], r[:, :n], ALU.mult)
            P.act(outs[k][:, :n], t[:, :n], AF.Identity, scale=gains[:, k:k + 1])

    def head_norm_rope(praw, n, gcol, dst):
        i = nxt()
        sq = sqb[i % 2]
        P.act(sq[:96, :n], praw[:96, :n], AF.Square)
        pss = ps[6]
        P.mm(pss[:96, :n], oblk, sq[:96, :n])
        r = rr[i % 2]
        rstd_from(pss, n, 96, c96[:, 2:3], r)
        q = qn[i % 2]
        P.tt("dve", q[:, :n], praw[:96, :n], r[:96, :n], ALU.mult)
        P.act(q[:, :n], q[:, :n], AF.Identity, scale=c96[:, gcol:gcol + 1])
        prot = ps[7]
        P.mm(prot[:96, :n], rm, q[:, :n])
        a = t1[i % 2]
        b = t2[i % 2]
        P.tt("pool", a[:, :n], q[:, :n], cosb[:, :n], ALU.mult)
        P.tt("dve", b[:, :n], prot[:96, :n], sinb[:, :n], ALU.mult)
        o = qo[i % 2]
        P.tt("dve", o[:, :n], a[:, :n], b[:, :n], ALU.add)
        P.dma("sp", dst, o[:, :n], sem="qk")

    COL = dict(cq=0, ckv=384, kr=640, mq=672, mk=1184, mv=1696, mo=2208, mg=2720)
    for bi, (s, n, isc) in enumerate(C.blocks):
        H = C.H[bi]
        P.dma("sp", cosb[:, :n], cos_d.with_ap(cos_d.ap[:, s:s + n]), sem="cs")
        P.dma("sp", sinb[:, :n], sin_d.with_ap(sin_d.ap[:, s:s + n]), sem="cs")
        for j in range(3):
            pp = ps[j % 2]
            for k in range(8):
                P.mm(pp[:, :n], win[:, k, COL["cq"] + j * 128:COL["cq"] + (j + 1) * 128], H[k], start=(k == 0), stop=(k == 7))
            P.copy("act", cq[j][:, :n], pp[:, :n])
        for j in range(2):
            pp = ps[j % 2]
            for k in range(8):
                P.mm(pp[:, :n], win[:, k, COL["ckv"] + j * 128:COL["ckv"] + (j + 1) * 128], H[k], start=(k == 0), stop=(k == 7))
            P.copy("act", ckv[j][:, :n], pp[:, :n])
        pp = ps[2]
        for k in range(8):
            P.mm(pp[:32, :n], win[:, k, COL["kr"]:COL["kr"] + 32], H[k], start=(k == 0), stop=(k == 7))
        P.copy("act", krr[:, :n], pp[:32, :n])
        lowrank_norm(cq, 3, cqg, cqn, n)
        lowrank_norm(ckv, 2, ckvg, ckvn, n)
        for j in range(4):
            pp = ps[j % 2]
            for k in range(8):
                P.mm(pp[:, :n], win[:, k, COL["mo"] + j * 128:COL["mo"] + (j + 1) * 128], H[k], start=(k == 0), stop=(k == 7))
            ot = otile[j % 2]
            P.copy("act", ot[:, :n], pp[:, :n])
            P.dma("sp", OT.with_ap(OT.ap[j * 128:(j + 1) * 128, s:s + n]), ot[:, :n], sem="oo")
        for tt_ in range(n // 128):
            ts_ = slice(tt_ * 128, (tt_ + 1) * 128)
            rows = slice(s + tt_ * 128, s + (tt_ + 1) * 128)
            for nm, dst in (("mq", MQ), ("mk", MK), ("mv", MV)):
                pp = ps[3 + nxt() % 2]
                for k in range(8):
                    P.mm(pp[:, :], H[k][:, ts_], win[:, k, COL[nm]:COL[nm] + 512], start=(k == 0), stop=(k == 7))
                tk = tok[nxt() % 3]
                P.copy("act", tk, pp)
                P.dma("sp", dst.with_ap(dst.ap[rows, :]), tk, sem="tok")
            pp = ps[5]
            for k in range(8):
                P.mm(pp[:, 0:16], H[k][:, ts_], win[:, k, COL["mg"]:COL["mg"] + 16], start=(k == 0), stop=(k == 7))
            g = gts[nxt() % 2]
            P.copy("act", g, pp[:, 0:16])
            P.dma("sp", GT.with_ap(GT.ap[rows, :]), g, sem="tok")
            pp = ps[3 + nxt() % 2]
            for k in range(2):
                P.mm(pp[:, :], ckvn[k][:, ts_], wukv[:, k, :], start=(k == 0), stop=(k == 1))
            tk = tok[nxt() % 3]
            P.copy("act", tk, pp)
            P.dma("sp", Vo.with_ap(Vo.ap[rows, :]), tk, sem="tok")
        for h in range(8):
            pp = ps[h % 2]
            for k in range(3):
                P.mm(pp[:96, :n], wuq[:, k, h * 96:(h + 1) * 96], cqn[k][:, :n], start=(k == 0), stop=(k == 2))
            head_norm_rope(pp, n, 0, QT.with_ap(QT.ap[h, :, s:s + n]))
            pp = ps[2 + h % 2]
            for k in range(2):
                P.mm(pp[:96, :n], wukk[:, k, h * 96:(h + 1) * 96], ckvn[k][:, :n], start=(k == 0), stop=False)
            P.mm(pp[:96, :n], e32, krr[:, :n], start=False, stop=True)
            head_norm_rope(pp, n, 1, KT.with_ap(KT.ap[h, :, s:s + n]))


def rope_tables(s):
    pos = np.arange(s * NTL, (s + 1) * NTL)
    rows = (pos // 64).astype(np.float32)
    cols = (pos % 64).astype(np.float32)
    inv = (np.float32(10000.0) ** (-np.arange(0, 16, 2, dtype=np.float32) / np.float32(16))).astype(np.float32)
    cosT = np.ones((96, NT), np.float32)
    sinT = np.zeros((96, NT), np.float32)
    for a, pp in enumerate((rows, cols)):
        ang = (pp[None, :] * inv[:, None]).astype(np.float32)
        for half in range(2):
            r0 = 64 + a * 16 + half * 8
            cosT[r0:r0 + 8, :NTL] = np.cos(ang)
            sinT[r0:r0 + 8, :NTL] = np.sin(ang)
    return cosT, sinT


def consts_A():
    oblk = np.zeros((96, 96), np.float32)
    oblk[:64, :64] = 1
    oblk[64:, 64:] = 1
    rm = np.zeros((96, 96), np.float32)
    for a in range(2):
        for f in range(8):
            p1 = 64 + a * 16 + f
            p2 = 64 + a * 16 + 8 + f
            rm[p2, p1] = -1.0
            rm[p1, p2] = 1.0
    e32 = np.zeros((32, 96), np.float32)
    for k in range(32):
        e32[k, 64 + k] = 1
    return oblk, rm, e32


def normg_table(norm_g_l):
    return np.ascontiguousarray(np.concatenate([fm(norm_g_l[i]) for i in range(3)], axis=1))


def maps_A(inp, m):
    oblk, rm, e32 = consts_A()
    ukv = inp["mla_w_ukv"][0].reshape(256, 8, 128)
    w_ukk = np.zeros((256, 8, 96), np.float32)
    w_ukk[:, :, :64] = ukv[:, :, :64]
    w_ukk = w_ukk.reshape(256, 768)
    w_ukv = np.ascontiguousarray(ukv[:, :, 64:].reshape(256, 512))
    c96 = np.zeros((96, 3), np.float32)
    c96[:, 0] = inp["mla_q_g"][0]
    c96[:, 1] = inp["mla_k_g"][0]
    c96[:64, 2] = 1.0 / 64
    c96[64:, 2] = 1.0 / 32
    maps = []
    for r in range(8):
        b, s = r // 4, r % 4
        cosT, sinT = rope_tables(s)
        xT = np.ascontiguousarray(np.concatenate([inp["x"][b, s * NTL:(s + 1) * NTL].T, inp["ctx"][b].T], axis=1))
        maps.append(dict(xT=xT, modt=mod_table(m, 0, b), normg=normg_table(inp["norm_g"][0]),
                         wg=inp["ffn_w_gate"][0, 0], wu=inp["ffn_w_up"][0, 0], wd=inp["ffn_w_down"][0, 0],
                         w_in=inp["even_w_in"][0], w_uq=inp["mla_w_uq"][0], w_ukk=w_ukk, w_ukv=w_ukv,
                         cqg=fm(inp["mla_cq_g"][0]), ckvg=fm(inp["mla_ckv_g"][0]), c96=c96,
                         oblk=oblk, rm=rm, e32=e32, cosT=cosT, sinT=sinT))
    return maps


NK = NTC + 4 * NTL
NCH = NK // 128


def psplit(P, name, dt, cols, n):
    ph = P.stack.enter_context(P.nc.psum_tensor(uname(name), [128, n * cols], dt))
    bank_bytes = 2048
    esz = 4 if dt == F32 else 2
    bufs = {}
    out = []
    for i in range(n):
        bk = (i * cols * esz) // bank_bytes
        if bk not in bufs:
            bufs[bk] = Buf(f"{name}_b{bk}", excl=True)
        out.append(T(ph[:, i * cols:(i + 1) * cols], bufs[bk]))
    return out


def build_B(P, dr):
    QT = dr.inp("QT", [8, 96, NT], BF16)
    KT = dr.inp("KTa", [8, 96, NK], BF16)
    Va = dr.inp("Va", [NK, 512], BF16)
    mq_d = dr.inp("mq", [NK, 128], BF16)
    mk_d = dr.inp("mk", [NK, 128], BF16)
    mv_d = dr.inp("mv", [NK, 128], BF16)
    g4_d = dr.inp("g4", [NK, 4])
    gb_d = dr.inp("gb", [128, 4])
    triF_d = dr.inp("triF", [128, 128])
    triB_d = dr.inp("triB", [128, 128])
    id_d = dr.inp("ident", [128, 128])
    attT = dr.out("attT", [512, NT])
    HT = dr.out("HT", [128, NK])

    blocks = [(s, 512, False) for s in range(0, NTL, 512)] + [(NTL, NTC, True)]
    psS = psplit(P, "psS", F32, 512, 2)
    psO = psplit(P, "psO", F32, 512, 2)
    psT = psplit(P, "psT", BF16, 512, 2)
    psG = psplit(P, "psG", F32, 128, 4)
    psD = psplit(P, "psD", F32, 256, 2)
    psN = psplit(P, "psN", F32, 256, 2)

    ones_bf = sbt(P, "ones_bf", [128, 128], BF16)
    P.memset("pool", ones_bf, 1.0)
    ones_f = sbt(P, "ones_f", [128, 128], F32)
    P.memset("pool", ones_f, 1.0)
    triF = wload(P, "triF", triF_d.ap, [128, 128], F32)
    triB = wload(P, "triB", triB_d.ap, [128, 128], F32)
    ident = wload(P, "ident", id_d.ap, [128, 128], BF16)
    tri = [triF, triB]

    kth = [sbt(P, f"kth{i}", [96, NK], BF16) for i in range(2)]
    vah = [sbt(P, f"vah{i}", [128, NCH, 128], BF16) for i in range(2)]
    qh = [sbt(P, f"qh{i}", [96, NT], BF16) for i in range(2)]
    for i in range(2):
        P.memset("pool", vah[i][:, :, 64:128], 1.0)
    Et = [sbt(P, f"Et{i}", [128, 512], BF16) for i in range(3)]
    rc = [sbt(P, f"rc{i}", [128, 512], F32) for i in range(2)]
    ob = [sbt(P, f"ob{i}", [64, 512], F32) for i in range(2)]
    vv = Va.ap.rearrange("(t p) c -> p t c", p=128)
    SC = 1.0 / np.sqrt(96.0)

    def load_head(h):
        i = h % 2
        P.dma("sp", kth[i], KT.with_ap(KT.ap[h]), sem=f"kh{i}")
        P.dma("sp", qh[i], QT.with_ap(QT.ap[h]), sem=f"kh{i}")
        for c0 in range(0, NCH, 6):
            P.dma("sp", vah[i][:, c0:c0 + 6, 0:64], Va.with_ap(vv[:, c0:c0 + 6, h * 64:(h + 1) * 64]), sem=f"kh{i}")

    def att_gen():
        it = 0
        load_head(0)
        for h in range(8):
            if h + 1 < 8:
                load_head(h + 1)
            i = h % 2
            for bi, (s, n, isc) in enumerate(blocks):
                po = psO[bi % 2]
                tiles = range(NTC // 128) if isc else range(NCH)
                last = tiles[-1]
                for kt in tiles:
                    pS = psS[it % 2]
                    P.mm(pS[:, :n], kth[i][:, kt * 128:(kt + 1) * 128], qh[i][:, s:s + n])
                    E = Et[it % 3]
                    P.act(E[:, :n], pS[:, :n], AF.Exp, scale=SC)
                    P.mm(po[:, :n], vah[i][:, kt, :], E[:, :n], start=(kt == 0), stop=(kt == last))
                    it += 1
                    yield
                r = rc[bi % 2]
                P.recip(r[64:128, :n], po[64:128, :n])
                o = ob[bi % 2]
                P.tt("dve", o[:, :n], po[0:64, :n], r[64:128, :n], ALU.mult)
                P.dma("sp", attT.with_ap(attT.ap[h * 64:(h + 1) * 64, s:s + n]), o[:, :n], sem="ao")

    gt = sbt(P, "gt", [128, NCH, 4], F32)
    gb = wload(P, "gb", gb_d.ap, [128, 4], F32)
    g4v = g4_d.ap.rearrange("(c p) g -> p c g", p=128)
    for c0 in range(0, NCH, 6):
        P.dma("sp", gt[:, c0:c0 + 6, :], g4_d.with_ap(g4v[:, c0:c0 + 6, :]), sem="c")
    for g in range(4):
        P.ts("dve", gt[:, :, g], gt[:, :, g], gb[:, g:g + 1], None, ALU.add)
    mq = sbt(P, "mq", [128, NCH, 128], BF16)
    mk = sbt(P, "mk", [128, NCH, 128], BF16)
    mv = sbt(P, "mv", [128, NCH, 128], BF16)
    for t_, d_ in ((mq, mq_d), (mk, mk_d), (mv, mv_d)):
        dv_ = d_.ap.rearrange("(c p) d -> p c d", p=128)
        for c0 in range(0, NCH, 6):
            P.dma("sp", t_[:, c0:c0 + 6, :], d_.with_ap(dv_[:, c0:c0 + 6, :]), sem="c")
    HS = sbt(P, "HS", [128, NK], F32)
    HSb = [Buf(f"HS{c}") for c in range(NCH)]
    EQ, EK, EL = [], [], []
    for d in range(2):
        lf = sbt(P, f"lf{d}", [128, NCH], F32)
        P.act(lf, gt[:, :, 2 * d + 1], AF.Exp, scale=-1.0)
        P.ts("dve", lf, lf, 1.0, None, ALU.add)
        P.act(lf, lf, AF.Ln)
        P.ts("dve", lf, lf, -1.0, None, ALU.mult)
        pB = psG[2]
        pTt = psG[3]
        P.mm(pB[:, :NCH], tri[d], lf)
        P.mm(pTt[:, :NCH], ones_f, lf)
        eq = sbt(P, f"eq{d}", [128, NCH], F32)
        ek = sbt(P, f"ek{d}", [128, NCH], F32)
        el = sbt(P, f"el{d}", [128, NCH], F32)
        P.act(eq, pB[:, :NCH], AF.Exp)
        P.tt("dve", ek, gt[:, :, 2 * d], pB[:, :NCH], ALU.subtract)
        P.act(ek, ek, AF.Exp)
        P.ts("dve", ek, ek, float(128.0 ** -0.5), None, ALU.mult)
        P.act(el, pTt[:, :NCH], AF.Exp)
        EQ.append(eq)
        EK.append(ek)
        EL.append(el)
    CN = [sbt(P, f"CN{d}", [128, 256], F32) for d in range(2)]
    CNb = [sbt(P, f"CNb{d}", [128, 256], BF16) for d in range(2)]
    for d in range(2):
        P.memset("pool", CN[d], 0.0)
        P.memset("pool", CNb[d], 0.0)
    qs = [[sbt(P, f"qs{d}{i}", [128, 128], BF16) for i in range(2)] for d in range(2)]
    ks = [[sbt(P, f"ks{d}{i}", [128, 128], BF16) for i in range(2)] for d in range(2)]
    qkT = [[sbt(P, f"qkT{d}{i}", [128, 256], BF16) for i in range(2)] for d in range(2)]
    Sm = [[sbt(P, f"Sm{d}{i}", [128, 128], BF16) for i in range(2)] for d in range(2)]
    dn = [sbt(P, f"dn{d}", [128, 128], F32) for d in range(2)]
    hc = [sbt(P, f"hc{d}", [128, 128], F32) for d in range(2)]
    tmpc = [sbt(P, f"tmpc{d}", [128, 256], F32) for d in range(2)]
    order = [list(range(NCH)), [1, 0] + list(range(NCH - 1, 1, -1))]
    written = set()

    def pre(d, c, i):
        P.ts("dve", qs[d][i], mq[:, c, :], EQ[d][:, c:c + 1], None, ALU.mult)
        P.act(ks[d][i], mk[:, c, :], AF.Identity, scale=EK[d][:, c:c + 1])
        pT = psT[d]
        P.transpose(pT[:, 0:128], qs[d][i], ident)
        P.transpose(pT[:, 128:256], ks[d][i], ident)
        P.copy("act", qkT[d][i], pT[:, 0:256])
        pS = psG[d]
        P.mm(pS, qkT[d][i][:, 128:256], qkT[d][i][:, 0:128])
        P.tt("dve", Sm[d][i], pS, tri[d], ALU.mult)

    def post(d, c, i):
        pD = psD[d]
        P.mm(pD[:, 0:128], ks[d][i], mv[:, c, :])
        P.mm(pD[:, 128:256], ks[d][i], ones_bf)
        pN = psN[d]
        P.mm(pN[:, 0:128], mv[:, c, :], Sm[d][i], start=True, stop=False)
        P.mm(pN[:, 0:128], CNb[d][:, 0:128], qkT[d][i][:, 0:128], start=False, stop=True)
        P.mm(pN[:, 128:256], ones_bf, Sm[d][i], start=True, stop=False)
        P.mm(pN[:, 128:256], CNb[d][:, 128:256], qkT[d][i][:, 0:128], start=False, stop=True)
        P.act(dn[d], pN[:, 128:256], AF.Abs)
        P.ts("dve", dn[d], dn[d], 1.0, None, ALU.max)
        P.recip(dn[d], dn[d])
        hsl = T(HS.ap[:, c * 128:(c + 1) * 128], HSb[c])
        if c in written:
            P.tt("dve", hc[d], pN[:, 0:128], dn[d], ALU.mult)
            P.tt("pool", hsl, hsl, hc[d], ALU.add)
            P.dma("sp", HT.with_ap(HT.ap[:, c * 128:(c + 1) * 128]), hsl, sem="ho")
        else:
            P.tt("dve", hsl, pN[:, 0:128], dn[d], ALU.mult)
            written.add(c)
        P.tt("dve", tmpc[d], CN[d], psD[d], ALU.add)
        P.ts("dve", CN[d], tmpc[d], EL[d][:, c:c + 1], None, ALU.mult)
        P.copy("act", CNb[d], CN[d])

    def ml_gen():
        for d in range(2):
            pre(d, order[d][0], 0)
        for st in range(NCH):
            for d in range(2):
                if st + 1 < NCH:
                    pre(d, order[d][st + 1], (st + 1) % 2)
                post(d, order[d][st], st % 2)
            yield

    a = att_gen() if B_ATT else iter(())
    m = ml_gen() if B_ML else iter(())
    for i, _ in enumerate(a):
        if i % 30 == 5:
            next(m, None)
    for _ in m:
        pass


B_ATT = True
B_ML = True


def consts_B():
    s = np.arange(128)[:, None]
    t = np.arange(128)[None, :]
    triF = (s <= t).astype(np.float32)
    triB = (s >= t).astype(np.float32)
    return triF, triB, np.eye(128, dtype=np.float32)


def maps_B(inp, resA):
    triF, triB, ident = consts_B()
    maps = []
    cat = {}
    for b in range(2):
        rs = [resA[4 * b + s] for s in range(4)]
        cat[b] = dict(
            KT=np.concatenate([rs[0]["KT"][:, :, NTL:]] + [r["KT"][:, :, :NTL] for r in rs], axis=2),
            V=np.concatenate([rs[0]["V"][NTL:]] + [r["V"][:NTL] for r in rs], axis=0),
            MQ=np.concatenate([rs[0]["MQ"][NTL:]] + [r["MQ"][:NTL] for r in rs], axis=0),
            MK=np.concatenate([rs[0]["MK"][NTL:]] + [r["MK"][:NTL] for r in rs], axis=0),
            MV=np.concatenate([rs[0]["MV"][NTL:]] + [r["MV"][:NTL] for r in rs], axis=0),
            GT=np.concatenate([rs[0]["GT"][NTL:]] + [r["GT"][:NTL] for r in rs], axis=0))
    gbias = inp["mlstm_gate_b"][0].reshape(4, 4)
    for r in range(8):
        b, h = r // 4, r % 4
        cb = cat[b]
        hs = slice(h * 128, (h + 1) * 128)
        g4 = np.ascontiguousarray(cb["GT"].reshape(NK, 4, 4)[:, :, h])
        gb = np.ascontiguousarray(np.broadcast_to(gbias[:, h][None, :], (128, 4))).astype(np.float32)
        maps.append(dict(QT=resA[r]["QT"], KTa=np.ascontiguousarray(cb["KT"]), Va=np.ascontiguousarray(cb["V"]),
                         mq=np.ascontiguousarray(cb["MQ"][:, hs]), mk=np.ascontiguousarray(cb["MK"][:, hs]),
                         mv=np.ascontiguousarray(cb["MV"][:, hs]), g4=g4, gb=gb,
                         triF=triF, triB=triB, ident=ident))
    return maps


def mix_out(P, C, wout, mixf, Gsel):
    for bi, (s, n, isc) in enumerate(C.blocks):
        mix = mixf(bi)
        G = Gsel[isc][2]
        for m_ in range(8):
            py = C.ps[4 + m_ % 2]
            for k in range(8):
                P.mm(py[:, :n], wout[:, k, m_ * 128:(m_ + 1) * 128], mix[k][:, :n], start=(k == 0), stop=(k == 7))
            P.stt("dve", C.X[bi][m_], py[:, :n], G[:, m_:m_ + 1], C.X[bi][m_], ALU.mult, ALU.add)


def build_C(P, dr):
    C = setup_common(P, NTL, NTC)
    xin = dr.inp("xT", [1024, NT])
    attT = dr.inp("attT", [512, NT])
    hmT = dr.inp("hmT", [512, NT])
    oT = dr.inp("oT", [512, NT])
    wout_d = dr.inp("w_out", [1024, 1024])
    outg_d = dr.inp("outg", [128, 4])
    mods0 = load_mods(P, dr, C, "0")
    mods1 = load_mods(P, dr, C, "1")
    fw02 = ffn_w(dr, "02")
    fw11 = ffn_w(dr, "11")
    win_d = dr.inp("w_in1", [1024, 2048])
    xo = dr.out("xTo", [1024, NTL])
    ggT = dr.out("ggT", [1024, NTL])
    xrT = dr.out("xrT", [1024, NT])

    load_X(P, C, xin)
    wslab = C.wh[:, :, :].rearrange("p s c -> p (s c)")
    allw = [b for t in C.wslots for b in t.bufs]
    wout = T(wslab[:, 0:8192].rearrange("p (k c) -> p k c", k=8), allw)
    wv = wout_d.ap.rearrange("(k p) c -> p k c", p=128)
    for k in range(8):
        P.dma("pool", wout[:, k, :], T(wv[:, k, :], WDRAM), sem="win")
    outg = wload(P, "outg", outg_d.ap, [128, 4], F32)
    mixb = [sbt(P, f"mix{k}", [128, 512], BF16) for k in range(8)]
    hmt = [sbt(P, f"hmt{i}", [128, 512], F32) for i in range(2)]
    ott = [sbt(P, f"ott{i}", [128, 512], F32) for i in range(2)]
    rr = [sbt(P, f"rrc{i}", [128, 512], F32) for i in range(2)]

    def mixf(bi):
        s, n, isc = C.blocks[bi]
        for j in range(4):
            P.dma("pool", mixb[j][:, :n], attT.with_ap(attT.ap[j * 128:(j + 1) * 128, s:s + n]), sem="mx")
            hm = hmt[j % 2]
            ot = ott[j % 2]
            P.dma("sp", hm[:, :n], hmT.with_ap(hmT.ap[j * 128:(j + 1) * 128, s:s + n]), sem="mx2")
            P.dma("sp", ot[:, :n], oT.with_ap(oT.ap[j * 128:(j + 1) * 128, s:s + n]), sem="mx2")
            sq = C.sq[j % 2]
            P.act(sq[:, :n], hm[:, :n], AF.Square)
            pss = C.ps[6]
            P.mm(pss[:, :n], C.ones_bf, sq[:, :n])
            r = rr[j % 2]
            P.ts("dve", r[:, :n], pss[:, :n], 1.0 / 128, EPS, ALU.mult, ALU.add)
            P.act(r[:, :n], r[:, :n], AF.Sqrt)
            P.recip(r[:, :n], r[:, :n])
            P.tt("dve", hm[:, :n], hm[:, :n], r[:, :n], ALU.mult)
            P.act(ot[:, :n], ot[:, :n], AF.Sigmoid)
            P.stt("dve", mixb[4 + j][:, :n], ot[:, :n], outg[:, j:j + 1], hm[:, :n], ALU.mult, ALU.mult)
        return mixb

    mix_out(P, C, wout, mixf, mods0[1])
    P.barrier()
    ffn(P, C, fw02[0], fw02[1], fw02[2], mods0[2])
    ffn(P, C, fw11[0], fw11[1], fw11[2], mods1[0])
    xv = xo.ap.rearrange("(k p) t -> p k t", p=128)
    for bi, (s, n, isc) in enumerate(C.blocks):
        if not isc:
            for k in range(8):
                P.dma("sp", xo.with_ap(xv[:, k, s:s + n]), C.X[bi][k], sem="xout")
    for bi in range(len(C.blocks)):
        isc = C.blocks[bi][2]
        norm_mod(P, C, bi, mods1[1][isc][0], mods1[1][isc][1])
    P.barrier()
    win = T(wslab[:, 0:8 * 2048].rearrange("p (k c) -> p k c", k=8), allw)
    wv = win_d.ap.rearrange("(k p) c -> p k c", p=128)
    for k in range(8):
        P.dma("pool", win[:, k, :], T(wv[:, k, :], WDRAM), sem="win")
    ga = hmt
    gb_ = ott
    it = 0
    for bi, (s, n, isc) in enumerate(C.blocks):
        H = C.H[bi]
        if not isc:
            for j in range(8):
                pp = C.ps[j % 2]
                for k in range(8):
                    P.mm(pp[:, :n], win[:, k, j * 128:(j + 1) * 128], H[k], start=(k == 0), stop=(k == 7))
                a = ga[it % 2]
                b = gb_[it % 2]
                it += 1
                P.act(a[:, :n], pp[:, :n], AF.Square)
                P.ts("dve", a[:, :n], a[:, :n], 0.044715, 1.0, ALU.mult, ALU.add)
                P.tt("dve", a[:, :n], a[:, :n], pp[:, :n], ALU.mult)
                P.act(b[:, :n], a[:, :n], AF.Sigmoid, scale=1.5957691216057308)
                P.tt("dve", b[:, :n], b[:, :n], pp[:, :n], ALU.mult)
                P.dma("sp", ggT.with_ap(ggT.ap[j * 128:(j + 1) * 128, s:s + n]), b[:, :n], sem="go")
        for j in range(8):
            pp = C.ps[2 + j % 2]
            for k in range(8):
                P.mm(pp[:, :n], win[:, k, 1024 + j * 128:1024 + (j + 1) * 128], H[k], start=(k == 0), stop=(k == 7))
            a = ga[it % 2]
            it += 1
            P.copy("act", a[:, :n], pp[:, :n])
            P.dma("sp", xrT.with_ap(xrT.ap[j * 128:(j + 1) * 128, s:s + n]), a[:, :n], sem="go")


def maps_C(inp, m, resA, resB):
    maps = []
    for r in range(8):
        b, s = r // 4, r % 4
        hm = np.zeros((512, NT), np.float32)
        for h in range(4):
            HTb = resB[4 * b + h]["HT"]
            hm[h * 128:(h + 1) * 128, :NTL] = HTb[:, NTC + s * NTL:NTC + (s + 1) * NTL]
            hm[h * 128:(h + 1) * 128, NTL:] = HTb[:, :NTC]
        maps.append(dict(xT=resA[r]["xTo"], attT=resB[r]["attT"], hmT=hm, oT=resA[r]["OT"],
                         w_out=inp["even_w_out"][0], outg=fm(inp["mlstm_out_g"][0]),
                         modt0=mod_table(m, 0, b), normg0=normg_table(inp["norm_g"][0]),
                         modt1=mod_table(m, 1, b), normg1=normg_table(inp["norm_g"][1]),
                         wg02=inp["ffn_w_gate"][0, 1], wu02=inp["ffn_w_up"][0, 1], wd02=inp["ffn_w_down"][0, 1],
                         wg11=inp["ffn_w_gate"][1, 0], wu11=inp["ffn_w_up"][1, 0], wd11=inp["ffn_w_down"][1, 0],
                         w_in1=inp["odd_w_in"][0]))
    return maps


TS = 2048


def build_D(P, dr):
    u_d = dr.inp("u", [2, 128, NK])
    ur_d = dr.inp("urev", [2, 128, NK])
    cw_d = dr.inp("cw", [128, 4])
    cb_d = dr.inp("cb", [128, 1])
    wa_d = dr.inp("wa", [2, 128, 128])
    wx_d = dr.inp("wx", [2, 128, 128])
    bab_d = dr.inp("bab", [128, 6])
    hf = dr.out("hf", [2, 128, 4 * NTL])
    hb = dr.out("hb", [2, 128, 4 * NTL])
    cw = wload(P, "cw", cw_d.ap, [128, 4], F32)
    cb = wload(P, "cb", cb_d.ap, [128, 1], F32)
    wa = [wload(P, f"wa{d}", wa_d.ap[d], [128, 128], F32) for d in range(2)]
    wx = [wload(P, f"wx{d}", wx_d.ap[d], [128, 128], F32) for d in range(2)]
    bab = wload(P, "bab", bab_d.ap, [128, 6], F32)
    cc = sbt(P, "cc", [128, 2], F32)
    P.act(cc, bab[:, 4:6], AF.Exp, scale=-1.0)
    P.ts("dve", cc, cc, 1.0, None, ALU.add)
    P.act(cc, cc, AF.Ln)
    P.ts("dve", cc, cc, -8.0, None, ALU.mult)
    psA = psplit(P, "psA", F32, 512, 2)
    psX = psplit(P, "psX", F32, 512, 2)
    ub = [sbt(P, f"ub{i}", [128, TS + 4], F32) for i in range(2)]
    xc = [sbt(P, f"xc{i}", [128, TS], F32) for i in range(2)]
    rt = sbt(P, "rt", [128, TS], F32)
    ig = sbt(P, "ig", [128, TS], F32)
    at = sbt(P, "at", [128, TS], F32)
    om = sbt(P, "om", [128, TS], F32)
    ip = sbt(P, "ip", [128, TS], F32)
    ht = [sbt(P, f"ht{i}", [128, TS], F32) for i in range(2)]
    it = 0
    for b in range(2):
        for d in range(2):
            src = u_d if d == 0 else ur_d
            dst = hf if d == 0 else hb
            offs = [j - 2 for j in range(4)] if d == 0 else [2 - j for j in range(4)]
            prev = None
            for (seg0, segn) in ((0, NTC), (NTC, 4 * NTL)):
                for c0 in range(0, segn, TS):
                    n = min(TS, segn - c0)
                    u = ub[it % 2]
                    x = xc[it % 2]
                    h = ht[it % 2]
                    it += 1
                    lo = max(c0 - 2, 0)
                    hi = min(c0 + n + 2, segn)
                    if lo > c0 - 2:
                        P.memset("pool", u[:, 0:2], 0.0)
                    if hi < c0 + n + 2:
                        P.memset("pool", u[:, n + 2:n + 4], 0.0)
                    P.dma("sp", u[:, 2 + (lo - c0):2 + (hi - c0)], src.with_ap(src.ap[b, :, seg0 + lo:seg0 + hi]), sem=f"u{it % 2}")
                    P.act(x[:, :n], u[:, 2 + offs[0]:2 + offs[0] + n], AF.Identity, bias=cb[:, 0:1], scale=cw[:, 0:1])
                    for j in range(1, 4):
                        P.stt("dve", x[:, :n], u[:, 2 + offs[j]:2 + offs[j] + n], cw[:, j:j + 1], x[:, :n], ALU.mult, ALU.add)
                    for q0 in range(0, n, 512):
                        qn_ = min(512, n - q0)
                        pa = psA[(q0 // 512) % 2]
                        px = psX[(q0 // 512) % 2]
                        P.mm(pa[:, :qn_], wa[d], x[:, q0:q0 + qn_])
                        P.act(rt[:, q0:q0 + qn_], pa[:, :qn_], AF.Sigmoid, bias=bab[:, d:d + 1])
                        P.mm(px[:, :qn_], wx[d], x[:, q0:q0 + qn_])
                        P.act(ig[:, q0:q0 + qn_], px[:, :qn_], AF.Sigmoid, bias=bab[:, 2 + d:3 + d])
                    P.act(at[:, :n], rt[:, :n], AF.Exp, scale=cc[:, d:d + 1])
                    P.tt("pool", om[:, :n], at[:, :n], at[:, :n], ALU.mult)
                    P.ts("dve", om[:, :n], om[:, :n], -1.0, 1.0, ALU.mult, ALU.add)
                    P.act(om[:, :n], om[:, :n], AF.Sqrt)
                    P.tt("pool", ip[:, :n], ig[:, :n], x[:, :n], ALU.mult)
                    P.tt("dve", ip[:, :n], ip[:, :n], om[:, :n], ALU.mult)
                    init = 0.0 if prev is None else prev
                    P.scan(h[:, :n], at[:, :n], ip[:, :n], init, ALU.mult, ALU.add)
                    prev = h[:, n - 1:n]
                    if seg0 > 0:
                        P.dma("sp", dst.with_ap(dst.ap[b, :, c0:c0 + n]), h[:, :n], sem="ho")


def maps_D(inp, resC):
    maps = []
    xr = []
    for b in range(2):
        rs = [resC[4 * b + s]["xrT"] for s in range(4)]
        xr.append(np.concatenate([rs[0][:, NTL:]] + [r[:, :NTL] for r in rs], axis=1))
    for n in range(8):
        cs = slice(n * 128, (n + 1) * 128)
        u = np.stack([xr[b][cs] for b in range(2)], 0)
        urev = np.concatenate([u[:, :, :NTC][:, :, ::-1], u[:, :, NTC:][:, :, ::-1]], axis=2)
        bab = np.stack([inp["lru_b_a"][0, 0, cs], inp["lru_b_a"][0, 1, cs], inp["lru_b_x"][0, 0, cs],
                        inp["lru_b_x"][0, 1, cs], inp["lru_lam"][0, 0, cs], inp["lru_lam"][0, 1, cs]], axis=1)
        maps.append(dict(u=np.ascontiguousarray(u), urev=np.ascontiguousarray(urev),
                         cw=np.ascontiguousarray(inp["odd_conv_w"][0][:, cs].T), cb=np.ascontiguousarray(inp["odd_conv_b"][0][cs][:, None]),
                         wa=np.ascontiguousarray(inp["lru_w_a"][0, :, n]), wx=np.ascontiguousarray(inp["lru_w_x"][0, :, n]),
                         bab=np.ascontiguousarray(bab.astype(np.float32))))
    return maps


def build_E(P, dr):
    C = setup_common(P, NTL, 0)
    xin = dr.inp("xT", [1024, NTL])
    ggT = dr.inp("ggT", [1024, NTL])
    hfT = dr.inp("hfT", [1024, NTL])
    hbT = dr.inp("hbT", [1024, NTL])
    wout_d = dr.inp("w_out", [1024, 1024])
    mods1 = load_mods(P, dr, C, "1")
    fw12 = ffn_w(dr, "12")
    xo = dr.out("xTo", [1024, NTL])
    load_X(P, C, xin)
    wslab = C.wh[:, :, :].rearrange("p s c -> p (s c)")
    allw = [b for t in C.wslots for b in t.bufs]
    wout = T(wslab[:, 0:8192].rearrange("p (k c) -> p k c", k=8), allw)
    wv = wout_d.ap.rearrange("(k p) c -> p k c", p=128)
    for k in range(8):
        P.dma("pool", wout[:, k, :], T(wv[:, k, :], WDRAM), sem="win")
    mixb = [sbt(P, f"mix{k}", [128, 512], BF16) for k in range(8)]
    ta = [sbt(P, f"ta{i}", [128, 512], F32) for i in range(2)]
    tb = [sbt(P, f"tb{i}", [128, 512], F32) for i in range(2)]
    tg = [sbt(P, f"tg{i}", [128, 512], F32) for i in range(2)]

    def mixf(bi):
        s, n, isc = C.blocks[bi]
        for k in range(8):
            a, b_, g = ta[k % 2], tb[k % 2], tg[k % 2]
            rows = slice(k * 128, (k + 1) * 128)
            P.dma("sp", a[:, :n], hfT.with_ap(hfT.ap[rows, s:s + n]), sem="mx")
            P.dma("sp", b_[:, :n], hbT.with_ap(hbT.ap[rows, s:s + n]), sem="mx")
            P.dma("sp", g[:, :n], ggT.with_ap(ggT.ap[rows, s:s + n]), sem="mx")
            P.tt("pool", a[:, :n], a[:, :n], b_[:, :n], ALU.add)
            P.tt("dve", mixb[k][:, :n], a[:, :n], g[:, :n], ALU.mult)
        return mixb

    mix_out(P, C, wout, mixf, mods1[1])
    P.barrier()
    ffn(P, C, fw12[0], fw12[1], fw12[2], mods1[2])
    store_X(P, C, xo)


def maps_E(inp, m, resC, resD):
    maps = []
    for r in range(8):
        b, s = r // 4, r % 4
        hf = np.concatenate([resD[n]["hf"][b][:, s * NTL:(s + 1) * NTL] for n in range(8)], axis=0)
        hbfull = [resD[n]["hb"][b][:, ::-1] for n in range(8)]
        hb = np.concatenate([h[:, s * NTL:(s + 1) * NTL] for h in hbfull], axis=0)
        maps.append(dict(xT=resC[r]["xTo"], ggT=resC[r]["ggT"], hfT=np.ascontiguousarray(hf), hbT=np.ascontiguousarray(hb),
                         w_out=inp["odd_w_out"][0], modt1=mod_table(m, 1, b), normg1=normg_table(inp["norm_g"][1]),
                         wg12=inp["ffn_w_gate"][1, 1], wu12=inp["ffn_w_up"][1, 1], wd12=inp["ffn_w_down"][1, 1]))
    return maps


NSEG = 4


def seg_blocks(seg):
    bl = [(s, 512, False, NTC + seg * NTL + s) for s in range(0, NTL, 512)]
    if seg == 0:
        bl.append((NTL, NTC, True, 0))
    return bl


def setup_seg(P, seg):
    C = setup_common(P, NTL, NTC if seg == 0 else 0)
    C.gcol = [b[3] for b in seg_blocks(seg)]
    return C


def load_Xg(P, C, xin):
    xv = xin.ap.rearrange("(k p) t -> p k t", p=128)
    for bi, (s, n, c) in enumerate(C.blocks):
        g = C.gcol[bi]
        for k in range(8):
            P.dma("sp", C.X[bi][k], xin.with_ap(xv[:, k, g:g + n]), sem="xin")


def store_Xg(P, C, xout, goff=0, latent_only=False):
    xv = xout.ap.rearrange("(k p) t -> p k t", p=128)
    for bi, (s, n, c) in enumerate(C.blocks):
        if latent_only and c:
            continue
        g = C.gcol[bi] - goff
        for k in range(8):
            P.dma("sp", xout.with_ap(xv[:, k, g:g + n]), C.X[bi][k], sem="xout")


def phase_M(P, dr):
    modw = dr.inp("mod_w", [2, 1024, 9216])
    modb = dr.inp("modb", [128, 144])
    cv = dr.inp("cv", [128, 16])
    MODS = dr.tmp("MODS", [128, 288])
    cvt = sbt(P, "cvt", [128, 8, 2], F32)
    scv = sbt(P, "scv", [128, 8, 2], F32)
    mb = sbt(P, "mb", [128, 144], F32)
    res = sbt(P, "res", [128, 144, 2], F32)
    P.dma("sp", cvt, cv.with_ap(cv.ap.rearrange("p (k v) -> p k v", v=2)), sem="c")
    P.dma("sp", mb, modb, sem="c")
    P.act(scv, cvt, AF.Silu)
    ph = P.stack.enter_context(P.nc.psum_tensor(uname("psm"), [128, 144, 2], F32))
    ps = T(ph[:, :, :], Buf("psm", excl=True))
    ws = [sbt(P, f"mw{i}", [128, 8, 384], F32) for i in range(2)]
    n = 0
    for l in range(2):
        wv = modw.ap[l].rearrange("(k p) c -> p k c", p=128)
        for pc in range(24):
            w = ws[n % 2]
            P.dma("sp", w, modw.with_ap(wv[:, :, pc * 384:(pc + 1) * 384]), sem=f"mw{n % 2}")
            n += 1
            for jj in range(3):
                j = l * 72 + pc * 3 + jj
                for k in range(8):
                    P.mm(ps[:, j, :], w[:, k, jj * 128:(jj + 1) * 128], scv[:, k, :], start=(k == 0), stop=(k == 7))
    for v in range(2):
        P.tt("dve", res[:, :, v], ps[:, :, v], mb, ALU.add)
    P.dma("sp", MODS, res.with_ap(res.ap.rearrange("p j v -> p (j v)")), sem="o")


def load_modsF(P, dr, C, l):
    MODS = dr.tmp("MODS", [128, 288])
    ng_d = dr.inp(f"normg{l}", [128, 24])
    mt = sbt(P, "mt", [128, 9, 8, 2], F32)
    ng = sbt(P, "ng", [128, 3, 8], F32)
    mv = MODS.ap.rearrange("p (l j v) -> p l j v", l=2, v=2)[:, l].rearrange("p (i k) v -> p i k v", i=9)
    P.dma("sp", mt, MODS.with_ap(mv), sem="c")
    P.dma("sp", ng, ng_d.with_ap(ng_d.ap.rearrange("p (i k) -> p i k", i=3)), sem="c")
    ab = sbt(P, "modAB", [128, 3, 2, 3, 8], F32)
    mods = []
    for i in range(3):
        row = {}
        for xc in range(2):
            A = ab[:, i, xc, 0, :]
            B = ab[:, i, xc, 1, :]
            G = ab[:, i, xc, 2, :]
            P.ts("dve", A, mt[:, 3 * i + 1, :, xc], 1.0, None, ALU.add)
            P.tt("dve", A, A, ng[:, i, :], ALU.mult)
            P.copy("dve", B, mt[:, 3 * i, :, xc])
            P.ts("dve", G, mt[:, 3 * i + 2, :, xc], 0.5 if i != 1 else 1.0, None, ALU.mult)
            row[bool(xc)] = (A, B, G)
        mods.append(row)
    return mods


def ffn_wF(dr, l, i):
    wg = dr.inp("ffn_w_gate", [2, 2, 1024, 2816])
    wu = dr.inp("ffn_w_up", [2, 2, 1024, 2816])
    wd = dr.inp("ffn_w_down", [2, 2, 2816, 1024])
    return (wg.ap[l, i].rearrange("(k p) c -> p k c", p=128), wu.ap[l, i].rearrange("(k p) c -> p k c", p=128),
            wd.ap[l, i].rearrange("(j p) c -> p j c", p=128))


def scratch(dr):
    D_ = {}
    D_["XA"] = dr.tmp("XA", [1024, NK])
    D_["XC"] = dr.tmp("XC", [1024, NK])
    D_["QT"] = dr.tmp("QT", [8, 96, NK], BF16)
    D_["KT"] = dr.tmp("KT", [8, 96, NK], BF16)
    D_["V"] = dr.tmp("V", [NK, 512], BF16)
    D_["MQ"] = dr.tmp("MQ", [NK, 512], BF16)
    D_["MK"] = dr.tmp("MK", [NK, 512], BF16)
    D_["MV"] = dr.tmp("MV", [NK, 512], BF16)
    D_["OT"] = dr.tmp("OT", [512, NK])
    D_["GT"] = dr.tmp("GT", [NK, 16])
    D_["attT"] = dr.tmp("attT", [512, NK])
    D_["HT"] = dr.tmp("HT", [512, NK])
    D_["xrT"] = dr.tmp("xrT", [1024, NK])
    D_["ggT"] = dr.tmp("ggT", [1024, NK])
    D_["hf"] = dr.tmp("hf", [1024, 4 * NTL])
    D_["hb"] = dr.tmp("hb", [1024, 4 * NTL])
    return D_


def phase_A(P, dr, seg):
    S = scratch(dr)
    C = setup_seg(P, seg)
    xin = dr.inp("xT", [1024, NK])
    mods = load_modsF(P, dr, C, 0)
    fw = ffn_wF(dr, 0, 0)
    w_in = dr.inp("w_in", [1024, 2736])
    w_uq = dr.inp("w_uq", [384, 768])
    w_ukk = dr.inp("w_ukk", [256, 768])
    w_ukv = dr.inp("w_ukv", [256, 512])
    cqg_d = dr.inp("cqg", [128, 3])
    ckvg_d = dr.inp("ckvg", [128, 2])
    c96_d = dr.inp("c96", [96, 3])
    oblk_d = dr.inp("oblk", [96, 96])
    rm_d = dr.inp("rm", [96, 96])
    e32_d = dr.inp("e32", [32, 96])
    cos_d = dr.inp("cosT", [96, NK])
    sin_d = dr.inp("sinT", [96, NK])
    xo, QT, KT, Vo, MQ, MK, MV, OT, GT = (S[k] for k in ("XA", "QT", "KT", "V", "MQ", "MK", "MV", "OT", "GT"))

    load_Xg(P, C, xin)
    ffn(P, C, fw[0], fw[1], fw[2], mods[0])
    store_Xg(P, C, xo)
    for bi in range(len(C.blocks)):
        isc = C.blocks[bi][2]
        norm_mod(P, C, bi, mods[1][isc][0], mods[1][isc][1])
    P.barrier()
    wslab = C.wh
    win = T(wslab[:, :, :].rearrange("p s c -> p (s c)")[:, 0:8 * 2736].rearrange("p (k c) -> p k c", k=8),
            [b for t in C.wslots for b in t.bufs])
    winv = w_in.ap.rearrange("(k p) c -> p k c", p=128)
    for k in range(8):
        P.dma("pool", win[:, k, :], T(winv[:, k, :], WDRAM), sem="win")
    wuq = wload(P, "wuq", w_uq.ap.rearrange("(k p) c -> p k c", p=128), [128, 3, 768])
    wukk = wload(P, "wukk", w_ukk.ap.rearrange("(k p) c -> p k c", p=128), [128, 2, 768])
    wukv = wload(P, "wukv", w_ukv.ap.rearrange("(k p) c -> p k c", p=128), [128, 2, 512])
    cqg = wload(P, "cqg", cqg_d.ap, [128, 3], F32)
    ckvg = wload(P, "ckvg", ckvg_d.ap, [128, 2], F32)
    c96 = wload(P, "c96", c96_d.ap, [96, 3], F32)
    oblk = wload(P, "oblk", oblk_d.ap, [96, 96])
    rm = wload(P, "rm", rm_d.ap, [96, 96], F32)
    e32 = wload(P, "e32", e32_d.ap, [32, 96])
    xh = C.xh
    XS = xh[:, :, :].rearrange("p k t -> p (k t)")
    off = [0]

    def xs(name, parts, n):
        a = XS[0:parts, off[0]:off[0] + n]
        off[0] += n
        return T(a, Buf(name))

    cq = [xs(f"cq{k}", 128, 512) for k in range(3)]
    ckv = [xs(f"ckv{k}", 128, 512) for k in range(2)]
    cosb = xs("cosb", 96, 512)
    sinb = xs("sinb", 96, 512)
    qn = [xs(f"qn{i}", 96, 512) for i in range(2)]
    t1 = [xs(f"t1{i}", 96, 512) for i in range(2)]
    t2 = [xs(f"t2{i}", 96, 512) for i in range(2)]
    rr = [xs(f"rr{i}", 128, 512) for i in range(2)]
    otile = [xs(f"ot{i}", 128, 512) for i in range(2)]
    gts = [xs(f"gts{i}", 128, 16) for i in range(2)]
    cqn = [sbt(P, f"cqn{k}", [128, 512], BF16) for k in range(3)]
    ckvn = [sbt(P, f"ckvn{k}", [128, 512], BF16) for k in range(2)]
    krr = sbt(P, "krr", [32, 512], BF16)
    sqb = [sbt(P, f"sqb{i}", [128, 512], BF16) for i in range(2)]
    qo = [sbt(P, f"qo{i}", [96, 512], BF16) for i in range(2)]
    tok = [sbt(P, f"tok{i}", [128, 512], BF16) for i in range(3)]
    ps = C.ps
    cnt = [0]

    def nxt():
        cnt[0] += 1
        return cnt[0]

    def rstd_from(pss, n, parts, invn, out):
        P.ts("dve", out[:parts, :n], pss[:parts, :n], invn, EPS, ALU.mult, ALU.add)
        P.act(out[:parts, :n], out[:parts, :n], AF.Sqrt)
        P.recip(out[:parts, :n], out[:parts, :n])

    def lowrank_norm(raw, nk, gains, outs, n):
        pss = ps[6]
        for k in range(nk):
            q = sqb[k % 2]
            P.act(q[:, :n], raw[k][:, :n], AF.Square)
            P.mm(pss[:, :n], C.ones_bf, q[:, :n], start=(k == 0), stop=(k == nk - 1))
        r = rr[nxt() % 2]
        rstd_from(pss, n, 128, 1.0 / (nk * 128), r)
        for k in range(nk):
            t = C.tmpf[k % 3]
            P.tt("dve", t[:, :n], raw[k][:, :n], r[:, :n], ALU.mult)
            P.act(outs[k][:, :n], t[:, :n], AF.Identity, scale=gains[:, k:k + 1])

    def head_norm_rope(praw, n, gcol, dst):
        i = nxt()
        sq = sqb[i % 2]
        P.act(sq[:96, :n], praw[:96, :n], AF.Square)
        pss = ps[6]
        P.mm(pss[:96, :n], oblk, sq[:96, :n])
        r = rr[i % 2]
        rstd_from(pss, n, 96, c96[:, 2:3], r)
        q = qn[i % 2]
        P.tt("dve", q[:, :n], praw[:96, :n], r[:96, :n], ALU.mult)
        P.act(q[:, :n], q[:, :n], AF.Identity, scale=c96[:, gcol:gcol + 1])
        prot = ps[7]
        P.mm(prot[:96, :n], rm, q[:, :n])
        a = t1[i % 2]
        b = t2[i % 2]
        P.tt("pool", a[:, :n], q[:, :n], cosb[:, :n], ALU.mult)
        P.tt("dve", b[:, :n], prot[:96, :n], sinb[:, :n], ALU.mult)
        o = qo[i % 2]
        P.tt("dve", o[:, :n], a[:, :n], b[:, :n], ALU.add)
        P.dma("sp", dst, o[:, :n], sem="qk")

    COL = dict(cq=0, ckv=384, kr=640, mq=672, mk=1184, mv=1696, mo=2208, mg=2720)
    for bi, (s, n, isc) in enumerate(C.blocks):
        g = C.gcol[bi]
        H = C.H[bi]
        P.dma("sp", cosb[:, :n], cos_d.with_ap(cos_d.ap[:, g:g + n]), sem="cs")
        P.dma("sp", sinb[:, :n], sin_d.with_ap(sin_d.ap[:, g:g + n]), sem="cs")
        for j in range(3):
            pp = ps[j % 2]
            for k in range(8):
                P.mm(pp[:, :n], win[:, k, COL["cq"] + j * 128:COL["cq"] + (j + 1) * 128], H[k], start=(k == 0), stop=(k == 7))
            P.copy("act", cq[j][:, :n], pp[:, :n])
        for j in range(2):
            pp = ps[j % 2]
            for k in range(8):
                P.mm(pp[:, :n], win[:, k, COL["ckv"] + j * 128:COL["ckv"] + (j + 1) * 128], H[k], start=(k == 0), stop=(k == 7))
            P.copy("act", ckv[j][:, :n], pp[:, :n])
        pp = ps[2]
        for k in range(8):
            P.mm(pp[:32, :n], win[:, k, COL["kr"]:COL["kr"] + 32], H[k], start=(k == 0), stop=(k == 7))
        P.copy("act", krr[:, :n], pp[:32, :n])
        lowrank_norm(cq, 3, cqg, cqn, n)
        lowrank_norm(ckv, 2, ckvg, ckvn, n)
        for j in range(4):
            pp = ps[j % 2]
            for k in range(8):
                P.mm(pp[:, :n], win[:, k, COL["mo"] + j * 128:COL["mo"] + (j + 1) * 128], H[k], start=(k == 0), stop=(k == 7))
            ot = otile[j % 2]
            P.copy("act", ot[:, :n], pp[:, :n])
            P.dma("sp", OT.with_ap(OT.ap[j * 128:(j + 1) * 128, g:g + n]), ot[:, :n], sem="oo")
        for tt_ in range(n // 128):
            ts_ = slice(tt_ * 128, (tt_ + 1) * 128)
            rows = slice(g + tt_ * 128, g + (tt_ + 1) * 128)
            for nm, dst in (("mq", MQ), ("mk", MK), ("mv", MV)):
                pp = ps[3 + nxt() % 2]
                for k in range(8):
                    P.mm(pp[:, :], H[k][:, ts_], win[:, k, COL[nm]:COL[nm] + 512], start=(k == 0), stop=(k == 7))
                tk = tok[nxt() % 3]
                P.copy("act", tk, pp)
                P.dma("sp", dst.with_ap(dst.ap[rows, :]), tk, sem="tok")
            pp = ps[5]
            for k in range(8):
                P.mm(pp[:, 0:16], H[k][:, ts_], win[:, k, COL["mg"]:COL["mg"] + 16], start=(k == 0), stop=(k == 7))
            gg_ = gts[nxt() % 2]
            P.copy("act", gg_, pp[:, 0:16])
            P.dma("sp", GT.with_ap(GT.ap[rows, :]), gg_, sem="tok")
            pp = ps[3 + nxt() % 2]
            for k in range(2):
                P.mm(pp[:, :], ckvn[k][:, ts_], wukv[:, k, :], start=(k == 0), stop=(k == 1))
            tk = tok[nxt() % 3]
            P.copy("act", tk, pp)
            P.dma("sp", Vo.with_ap(Vo.ap[rows, :]), tk, sem="tok")
        for h in range(8):
            pp = ps[h % 2]
            for k in range(3):
                P.mm(pp[:96, :n], wuq[:, k, h * 96:(h + 1) * 96], cqn[k][:, :n], start=(k == 0), stop=(k == 2))
            head_norm_rope(pp, n, 0, QT.with_ap(QT.ap[h, :, g:g + n]))
            pp = ps[2 + h % 2]
            for k in range(2):
                P.mm(pp[:96, :n], wukk[:, k, h * 96:(h + 1) * 96], ckvn[k][:, :n], start=(k == 0), stop=False)
            P.mm(pp[:96, :n], e32, krr[:, :n], start=False, stop=True)
            head_norm_rope(pp, n, 1, KT.with_ap(KT.ap[h, :, g:g + n]))


def phase_B(P, dr):
    S = scratch(dr)
    QT, KT, Va, MQd, MKd, MVd, GT, attT, HT = (S[k] for k in ("QT", "KT", "V", "MQ", "MK", "MV", "GT", "attT", "HT"))
    gb_d = dr.inp("gb16", [128, 16])
    triF_d = dr.inp("triF", [128, 128])
    triB_d = dr.inp("triB", [128, 128])
    id_d = dr.inp("ident", [128, 128])
    blocks = [(NTC + s, 512, False) for s in range(0, 4 * NTL, 512)] + [(0, NTC, True)]
    psS = psplit(P, "psS", F32, 512, 2)
    psO = psplit(P, "psO", F32, 512, 2)
    psT = psplit(P, "psT", BF16, 512, 2)
    psG = psplit(P, "psG", F32, 128, 4)
    psD = psplit(P, "psD", F32, 256, 2)
    psN = psplit(P, "psN", F32, 256, 2)
    ones_bf = sbt(P, "ones_bf", [128, 128], BF16)
    P.memset("pool", ones_bf, 1.0)
    ones_f = sbt(P, "ones_f", [128, 128], F32)
    P.memset("pool", ones_f, 1.0)
    triF = wload(P, "triF", triF_d.ap, [128, 128], F32)
    triB = wload(P, "triB", triB_d.ap, [128, 128], F32)
    ident = wload(P, "ident", id_d.ap, [128, 128], BF16)
    gb16 = wload(P, "gb16", gb_d.ap, [128, 16], F32)
    tri = [triF, triB]
    kth = [sbt(P, f"kth{i}", [96, NK], BF16) for i in range(2)]
    vah = [sbt(P, f"vah{i}", [128, NCH, 128], BF16) for i in range(2)]
    qh1 = sbt(P, "qh", [96, NK], BF16)
    qh = [qh1, qh1]
    for i in range(2):
        P.memset("pool", vah[i][:, :, 64:128], 1.0)
    Et = [sbt(P, f"Et{i}", [128, 512], BF16) for i in range(3)]
    rc = [sbt(P, f"rc{i}", [128, 512], F32) for i in range(2)]
    ob = [sbt(P, f"ob{i}", [64, 512], F32) for i in range(2)]
    vv = Va.ap.rearrange("(t p) c -> p t c", p=128)
    SC = 1.0 / np.sqrt(96.0)

    def load_head(h):
        i = h % 2
        P.dma("sp", kth[i], KT.with_ap(KT.ap[h]), sem=f"kh{i}")
        for c0 in range(0, NCH, 6):
            P.dma("sp", vah[i][:, c0:c0 + 6, 0:64], Va.with_ap(vv[:, c0:c0 + 6, h * 64:(h + 1) * 64]), sem=f"kh{i}")

    def att_gen():
        it = 0
        nb = 0
        load_head(0)
        for h in range(8):
            P.dma("sp", qh1, QT.with_ap(QT.ap[h]), sem="qh")
            if h + 1 < 8:
                load_head(h + 1)
            i = h % 2
            for (g, n, isc) in blocks:
                po = psO[nb % 2]
                tiles = range(NTC // 128) if isc else range(NCH)
                last = tiles[-1]
                for kt in tiles:
                    pS = psS[it % 2]
                    P.mm(pS[:, :n], kth[i][:, kt * 128:(kt + 1) * 128], qh[i][:, g:g + n])
                    E = Et[it % 3]
                    P.act(E[:, :n], pS[:, :n], AF.Exp, scale=SC)
                    P.mm(po[:, :n], vah[i][:, kt, :], E[:, :n], start=(kt == 0), stop=(kt == last))
                    it += 1
                    yield
                r = rc[nb % 2]
                P.recip(r[64:128, :n], po[64:128, :n])
                o = ob[nb % 2]
                P.tt("dve", o[:, :n], po[0:64, :n], r[64:128, :n], ALU.mult)
                P.dma("sp", attT.with_ap(attT.ap[h * 64:(h + 1) * 64, g:g + n]), o[:, :n], sem="ao")
                nb += 1

    gt = sbt(P, "gt", [128, NCH, 4], F32)
    mq = sbt(P, "mq", [128, NCH, 128], BF16)
    mk = sbt(P, "mk", [128, NCH, 128], BF16)
    mv = sbt(P, "mv", [128, NCH, 128], BF16)
    HS = sbt(P, "HS", [128, NK], F32)
    HSb = [Buf(f"HS{c}") for c in range(NCH)]
    lf = [sbt(P, f"lf{d}", [128, NCH], F32) for d in range(2)]
    EQ = [sbt(P, f"eq{d}", [128, NCH], F32) for d in range(2)]
    EK = [sbt(P, f"ek{d}", [128, NCH], F32) for d in range(2)]
    EL = [sbt(P, f"el{d}", [128, NCH], F32) for d in range(2)]
    CN = [sbt(P, f"CN{d}", [128, 256], F32) for d in range(2)]
    CNb = [sbt(P, f"CNb{d}", [128, 256], BF16) for d in range(2)]
    qs = [[sbt(P, f"qs{d}{i}", [128, 128], BF16) for i in range(2)] for d in range(2)]
    ks = [[sbt(P, f"ks{d}{i}", [128, 128], BF16) for i in range(2)] for d in range(2)]
    qkT = [[sbt(P, f"qkT{d}{i}", [128, 256], BF16) for i in range(2)] for d in range(2)]
    Sm = [[sbt(P, f"Sm{d}{i}", [128, 128], BF16) for i in range(2)] for d in range(2)]
    dn = [sbt(P, f"dn{d}", [128, 128], F32) for d in range(2)]
    hc = [sbt(P, f"hc{d}", [128, 128], F32) for d in range(2)]
    tmpc = [sbt(P, f"tmpc{d}", [128, 256], F32) for d in range(2)]
    order = [list(range(NCH)), [1, 0] + list(range(NCH - 1, 1, -1))]
    g4v = GT.ap.rearrange("(c p) (g h) -> p c g h", p=128, h=4)

    def ml_setup(h):
        for c0 in range(0, NCH, 6):
            for g_ in range(4):
                P.dma("sp", gt[:, c0:c0 + 6, g_], GT.with_ap(g4v[:, c0:c0 + 6, g_, h]), sem="mlc", allow_slow_non_contiguous=True)
        for g in range(4):
            P.ts("dve", gt[:, :, g], gt[:, :, g], gb16[:, g * 4 + h:g * 4 + h + 1], None, ALU.add)
        for t_, d_ in ((mq, MQd), (mk, MKd), (mv, MVd)):
            dv_ = d_.ap.rearrange("(c p) d -> p c d", p=128)
            for c0 in range(0, NCH, 6):
                P.dma("sp", t_[:, c0:c0 + 6, :], d_.with_ap(dv_[:, c0:c0 + 6, h * 128:(h + 1) * 128]), sem="mlc")
        for d in range(2):
            P.act(lf[d], gt[:, :, 2 * d + 1], AF.Exp, scale=-1.0)
            P.ts("dve", lf[d], lf[d], 1.0, None, ALU.add)
            P.act(lf[d], lf[d], AF.Ln)
            P.ts("dve", lf[d], lf[d], -1.0, None, ALU.mult)
            pB = psG[2]
            pTt = psG[3]
            P.mm(pB[:, :NCH], tri[d], lf[d])
            P.mm(pTt[:, :NCH], ones_f, lf[d])
            P.act(EQ[d], pB[:, :NCH], AF.Exp)
            P.tt("dve", EK[d], gt[:, :, 2 * d], pB[:, :NCH], ALU.subtract)
            P.act(EK[d], EK[d], AF.Exp)
            P.ts("dve", EK[d], EK[d], float(128.0 ** -0.5), None, ALU.mult)
            P.act(EL[d], pTt[:, :NCH], AF.Exp)
            P.memset("pool", CN[d], 0.0)
            P.memset("pool", CNb[d], 0.0)

    def pre(d, c, i):
        P.ts("dve", qs[d][i], mq[:, c, :], EQ[d][:, c:c + 1], None, ALU.mult)
        P.act(ks[d][i], mk[:, c, :], AF.Identity, scale=EK[d][:, c:c + 1])
        pT = psT[d]
        P.transpose(pT[:, 0:128], qs[d][i], ident)
        P.transpose(pT[:, 128:256], ks[d][i], ident)
        P.copy("act", qkT[d][i], pT[:, 0:256])
        pS = psG[d]
        P.mm(pS, qkT[d][i][:, 128:256], qkT[d][i][:, 0:128])
        P.tt("dve", Sm[d][i], pS, tri[d], ALU.mult)

    def post(h, written, d, c, i):
        pD = psD[d]
        P.mm(pD[:, 0:128], ks[d][i], mv[:, c, :])
        P.mm(pD[:, 128:256], ks[d][i], ones_bf)
        pN = psN[d]
        P.mm(pN[:, 0:128], mv[:, c, :], Sm[d][i], start=True, stop=False)
        P.mm(pN[:, 0:128], CNb[d][:, 0:128], qkT[d][i][:, 0:128], start=False, stop=True)
        P.mm(pN[:, 128:256], ones_bf, Sm[d][i], start=True, stop=False)
        P.mm(pN[:, 128:256], CNb[d][:, 128:256], qkT[d][i][:, 0:128], start=False, stop=True)
        P.act(dn[d], pN[:, 128:256], AF.Abs)
        P.ts("dve", dn[d], dn[d], 1.0, None, ALU.max)
        P.recip(dn[d], dn[d])
        hsl = T(HS.ap[:, c * 128:(c + 1) * 128], HSb[c])
        if c in written:
            P.tt("dve", hc[d], pN[:, 0:128], dn[d], ALU.mult)
            P.tt("pool", hsl, hsl, hc[d], ALU.add)
            P.dma("sp", HT.with_ap(HT.ap[h * 128:(h + 1) * 128, c * 128:(c + 1) * 128]), hsl, sem="ho")
        else:
            P.tt("dve", hsl, pN[:, 0:128], dn[d], ALU.mult)
            written.add(c)
        P.tt("dve", tmpc[d], CN[d], psD[d], ALU.add)
        P.ts("dve", CN[d], tmpc[d], EL[d][:, c:c + 1], None, ALU.mult)
        P.copy("act", CNb[d], CN[d])

    def ml_gen():
        for h in range(4):
            ml_setup(h)
            written = set()
            for d in range(2):
                pre(d, order[d][0], 0)
            for st in range(NCH):
                for d in range(2):
                    if st + 1 < NCH:
                        pre(d, order[d][st + 1], (st + 1) % 2)
                    post(h, written, d, order[d][st], st % 2)
                yield

    a = att_gen()
    m = ml_gen()
    for i, _ in enumerate(a):
        if i % 30 == 5:
            next(m, None)
    for _ in m:
        pass


def phase_C(P, dr, seg):
    S = scratch(dr)
    C = setup_seg(P, seg)
    xin, attT, hmT, oT, xo, ggT, xrT = (S[k] for k in ("XA", "attT", "HT", "OT", "XC", "ggT", "xrT"))
    wout_d = dr.inp("even_w_out", [1024, 1024])
    outg_d = dr.inp("outg", [128, 4])
    mods0 = load_modsF(P, dr, C, 0)
    mods1 = load_modsF(P, dr, C, 1)
    fw02 = ffn_wF(dr, 0, 1)
    fw11 = ffn_wF(dr, 1, 0)
    win_d = dr.inp("odd_w_in", [1024, 2048])
    load_Xg(P, C, xin)
    wslab = C.wh[:, :, :].rearrange("p s c -> p (s c)")
    allw = [b for t in C.wslots for b in t.bufs]
    wout = T(wslab[:, 0:8192].rearrange("p (k c) -> p k c", k=8), allw)
    wv = wout_d.ap.rearrange("(k p) c -> p k c", p=128)
    for k in range(8):
        P.dma("pool", wout[:, k, :], T(wv[:, k, :], WDRAM), sem="win")
    outg = wload(P, "outg", outg_d.ap, [128, 4], F32)
    mixb = [sbt(P, f"mix{k}", [128, 512], BF16) for k in range(8)]
    hmt = [sbt(P, f"hmt{i}", [128, 512], F32) for i in range(2)]
    ott = [sbt(P, f"ott{i}", [128, 512], F32) for i in range(2)]
    rr = [sbt(P, f"rrc{i}", [128, 512], F32) for i in range(2)]

    def mixf(bi):
        s, n, isc = C.blocks[bi]
        g = C.gcol[bi]
        for j in range(4):
            P.dma("pool", mixb[j][:, :n], attT.with_ap(attT.ap[j * 128:(j + 1) * 128, g:g + n]), sem="mx")
            hm = hmt[j % 2]
            ot = ott[j % 2]
            P.dma("sp", hm[:, :n], hmT.with_ap(hmT.ap[j * 128:(j + 1) * 128, g:g + n]), sem="mx2")
            P.dma("sp", ot[:, :n], oT.with_ap(oT.ap[j * 128:(j + 1) * 128, g:g + n]), sem="mx2")
            sq = C.sq[j % 2]
            P.act(sq[:, :n], hm[:, :n], AF.Square)
            pss = C.ps[6]
            P.mm(pss[:, :n], C.ones_bf, sq[:, :n])
            r = rr[j % 2]
            P.ts("dve", r[:, :n], pss[:, :n], 1.0 / 128, EPS, ALU.mult, ALU.add)
            P.act(r[:, :n], r[:, :n], AF.Sqrt)
            P.recip(r[:, :n], r[:, :n])
            P.tt("dve", hm[:, :n], hm[:, :n], r[:, :n], ALU.mult)
            P.act(ot[:, :n], ot[:, :n], AF.Sigmoid)
            P.stt("dve", mixb[4 + j][:, :n], ot[:, :n], outg[:, j:j + 1], hm[:, :n], ALU.mult, ALU.mult)
        return mixb

    mix_out(P, C, wout, mixf, mods0[1])
    P.barrier()
    ffn(P, C, fw02[0], fw02[1], fw02[2], mods0[2])
    ffn(P, C, fw11[0], fw11[1], fw11[2], mods1[0])
    store_Xg(P, C, xo, latent_only=True)
    for bi in range(len(C.blocks)):
        isc = C.blocks[bi][2]
        norm_mod(P, C, bi, mods1[1][isc][0], mods1[1][isc][1])
    P.barrier()
    win = T(wslab[:, 0:8 * 2048].rearrange("p (k c) -> p k c", k=8), allw)
    wv = win_d.ap.rearrange("(k p) c -> p k c", p=128)
    for k in range(8):
        P.dma("pool", win[:, k, :], T(wv[:, k, :], WDRAM), sem="win")
    ga = hmt
    gb_ = ott
    it = 0
    for bi, (s, n, isc) in enumerate(C.blocks):
        g = C.gcol[bi]
        H = C.H[bi]
        if not isc:
            for j in range(8):
                pp = C.ps[j % 2]
                for k in range(8):
                    P.mm(pp[:, :n], win[:, k, j * 128:(j + 1) * 128], H[k], start=(k == 0), stop=(k == 7))
                a = ga[it % 2]
                b = gb_[it % 2]
                it += 1
                P.act(a[:, :n], pp[:, :n], AF.Square)
                P.ts("dve", a[:, :n], a[:, :n], 0.044715, 1.0, ALU.mult, ALU.add)
                P.tt("dve", a[:, :n], a[:, :n], pp[:, :n], ALU.mult)
                P.act(b[:, :n], a[:, :n], AF.Sigmoid, scale=1.5957691216057308)
                P.tt("dve", b[:, :n], b[:, :n], pp[:, :n], ALU.mult)
                P.dma("sp", ggT.with_ap(ggT.ap[j * 128:(j + 1) * 128, g:g + n]), b[:, :n], sem="go")
        for j in range(8):
            pp = C.ps[2 + j % 2]
            for k in range(8):
                P.mm(pp[:, :n], win[:, k, 1024 + j * 128:1024 + (j + 1) * 128], H[k], start=(k == 0), stop=(k == 7))
            a = ga[it % 2]
            it += 1
            P.copy("act", a[:, :n], pp[:, :n])
            P.dma("sp", xrT.with_ap(xrT.ap[j * 128:(j + 1) * 128, g:g + n]), a[:, :n], sem="go")


def phase_D(P, dr):
    S = scratch(dr)
    xrT, hf, hb = S["xrT"], S["hf"], S["hb"]
    cw_d = dr.inp("cw", [8, 128, 4])
    cb_d = dr.inp("cb", [8, 128, 1])
    wa_d = dr.inp("lru_w_a", [1, 2, 8, 128, 128])
    wx_d = dr.inp("lru_w_x", [1, 2, 8, 128, 128])
    bab_d = dr.inp("bab", [8, 128, 6])
    psA = psplit(P, "psA", F32, 512, 2)
    psX = psplit(P, "psX", F32, 512, 2)
    ub = [sbt(P, f"ub{i}", [128, TS + 4], F32) for i in range(2)]
    xc = [sbt(P, f"xc{i}", [128, TS], F32) for i in range(2)]
    rt = sbt(P, "rt", [128, TS], F32)
    ig = sbt(P, "ig", [128, TS], F32)
    at = sbt(P, "at", [128, TS], F32)
    om = sbt(P, "om", [128, TS], F32)
    ip = sbt(P, "ip", [128, TS], F32)
    ht = [sbt(P, f"ht{i}", [128, TS], F32) for i in range(2)]
    cw = sbt(P, "cw", [128, 4], F32)
    cb = sbt(P, "cb", [128, 1], F32)
    wa = [sbt(P, f"wa{d}", [128, 128], F32) for d in range(2)]
    wx = [sbt(P, f"wx{d}", [128, 128], F32) for d in range(2)]
    bab = sbt(P, "bab", [128, 6], F32)
    cc = sbt(P, "cc", [128, 2], F32)
    it = 0
    offs = [j - 2 for j in range(4)]
    for nb in range(8):
        rows = slice(nb * 128, (nb + 1) * 128)
        P.dma("sp", cw, cw_d.with_ap(cw_d.ap[nb]), sem="dc")
        P.dma("sp", cb, cb_d.with_ap(cb_d.ap[nb]), sem="dc")
        P.dma("sp", bab, bab_d.with_ap(bab_d.ap[nb]), sem="dc")
        for d in range(2):
            P.dma("sp", wa[d], wa_d.with_ap(wa_d.ap[0, d, nb]), sem="dc")
            P.dma("sp", wx[d], wx_d.with_ap(wx_d.ap[0, d, nb]), sem="dc")
        P.act(cc, bab[:, 4:6], AF.Exp, scale=-1.0)
        P.ts("dve", cc, cc, 1.0, None, ALU.add)
        P.act(cc, cc, AF.Ln)
        P.ts("dve", cc, cc, -8.0, None, ALU.mult)
        for d in range(2):
            dst = hf if d == 0 else hb
            lat = [(NTC, 4 * NTL, c0) for c0 in range(0, 4 * NTL, TS)]
            if d == 1:
                lat = lat[::-1]
            tiles = [(0, NTC, 0)] + lat
            prev = None
            for (seg0, segn, c0) in tiles:
                n = min(TS, segn - c0)
                u = ub[it % 2]
                x = xc[it % 2]
                h = ht[it % 2]
                it += 1
                lo = max(c0 - 2, 0)
                hi = min(c0 + n + 2, segn)
                if lo > c0 - 2:
                    P.memset("pool", u[:, 0:2], 0.0)
                if hi < c0 + n + 2:
                    P.memset("pool", u[:, n + 2:n + 4], 0.0)
                P.dma("sp", u[:, 2 + (lo - c0):2 + (hi - c0)], xrT.with_ap(xrT.ap[rows, seg0 + lo:seg0 + hi]), sem=f"u{it % 2}")
                P.act(x[:, :n], u[:, 2 + offs[0]:2 + offs[0] + n], AF.Identity, bias=cb[:, 0:1], scale=cw[:, 0:1])
                for j in range(1, 4):
                    P.stt("dve", x[:, :n], u[:, 2 + offs[j]:2 + offs[j] + n], cw[:, j:j + 1], x[:, :n], ALU.mult, ALU.add)
                for q0 in range(0, n, 512):
                    qn_ = min(512, n - q0)
                    pa = psA[(q0 // 512) % 2]
                    px = psX[(q0 // 512) % 2]
                    P.mm(pa[:, :qn_], wa[d], x[:, q0:q0 + qn_])
                    P.act(rt[:, q0:q0 + qn_], pa[:, :qn_], AF.Sigmoid, bias=bab[:, d:d + 1])
                    P.mm(px[:, :qn_], wx[d], x[:, q0:q0 + qn_])
                    P.act(ig[:, q0:q0 + qn_], px[:, :qn_], AF.Sigmoid, bias=bab[:, 2 + d:3 + d])
                P.act(at[:, :n], rt[:, :n], AF.Exp, scale=cc[:, d:d + 1])
                P.tt("pool", om[:, :n], at[:, :n], at[:, :n], ALU.mult)
                P.ts("dve", om[:, :n], om[:, :n], -1.0, 1.0, ALU.mult, ALU.add)
                P.act(om[:, :n], om[:, :n], AF.Sqrt)
                P.tt("pool", ip[:, :n], ig[:, :n], x[:, :n], ALU.mult)
                P.tt("dve", ip[:, :n], ip[:, :n], om[:, :n], ALU.mult)
                init = 0.0 if prev is None else prev
                if d == 0:
                    P.scan(h[:, :n], at[:, :n], ip[:, :n], init, ALU.mult, ALU.add)
                    prev = h[:, n - 1:n]
                else:
                    P.scan(h.with_ap(h.ap[:, :n][:, ::-1]), at.with_ap(at.ap[:, :n][:, ::-1]),
                           ip.with_ap(ip.ap[:, :n][:, ::-1]), init, ALU.mult, ALU.add)
                    prev = h[:, 0:1]
                if seg0 > 0:
                    P.dma("sp", dst.with_ap(dst.ap[rows, c0:c0 + n]), h[:, :n], sem="ho")


def phase_E(P, dr, seg):
    S = scratch(dr)
    C = setup_common(P, NTL, 0)
    C.gcol = [NTC + seg * NTL + s for s in range(0, NTL, 512)]
    xin, ggT, hfT, hbT = S["XC"], S["ggT"], S["hf"], S["hb"]
    wout_d = dr.inp("odd_w_out", [1024, 1024])
    mods1 = load_modsF(P, dr, C, 1)
    fw12 = ffn_wF(dr, 1, 1)
    xo = dr.out("out", [1024, 4 * NTL])
    load_Xg(P, C, xin)
    wslab = C.wh[:, :, :].rearrange("p s c -> p (s c)")
    allw = [b for t in C.wslots for b in t.bufs]
    wout = T(wslab[:, 0:8192].rearrange("p (k c) -> p k c", k=8), allw)
    wv = wout_d.ap.rearrange("(k p) c -> p k c", p=128)
    for k in range(8):
        P.dma("pool", wout[:, k, :], T(wv[:, k, :], WDRAM), sem="win")
    mixb = [sbt(P, f"mix{k}", [128, 512], BF16) for k in range(8)]
    ta = [sbt(P, f"ta{i}", [128, 512], F32) for i in range(2)]
    tb = [sbt(P, f"tb{i}", [128, 512], F32) for i in range(2)]
    tg = [sbt(P, f"tg{i}", [128, 512], F32) for i in range(2)]

    def mixf(bi):
        s, n, isc = C.blocks[bi]
        g = C.gcol[bi]
        for k in range(8):
            a, b_, gg_ = ta[k % 2], tb[k % 2], tg[k % 2]
            rows = slice(k * 128, (k + 1) * 128)
            P.dma("sp", a[:, :n], hfT.with_ap(hfT.ap[rows, g - NTC:g - NTC + n]), sem="mx")
            P.dma("sp", b_[:, :n], hbT.with_ap(hbT.ap[rows, g - NTC:g - NTC + n]), sem="mx")
            P.dma("sp", gg_[:, :n], ggT.with_ap(ggT.ap[rows, g:g + n]), sem="mx")
            P.tt("pool", a[:, :n], a[:, :n], b_[:, :n], ALU.add)
            P.tt("dve", mixb[k][:, :n], a[:, :n], gg_[:, :n], ALU.mult)
        return mixb

    mix_out(P, C, wout, mixf, mods1[1])
    P.barrier()
    ffn(P, C, fw12[0], fw12[1], fw12[2], mods1[2])
    store_Xg(P, C, xo, goff=NTC)


PHASES = None


def build_fused(nc):
    with contextlib.ExitStack() as semstack:
        P = Prog(nc, semstack)
        dr = DR(nc)
        plan = [("M", None)] + [("A", s) for s in range(NSEG)] + [("B", None)] + \
               [("C", s) for s in range(NSEG)] + [("D", None)] + [("E", s) for s in range(NSEG)]
        for nm, seg in plan:
            if PHASES is not None and nm not in PHASES:
                continue
            with contextlib.ExitStack() as pstack:
                P.stack = pstack
                if nm == "M":
                    phase_M(P, dr)
                elif nm == "A":
                    phase_A(P, dr, seg)
                elif nm == "B":
                    phase_B(P, dr)
                elif nm == "C":
                    phase_C(P, dr, seg)
                elif nm == "D":
                    phase_D(P, dr)
                else:
                    phase_E(P, dr, seg)
                P.barrier()
                P.emit()


def rope_tables_g():
    pos = np.arange(4 * NTL)
    rows = (pos // 64).astype(np.float32)
    cols = (pos % 64).astype(np.float32)
    inv = (np.float32(10000.0) ** (-np.arange(0, 16, 2, dtype=np.float32) / np.float32(16))).astype(np.float32)
    cosT = np.ones((96, NK), np.float32)
    sinT = np.zeros((96, NK), np.float32)
    for a, pp in enumerate((rows, cols)):
        ang = (pp[None, :] * inv[:, None]).astype(np.float32)
        for half in range(2):
            r0 = 64 + a * 16 + half * 8
            cosT[r0:r0 + 8, NTC:] = np.cos(ang)
            sinT[r0:r0 + 8, NTC:] = np.sin(ang)
    return cosT, sinT


def maps_fused(inp):
    oblk, rm, e32 = consts_A()
    triF, triB, ident = consts_B()
    cosT, sinT = rope_tables_g()
    ukv = inp["mla_w_ukv"][0].reshape(256, 8, 128)
    w_ukk = np.zeros((256, 8, 96), np.float32)
    w_ukk[:, :, :64] = ukv[:, :, :64]
    w_ukk = w_ukk.reshape(256, 768)
    w_ukv = np.ascontiguousarray(ukv[:, :, 64:].reshape(256, 512))
    c96 = np.zeros((96, 3), np.float32)
    c96[:, 0] = inp["mla_q_g"][0]
    c96[:, 1] = inp["mla_k_g"][0]
    c96[:64, 2] = 1.0 / 64
    c96[64:, 2] = 1.0 / 32
    modb = np.ascontiguousarray(inp["mod_b"].reshape(2, 72, 128).transpose(2, 0, 1).reshape(128, 144))
    gb16 = np.ascontiguousarray(np.broadcast_to(inp["mlstm_gate_b"][0][None, :], (128, 16))).astype(np.float32)
    cw = np.ascontiguousarray(inp["odd_conv_w"][0].reshape(4, 8, 128).transpose(1, 2, 0))
    cb = np.ascontiguousarray(inp["odd_conv_b"][0].reshape(8, 128, 1))
    bab = np.stack([inp["lru_b_a"][0, 0], inp["lru_b_a"][0, 1], inp["lru_b_x"][0, 0],
                    inp["lru_b_x"][0, 1], inp["lru_lam"][0, 0], inp["lru_lam"][0, 1]], axis=1)
    bab = np.ascontiguousarray(bab.reshape(8, 128, 6).astype(np.float32))
    common = dict(mod_w=inp["mod_w"], modb=modb, normg0=normg_table(inp["norm_g"][0]), normg1=normg_table(inp["norm_g"][1]),
                  ffn_w_gate=inp["ffn_w_gate"], ffn_w_up=inp["ffn_w_up"], ffn_w_down=inp["ffn_w_down"],
                  w_in=inp["even_w_in"][0], w_uq=inp["mla_w_uq"][0], w_ukk=w_ukk, w_ukv=w_ukv,
                  cqg=fm(inp["mla_cq_g"][0]), ckvg=fm(inp["mla_ckv_g"][0]), c96=c96, oblk=oblk, rm=rm, e32=e32,
                  cosT=cosT, sinT=sinT, gb16=gb16, triF=triF, triB=triB, ident=ident,
                  even_w_out=inp["even_w_out"][0], outg=fm(inp["mlstm_out_g"][0]), odd_w_in=inp["odd_w_in"][0],
                  cw=cw, cb=cb, lru_w_a=inp["lru_w_a"], lru_w_x=inp["lru_w_x"], bab=bab, odd_w_out=inp["odd_w_out"][0])
    maps = []
    for r in range(8):
        b = r // 4
        xT = np.ascontiguousarray(np.concatenate([inp["ctx"][b].T, inp["x"][b].T], axis=1))
        vecs = np.stack([inp["c"][b], inp["c_ctx"]], 0)
        cv = np.ascontiguousarray(vecs.reshape(2, 8, 128).transpose(2, 1, 0).reshape(128, 16))
        mp = dict(common)
        mp["xT"] = xT
        mp["cv"] = cv
        maps.append(mp)
    return maps


def kernel(**inp):
    inp = {k: np.asarray(v) for k, v in inp.items()}
    nc = bass.Bass("TRN2", target_bir_lowering=False)
    build_fused(nc)
    res = run_bass_kernel_spmd(nc, maps_fused(inp), core_ids=list(range(8))).results
    out = np.zeros((2, 4 * NTL, 1024), np.float32)
    for b in range(2):
        out[b] = np.asarray(res[4 * b]["out"], np.float32).T
    return out
```

```python
import contextlib
import numpy as np
import concourse.bass as bass
import concourse.mybir as mybir
from concourse.bass_utils import run_bass_kernel_spmd

F32 = mybir.dt.float32
BF16 = mybir.dt.bfloat16
AF = mybir.ActivationFunctionType
ALU = mybir.AluOpType
AX = mybir.AxisListType


class Buf:
    __slots__ = ("name", "writer", "readers", "excl", "nowaw")

    def __init__(self, name="", excl=False, nowaw=False):
        self.name = name
        self.nowaw = nowaw
        self.writer = None
        self.readers = []
        self.excl = excl


class Sem:
    __slots__ = ("h", "total", "dma", "name")

    def __init__(self, h, dma, name):
        self.h = h
        self.total = 0
        self.dma = dma
        self.name = name


class T:
    __slots__ = ("ap", "bufs")

    def __init__(self, ap, bufs):
        self.ap = ap
        self.bufs = bufs if isinstance(bufs, (list, tuple)) else [bufs]

    def __getitem__(self, idx):
        return T(self.ap[idx], self.bufs)

    def with_ap(self, ap):
        return T(ap, self.bufs)


ENGS = ("pe", "act", "dve", "pool", "sp")


class Prog:
    def __init__(self, nc, stack):
        self.nc = nc
        self.stack = stack
        self.sem_stack = stack
        self.ops = {e: [] for e in ENGS}
        self.esem = {e: self.new_sem("e_" + e, False) for e in ENGS}
        self.dsems = {}
        self.seen = {e: {} for e in ENGS}
        self.nops = 0

    def new_sem(self, name, dma):
        h = self.sem_stack.enter_context(self.nc.semaphore(name))
        return Sem(h, dma, name)

    def dsem(self, name):
        if name not in self.dsems:
            self.dsems[name] = self.new_sem("d_" + name, True)
        return self.dsems[name]

    def _record(self, eng, fn, reads, writes, sem=None):
        waits = {}

        def need(tok, kind):
            if tok is None:
                return
            s, v, peng = tok
            if not s.dma and peng == eng:
                if kind != "raw":
                    return
                if eng == "pe":
                    return
            if s.dma:
                v = max(v, s.total)
            if self.seen[eng].get(s, 0) >= v:
                return
            if waits.get(s, 0) < v:
                waits[s] = v

        rb = []
        xb = []
        for t in reads:
            if t is None or isinstance(t, (int, float)):
                continue
            for b in t.bufs:
                need(b.writer, "raw")
                if b.excl:
                    for r in b.readers:
                        need(r, "war")
                    xb.append(b)
                else:
                    rb.append(b)
        wb = []
        for t in writes:
            if t is None:
                continue
            for b in t.bufs:
                wb.append(b)
                if not b.nowaw:
                    need(b.writer, "waw")
                for r in b.readers:
                    need(r, "war")
        for s, v in waits.items():
            self.seen[eng][s] = v
        if sem is None:
            s = self.esem[eng]
            s.total += 1
        else:
            s = sem
            s.total += 16
        tok = (s, s.total, eng)
        for b in rb:
            b.readers.append(tok)
        for b in wb:
            b.writer = tok
            b.readers = []
        for b in xb:
            if b.writer is not tok:
                b.readers.append(tok)
                b.writer = (tok[0], tok[1], tok[2]) if False else b.writer
                b.writer = tok
                b.readers = []
        self.ops[eng].append((list(waits.items()), fn, s, 16 if s.dma else 1))
        self.nops += 1

    def barrier(self):
        sems = list(self.esem.values()) + list(self.dsems.values())
        for e in ENGS:
            waits = []
            for s in sems:
                if s.total > self.seen[e].get(s, 0) and not (s is self.esem[e]):
                    waits.append((s, s.total))
                    self.seen[e][s] = s.total
            if waits:
                self.ops[e].append((waits, None, None, 0))

    def emit(self):
        nc = self.nc
        ops = self.ops

        def run(eng, lst):
            for waits, fn, s, inc in lst:
                for ws, wv in waits:
                    eng.wait_ge(ws.h, wv)
                if fn is not None:
                    fn(eng).then_inc(s.h, inc)

        with nc.Block() as block:
            @block.tensor
            def _(e):
                run(e, ops["pe"])

            @block.scalar
            def _(e):
                run(e, ops["act"])

            @block.vector
            def _(e):
                run(e, ops["dve"])

            @block.gpsimd
            def _(e):
                run(e, ops["pool"])

            @block.sync
            def _(e):
                run(e, ops["sp"])

        self.ops = {e: [] for e in ENGS}

    def dma(self, q, out, in_, sem="ld", **kw):
        s = self.dsem(sem)
        self._record(q, lambda e: e.dma_start(out=out.ap, in_=in_.ap, **kw), [in_], [out], sem=s)

    def mm(self, out, lhsT, rhs, start=True, stop=True):
        self._record("pe", lambda e: e.matmul(out.ap, lhsT.ap, rhs.ap, start=start, stop=stop),
                     [lhsT, rhs], [out])

    def transpose(self, out, in_, ident):
        self._record("pe", lambda e: e.transpose(out.ap, in_.ap, ident.ap), [in_, ident], [out])

    def act(self, out, in_, func, bias=None, scale=1.0, accum=None, eng="act"):
        kw = {}
        if bias is not None:
            kw["bias"] = bias.ap if isinstance(bias, T) else bias
        kw["scale"] = scale.ap if isinstance(scale, T) else scale
        if accum is not None:
            kw["accum_out"] = accum.ap
        self._record("act", lambda e: e.activation(out.ap, in_.ap, func, **kw),
                     [in_, bias if isinstance(bias, T) else None, scale if isinstance(scale, T) else None],
                     [out, accum])

    def tt(self, eng, out, in0, in1, op):
        self._record(eng, lambda e: e.tensor_tensor(out.ap, in0.ap, in1.ap, op), [in0, in1], [out])

    def ts(self, eng, out, in0, s1, s2, op0, op1=None):
        a1 = s1.ap if isinstance(s1, T) else s1
        a2 = s2.ap if isinstance(s2, T) else s2
        if op1 is None:
            f = lambda e: e.tensor_scalar(out.ap, in0.ap, a1, None, op0)
        else:
            f = lambda e: e.tensor_scalar(out.ap, in0.ap, a1, a2, op0, op1)
        self._record(eng, f, [in0, s1 if isinstance(s1, T) else None, s2 if isinstance(s2, T) else None], [out])

    def stt(self, eng, out, in0, scalar, in1, op0, op1):
        a = scalar.ap if isinstance(scalar, T) else scalar
        self._record(eng, lambda e: e.scalar_tensor_tensor(out.ap, in0.ap, a, in1.ap, op0, op1),
                     [in0, in1, scalar if isinstance(scalar, T) else None], [out])

    def copy(self, eng, out, in_):
        if eng == "act":
            self._record(eng, lambda e: e.copy(out.ap, in_.ap), [in_], [out])
        else:
            self._record(eng, lambda e: e.tensor_copy(out.ap, in_.ap), [in_], [out])

    def memset(self, eng, out, val):
        self._record(eng, lambda e: e.memset(out.ap, val), [], [out])

    def recip(self, out, in_):
        self._record("dve", lambda e: e.reciprocal(out.ap, in_.ap), [in_], [out])

    def scan(self, out, d0, d1, init, op0, op1):
        a = init.ap if isinstance(init, T) else init
        self._record("dve", lambda e: e.tensor_tensor_scan(out.ap, d0.ap, d1.ap, a, op0, op1),
                     [d0, d1, init if isinstance(init, T) else None], [out])


D = 1024
DFF = 2816
NFF = 22
EPS = 1e-6


class Ctx:
    pass


WDRAM = Buf("wdram")


_UID = [0]


def uname(name):
    _UID[0] += 1
    return f"{name}_u{_UID[0]}"


def sbt(P, name, shape, dt, nb=None):
    h = P.stack.enter_context(P.nc.sbuf_tensor(uname("s_" + name), shape, dt))
    return T(h[tuple(slice(None) for _ in shape)], Buf(name))


def setup_common(P, NTL, NTC):
    nc = P.nc
    C = Ctx()
    C.NTL, C.NTC = NTL, NTC
    NT = NTL + NTC
    C.NT = NT
    C.blocks = [(s, 512, False) for s in range(0, NTL, 512)] + ([(NTL, NTC, True)] if NTC else [])
    xh = P.stack.enter_context(nc.sbuf_tensor(uname("X"), [128, 8, NT], F32))
    C.xh = xh
    hh = P.stack.enter_context(nc.sbuf_tensor(uname("hT"), [128, 8, NT], BF16))
    C.X = [[T(xh[:, k, s:s + n], Buf(f"X{bi}_{k}")) for k in range(8)] for bi, (s, n, c) in enumerate(C.blocks)]
    C.H = [[T(hh[:, k, s:s + n], Buf(f"H{bi}_{k}")) for k in range(8)] for bi, (s, n, c) in enumerate(C.blocks)]
    C.ps = []
    for i in range(8):
        ph = P.stack.enter_context(nc.psum_tensor(uname(f"ps{i}"), [128, 512], F32))
        C.ps.append(T(ph[:, :], Buf(f"ps{i}", excl=True)))
    C.ones_bf = sbt(P, "ones_bf", [128, 128], BF16)
    P.memset("pool", C.ones_bf, 1.0)
    C.NSLOT = 6
    C.SLOTSZ = 4096
    wh = P.stack.enter_context(nc.sbuf_tensor(uname("wslots"), [128, C.NSLOT, C.SLOTSZ], BF16))
    C.wh = wh
    C.wslots = [T(wh[:, i, :], Buf(f"wslot{i}")) for i in range(C.NSLOT)]
    C.wi = 0
    C.wq = []
    C.wloaded = []
    C.sq = [sbt(P, f"sq{i}", [128, 512], BF16) for i in range(2)]
    C.rstd = [sbt(P, f"rstd{i}", [128, 512], F32) for i in range(2)]
    C.tmpf = [sbt(P, f"tmpf{i}", [128, 512], F32) for i in range(3)]
    C.sg = [sbt(P, f"sg{i}", [128, 512], F32) for i in range(2)]
    C.actb = [[sbt(P, f"act{i}_{j}", [128, 512], BF16) for j in range(4)] for i in range(2)]
    C.cnt = 0
    return C


def wq_push(C, dram_ap, K, cols):
    C.wq.append((dram_ap, K, cols))


def wq_issue(P, C):
    if not C.wq:
        return
    dram_ap, K, cols = C.wq.pop(0)
    si = C.wi % C.NSLOT
    C.wi += 1
    slot = C.wslots[si]
    view = slot.with_ap(slot.ap[:, 0:K * cols].rearrange("p (k c) -> p k c", k=K))
    P.dma("pool", view, T(dram_ap, WDRAM), sem=f"w{si}")
    C.wloaded.append(view)


def wq_prefetch(P, C, n):
    for _ in range(n):
        wq_issue(P, C)


def wq_get(P, C):
    return C.wloaded.pop(0)


def norm_mod(P, C, bi, A, B):
    s, n, isc = C.blocks[bi]
    i = C.cnt
    C.cnt += 1
    pss = C.ps[6]
    sq = C.sq
    for k in range(8):
        q = sq[k % 2]
        P.act(q[:, :n], C.X[bi][k], AF.Square)
        P.mm(pss[:, :n], C.ones_bf, q[:, :n], start=(k == 0), stop=(k == 7))
    r = C.rstd[i % 2]
    P.ts("dve", r[:, :n], pss[:, :n], 1.0 / D, EPS, ALU.mult, ALU.add)
    P.act(r[:, :n], r[:, :n], AF.Sqrt)
    P.recip(r[:, :n], r[:, :n])
    for k in range(8):
        t = C.tmpf[k % 3]
        P.tt("dve", t[:, :n], C.X[bi][k], r[:, :n], ALU.mult)
        P.act(C.H[bi][k], t[:, :n], AF.Identity, bias=B[:, k:k + 1], scale=A[:, k:k + 1])


def ffn(P, C, wg_d, wu_d, wd_d, mods, groups=(4, 4, 4, 4, 4, 2)):
    for bi in range(len(C.blocks)):
        isc = C.blocks[bi][2]
        norm_mod(P, C, bi, mods[isc][0], mods[isc][1])
    j0 = 0
    for g in groups:
        wq_push(C, wg_d[:, :, j0 * 128:(j0 + g) * 128], 8, g * 128)
        wq_push(C, wu_d[:, :, j0 * 128:(j0 + g) * 128], 8, g * 128)
        wq_push(C, wd_d[:, j0:j0 + g, :], g, 1024)
        j0 += g
    items = []
    for g in groups:
        for bi in range(len(C.blocks)):
            items.append((g, bi))
    state = {}

    def gateup(it, idx):
        g, bi = it
        s, n, isc = C.blocks[bi]
        if bi == 0:
            state["w"] = (wq_get(P, C), wq_get(P, C), wq_get(P, C))
        wg, wu, wd = state["w"]
        acts = C.actb[idx % 2]
        for jj in range(g):
            pg = C.ps[(2 * jj) % 4]
            pu = C.ps[(2 * jj + 1) % 4]
            for k in range(8):
                P.mm(pg[:, :n], wg[:, k, jj * 128:(jj + 1) * 128], C.H[bi][k], start=(k == 0), stop=(k == 7))
            for k in range(8):
                P.mm(pu[:, :n], wu[:, k, jj * 128:(jj + 1) * 128], C.H[bi][k], start=(k == 0), stop=(k == 7))
            sg = C.sg[jj % 2]
            P.act(sg[:, :n], pg[:, :n], AF.Silu)
            P.tt("dve", acts[jj][:, :n], sg[:, :n], pu[:, :n], ALU.mult)
        return (g, bi, wd, acts)

    def down(st):
        g, bi, wd, acts = st
        s, n, isc = C.blocks[bi]
        G = mods[isc][2]
        for m in range(8):
            py = C.ps[4 + m % 2]
            for jj in range(g):
                P.mm(py[:, :n], wd[:, jj, m * 128:(m + 1) * 128], acts[jj][:, :n], start=(jj == 0), stop=(jj == g - 1))
            P.stt("dve", C.X[bi][m], py[:, :n], G[:, m:m + 1], C.X[bi][m], ALU.mult, ALU.add)

    prev = None
    nb = len(C.blocks)
    wq_prefetch(P, C, 6)
    for idx, it in enumerate(items):
        cur = gateup(it, idx)
        if prev is not None:
            down(prev)
            if prev[1] == nb - 1:
                wq_prefetch(P, C, 3)
        prev = cur
    down(prev)


import ml_dtypes

NPBF = ml_dtypes.bfloat16


class DR:
    def __init__(self, nc):
        self.nc = nc
        self.bufs = {}

    def inp(self, name, shape, dt=F32):
        if name not in self.bufs:
            ap = self.nc.dram_tensor(name, list(shape), dt, kind="ExternalInput").ap()
            self.bufs[name] = T(ap, Buf(name))
        return self.bufs[name]

    def out(self, name, shape, dt=F32):
        if name not in self.bufs:
            ap = self.nc.dram_tensor(name, list(shape), dt, kind="ExternalOutput").ap()
            self.bufs[name] = T(ap, Buf(name, nowaw=True))
        return self.bufs[name]

    def tmp(self, name, shape, dt=F32):
        if name not in self.bufs:
            ap = self.nc.dram_tensor(name, list(shape), dt).ap()
            self.bufs[name] = T(ap, Buf(name, nowaw=True))
        return self.bufs[name]


def launch(build, in_maps):
    nc = bass.Bass("TRN2", target_bir_lowering=False)
    with contextlib.ExitStack() as stack:
        P = Prog(nc, stack)
        build(P, DR(nc))
        P.barrier()
        P.emit()
    res = run_bass_kernel_spmd(nc, in_maps, core_ids=list(range(8)))
    return res.results


def fm(v):
    v = np.asarray(v, np.float32)
    return np.ascontiguousarray(v.reshape(-1, 128).T)


def build_M(P, dr):
    modw = dr.inp("modw", [2, 1024, 1152])
    modb = dr.inp("modb", [128, 18])
    cv = dr.inp("cv", [128, 24])
    outd = dr.out("mod_out", [128, 54])
    cvt = sbt(P, "cvt", [128, 8, 3], F32)
    scv = sbt(P, "scv", [128, 8, 3], F32)
    mb = sbt(P, "mb", [128, 18], F32)
    res = sbt(P, "res", [128, 18, 3], F32)
    P.dma("sp", cvt, cv.with_ap(cv.ap.rearrange("p (k v) -> p k v", v=3)), sem="c")
    P.dma("sp", mb, modb, sem="c")
    P.act(scv, cvt, AF.Silu)
    ph = P.stack.enter_context(P.nc.psum_tensor(uname("psm"), [128, 18, 3], F32))
    ps = T(ph[:, :, :], Buf("psm", excl=True))
    ws = [sbt(P, f"mw{i}", [128, 8, 384], F32) for i in range(2)]
    n = 0
    for l in range(2):
        wv = modw.ap[l].rearrange("(k p) c -> p k c", p=128)
        for pc in range(3):
            w = ws[n % 2]
            P.dma("sp", w, modw.with_ap(wv[:, :, pc * 384:(pc + 1) * 384]), sem=f"mw{n % 2}")
            n += 1
            for jj in range(3):
                j = l * 9 + pc * 3 + jj
                for k in range(8):
                    P.mm(ps[:, j, :], w[:, k, jj * 128:(jj + 1) * 128], scv[:, k, :], start=(k == 0), stop=(k == 7))
    for v in range(3):
        P.tt("dve", res[:, :, v], ps[:, :, v], mb, ALU.add)
    P.dma("sp", outd, res.with_ap(res.ap.rearrange("p j v -> p (j v)")), sem="o")


def run_M(inp):
    c, c_ctx, mod_w, mod_b = inp["c"], inp["c_ctx"], inp["mod_w"], inp["mod_b"]
    vecs = np.stack([c[0], c[1], c_ctx], 0)
    cv = np.ascontiguousarray(vecs.reshape(3, 8, 128).transpose(2, 1, 0).reshape(128, 24))
    maps = []
    for r in range(8):
        cs = slice(1152 * r, 1152 * (r + 1))
        mbr = mod_b[:, cs].reshape(2, 9, 128).transpose(2, 0, 1).reshape(128, 18)
        maps.append({"modw": np.ascontiguousarray(mod_w[:, :, cs]), "modb": np.ascontiguousarray(mbr), "cv": cv})
    res = launch(build_M, maps)
    m = np.zeros((2, 3, 9216), np.float32)
    for r in range(8):
        o = res[r]["mod_out"].reshape(128, 2, 9, 3)
        for l in range(2):
            blk = o[:, l].transpose(2, 1, 0).reshape(3, 1152)
            m[l, :, 1152 * r:1152 * (r + 1)] = blk
    return m


def mod_table(m, l, b):
    t = np.zeros((128, 9, 8, 2), np.float32)
    for xc, v in enumerate((b, 2)):
        t[:, :, :, xc] = m[l, v].reshape(9, 8, 128).transpose(2, 0, 1)
    return np.ascontiguousarray(t.reshape(128, 144))


NTL = 2048
NTC = 256
NT = NTL + NTC


def load_mods(P, dr, C, tag=""):
    modt_d = dr.inp("modt" + tag, [128, 144])
    ng_d = dr.inp("normg" + tag, [128, 24])
    mt = sbt(P, "mt" + tag, [128, 9, 8, 2], F32)
    ng = sbt(P, "ng" + tag, [128, 3, 8], F32)
    P.dma("sp", mt, modt_d.with_ap(modt_d.ap.rearrange("p (i k x) -> p i k x", i=9, k=8)), sem="c")
    P.dma("sp", ng, ng_d.with_ap(ng_d.ap.rearrange("p (i k) -> p i k", i=3)), sem="c")
    ab = sbt(P, "modAB" + tag, [128, 3, 2, 3, 8], F32)
    mods = []
    for i in range(3):
        row = {}
        for xc in range(2):
            A = ab[:, i, xc, 0, :]
            B = ab[:, i, xc, 1, :]
            G = ab[:, i, xc, 2, :]
            P.ts("dve", A, mt[:, 3 * i + 1, :, xc], 1.0, None, ALU.add)
            P.tt("dve", A, A, ng[:, i, :], ALU.mult)
            P.copy("dve", B, mt[:, 3 * i, :, xc])
            P.ts("dve", G, mt[:, 3 * i + 2, :, xc], 0.5 if i != 1 else 1.0, None, ALU.mult)
            row[bool(xc)] = (A, B, G)
        mods.append(row)
    return mods


def ffn_w(dr, tag):
    wg = dr.inp("wg" + tag, [1024, 2816])
    wu = dr.inp("wu" + tag, [1024, 2816])
    wd = dr.inp("wd" + tag, [2816, 1024])
    return (wg.ap.rearrange("(k p) c -> p k c", p=128), wu.ap.rearrange("(k p) c -> p k c", p=128),
            wd.ap.rearrange("(j p) c -> p j c", p=128))


def load_X(P, C, xin, nblk=None):
    xv = xin.ap.rearrange("(k p) t -> p k t", p=128)
    for bi, (s, n, c) in enumerate(C.blocks):
        for k in range(8):
            P.dma("sp", C.X[bi][k], xin.with_ap(xv[:, k, s:s + n]), sem="xin")


def store_X(P, C, xout):
    xv = xout.ap.rearrange("(k p) t -> p k t", p=128)
    for bi, (s, n, c) in enumerate(C.blocks):
        for k in range(8):
            P.dma("sp", xout.with_ap(xv[:, k, s:s + n]), C.X[bi][k], sem="xout")


def wload(P, name, dram_ap, shape, dt=BF16, q=None):
    t = sbt(P, name, shape, dt)
    P.dma("pool" if dt == BF16 else "sp", t, T(dram_ap, WDRAM), sem="c")
    return t


def build_A(P, dr):
    C = setup_common(P, NTL, NTC)
    xin = dr.inp("xT", [1024, NT])
    mods = load_mods(P, dr, C)
    fw = ffn_w(dr, "")
    w_in = dr.inp("w_in", [1024, 2736])
    w_uq = dr.inp("w_uq", [384, 768])
    w_ukk = dr.inp("w_ukk", [256, 768])
    w_ukv = dr.inp("w_ukv", [256, 512])
    cqg_d = dr.inp("cqg", [128, 3])
    ckvg_d = dr.inp("ckvg", [128, 2])
    c96_d = dr.inp("c96", [96, 3])
    oblk_d = dr.inp("oblk", [96, 96])
    rm_d = dr.inp("rm", [96, 96])
    e32_d = dr.inp("e32", [32, 96])
    cos_d = dr.inp("cosT", [96, NT])
    sin_d = dr.inp("sinT", [96, NT])
    xo = dr.out("xTo", [1024, NT])
    QT = dr.out("QT", [8, 96, NT], BF16)
    KT = dr.out("KT", [8, 96, NT], BF16)
    Vo = dr.out("V", [NT, 512], BF16)
    MQ = dr.out("MQ", [NT, 512], BF16)
    MK = dr.out("MK", [NT, 512], BF16)
    MV = dr.out("MV", [NT, 512], BF16)
    OT = dr.out("OT", [512, NT], F32)
    GT = dr.out("GT", [NT, 16], F32)

    load_X(P, C, xin)
    ffn(P, C, fw[0], fw[1], fw[2], mods[0])
    store_X(P, C, xo)
    for bi in range(len(C.blocks)):
        isc = C.blocks[bi][2]
        norm_mod(P, C, bi, mods[1][isc][0], mods[1][isc][1])
    P.barrier()
    wslab = C.wh
    win = T(wslab[:, :, :].rearrange("p s c -> p (s c)")[:, 0:8 * 2736].rearrange("p (k c) -> p k c", k=8),
            [b for t in C.wslots for b in t.bufs])
    winv = w_in.ap.rearrange("(k p) c -> p k c", p=128)
    for k in range(8):
        P.dma("pool", win[:, k, :], T(winv[:, k, :], WDRAM), sem="win")
    wuq = wload(P, "wuq", w_uq.ap.rearrange("(k p) c -> p k c", p=128), [128, 3, 768])
    wukk = wload(P, "wukk", w_ukk.ap.rearrange("(k p) c -> p k c", p=128), [128, 2, 768])
    wukv = wload(P, "wukv", w_ukv.ap.rearrange("(k p) c -> p k c", p=128), [128, 2, 512])
    cqg = wload(P, "cqg", cqg_d.ap, [128, 3], F32)
    ckvg = wload(P, "ckvg", ckvg_d.ap, [128, 2], F32)
    c96 = wload(P, "c96", c96_d.ap, [96, 3], F32)
    oblk = wload(P, "oblk", oblk_d.ap, [96, 96])
    rm = wload(P, "rm", rm_d.ap, [96, 96], F32)
    e32 = wload(P, "e32", e32_d.ap, [32, 96])
    xh = C.xh
    XS = xh[:, :, :].rearrange("p k t -> p (k t)")
    off = [0]

    def xs(name, parts, n):
        a = XS[0:parts, off[0]:off[0] + n]
        off[0] += n
        return T(a, Buf(name))

    cq = [xs(f"cq{k}", 128, 512) for k in range(3)]
    ckv = [xs(f"ckv{k}", 128, 512) for k in range(2)]
    cosb = xs("cosb", 96, 512)
    sinb = xs("sinb", 96, 512)
    qn = [xs(f"qn{i}", 96, 512) for i in range(2)]
    t1 = [xs(f"t1{i}", 96, 512) for i in range(2)]
    t2 = [xs(f"t2{i}", 96, 512) for i in range(2)]
    rr = [xs(f"rr{i}", 128, 512) for i in range(2)]
    otile = [xs(f"ot{i}", 128, 512) for i in range(2)]
    gts = [xs(f"gts{i}", 128, 16) for i in range(2)]
    cqn = [sbt(P, f"cqn{k}", [128, 512], BF16) for k in range(3)]
    ckvn = [sbt(P, f"ckvn{k}", [128, 512], BF16) for k in range(2)]
    krr = sbt(P, "krr", [32, 512], BF16)
    sqb = [sbt(P, f"sqb{i}", [128, 512], BF16) for i in range(2)]
    qo = [sbt(P, f"qo{i}", [96, 512], BF16) for i in range(2)]
    tok = [sbt(P, f"tok{i}", [128, 512], BF16) for i in range(3)]
    ps = C.ps
    cnt = [0]

    def nxt():
        cnt[0] += 1
        return cnt[0]

    def rstd_from(pss, n, parts, invn, out):
        P.ts("dve", out[:parts, :n], pss[:parts, :n], invn, EPS, ALU.mult, ALU.add)
        P.act(out[:parts, :n], out[:parts, :n], AF.Sqrt)
        P.recip(out[:parts, :n], out[:parts, :n])

    def lowrank_norm(raw, nk, gains, outs, n):
        pss = ps[6]
        for k in range(nk):
            q = sqb[k % 2]
            P.act(q[:, :n], raw[k][:, :n], AF.Square)
            P.mm(pss[:, :n], C.ones_bf, q[:, :n], start=(k == 0), stop=(k == nk - 1))
        r = rr[nxt() % 2]
        rstd_from(pss, n, 128, 1.0 / (nk * 128), r)
        for k in range(nk):
            t = C.tmpf[k % 3]
            P.tt("dve", t[:, :n], raw[k][:, :n], r[:, :n], ALU.mult)
            P.act(outs[k][:, :n], t[:, :n], AF.Identity, scale=gains[:, k:k + 1])

    def head_norm_rope(praw, n, gcol, dst):
        i = nxt()
        sq = sqb[i % 2]
        P.act(sq[:96, :n], praw[:96, :n], AF.Square)
        pss = ps[6]
        P.mm(pss[:96, :n], oblk, sq[:96, :n])
        r = rr[i % 2]
        rstd_from(pss, n, 96, c96[:, 2:3], r)
        q = qn[i % 2]
        P.tt("dve", q[:, :n], praw[:96, :n], r[:96, :n], ALU.mult)
        P.act(q[:, :n], q[:, :n], AF.Identity, scale=c96[:, gcol:gcol + 1])
        prot = ps[7]
        P.mm(prot[:96, :n], rm, q[:, :n])
        a = t1[i % 2]
        b = t2[i % 2]
        P.tt("pool", a[:, :n], q[:, :n], cosb[:, :n], ALU.mult)
        P.tt("dve", b[:, :n], prot[:96, :n], sinb[:, :n], ALU.mult)
        o = qo[i % 2]
        P.tt("dve", o[:, :n], a[:, :n], b[:, :n], ALU.add)
        P.dma("sp", dst, o[:, :n], sem="qk")

    COL = dict(cq=0, ckv=384, kr=640, mq=672, mk=1184, mv=1696, mo=2208, mg=2720)
    for bi, (s, n, isc) in enumerate(C.blocks):
        H = C.H[bi]
        P.dma("sp", cosb[:, :n], cos_d.with_ap(cos_d.ap[:, s:s + n]), sem="cs")
        P.dma("sp", sinb[:, :n], sin_d.with_ap(sin_d.ap[:, s:s + n]), sem="cs")
        for j in range(3):
            pp = ps[j % 2]
            for k in range(8):
                P.mm(pp[:, :n], win[:, k, COL["cq"] + j * 128:COL["cq"] + (j + 1) * 128], H[k], start=(k == 0), stop=(k == 7))
            P.copy("act", cq[j][:, :n], pp[:, :n])
        for j in range(2):
            pp = ps[j % 2]
            for k in range(8):
                P.mm(pp[:, :n], win[:, k, COL["ckv"] + j * 128:COL["ckv"] + (j + 1) * 128], H[k], start=(k == 0), stop=(k == 7))
            P.copy("act", ckv[j][:, :n], pp[:, :n])
        pp = ps[2]
        for k in range(8):
            P.mm(pp[:32, :n], win[:, k, COL["kr"]:COL["kr"] + 32], H[k], start=(k == 0), stop=(k == 7))
        P.copy("act", krr[:, :n], pp[:32, :n])
        lowrank_norm(cq, 3, cqg, cqn, n)
        lowrank_norm(ckv, 2, ckvg, ckvn, n)
        for j in range(4):
            pp = ps[j % 2]
            for k in range(8):
                P.mm(pp[:, :n], win[:, k, COL["mo"] + j * 128:COL["mo"] + (j + 1) * 128], H[k], start=(k == 0), stop=(k == 7))
            ot = otile[j % 2]
            P.copy("act", ot[:, :n], pp[:, :n])
            P.dma("sp", OT.with_ap(OT.ap[j * 128:(j + 1) * 128, s:s + n]), ot[:, :n], sem="oo")
        for tt_ in range(n // 128):
            ts_ = slice(tt_ * 128, (tt_ + 1) * 128)
            rows = slice(s + tt_ * 128, s + (tt_ + 1) * 128)
            for nm, dst in (("mq", MQ), ("mk", MK), ("mv", MV)):
                pp = ps[3 + nxt() % 2]
                for k in range(8):
                    P.mm(pp[:, :], H[k][:, ts_], win[:, k, COL[nm]:COL[nm] + 512], start=(k == 0), stop=(k == 7))
                tk = tok[nxt() % 3]
                P.copy("act", tk, pp)
                P.dma("sp", dst.with_ap(dst.ap[rows, :]), tk, sem="tok")
            pp = ps[5]
            for k in range(8):
                P.mm(pp[:, 0:16], H[k][:, ts_], win[:, k, COL["mg"]:COL["mg"] + 16], start=(k == 0), stop=(k == 7))
            g = gts[nxt() % 2]
            P.copy("act", g, pp[:, 0:16])
            P.dma("sp", GT.with_ap(GT.ap[rows, :]), g, sem="tok")
            pp = ps[3 + nxt() % 2]
            for k in range(2):
                P.mm(pp[:, :], ckvn[k][:, ts_], wukv[:, k, :], start=(k == 0), stop=(k == 1))
            tk = tok[nxt() % 3]
            P.copy("act", tk, pp)
            P.dma("sp", Vo.with_ap(Vo.ap[rows, :]), tk, sem="tok")
        for h in range(8):
            pp = ps[h % 2]
            for k in range(3):
                P.mm(pp[:96, :n], wuq[:, k, h * 96:(h + 1) * 96], cqn[k][:, :n], start=(k == 0), stop=(k == 2))
            head_norm_rope(pp, n, 0, QT.with_ap(QT.ap[h, :, s:s + n]))
            pp = ps[2 + h % 2]
            for k in range(2):
                P.mm(pp[:96, :n], wukk[:, k, h * 96:(h + 1) * 96], ckvn[k][:, :n], start=(k == 0), stop=False)
            P.mm(pp[:96, :n], e32, krr[:, :n], start=False, stop=True)
            head_norm_rope(pp, n, 1, KT.with_ap(KT.ap[h, :, s:s + n]))


def rope_tables(s):
    pos = np.arange(s * NTL, (s + 1) * NTL)
    rows = (pos // 64).astype(np.float32)
    cols = (pos % 64).astype(np.float32)
    inv = (np.float32(10000.0) ** (-np.arange(0, 16, 2, dtype=np.float32) / np.float32(16))).astype(np.float32)
    cosT = np.ones((96, NT), np.float32)
    sinT = np.zeros((96, NT), np.float32)
    for a, pp in enumerate((rows, cols)):
        ang = (pp[None, :] * inv[:, None]).astype(np.float32)
        for half in range(2):
            r0 = 64 + a * 16 + half * 8
            cosT[r0:r0 + 8, :NTL] = np.cos(ang)
            sinT[r0:r0 + 8, :NTL] = np.sin(ang)
    return cosT, sinT


def consts_A():
    oblk = np.zeros((96, 96), np.float32)
    oblk[:64, :64] = 1
    oblk[64:, 64:] = 1
    rm = np.zeros((96, 96), np.float32)
    for a in range(2):
        for f in range(8):
            p1 = 64 + a * 16 + f
            p2 = 64 + a * 16 + 8 + f
            rm[p2, p1] = -1.0
            rm[p1, p2] = 1.0
    e32 = np.zeros((32, 96), np.float32)
    for k in range(32):
        e32[k, 64 + k] = 1
    return oblk, rm, e32


def normg_table(norm_g_l):
    return np.ascontiguousarray(np.concatenate([fm(norm_g_l[i]) for i in range(3)], axis=1))


def maps_A(inp, m):
    oblk, rm, e32 = consts_A()
    ukv = inp["mla_w_ukv"][0].reshape(256, 8, 128)
    w_ukk = np.zeros((256, 8, 96), np.float32)
    w_ukk[:, :, :64] = ukv[:, :, :64]
    w_ukk = w_ukk.reshape(256, 768)
    w_ukv = np.ascontiguousarray(ukv[:, :, 64:].reshape(256, 512))
    c96 = np.zeros((96, 3), np.float32)
    c96[:, 0] = inp["mla_q_g"][0]
    c96[:, 1] = inp["mla_k_g"][0]
    c96[:64, 2] = 1.0 / 64
    c96[64:, 2] = 1.0 / 32
    maps = []
    for r in range(8):
        b, s = r // 4, r % 4
        cosT, sinT = rope_tables(s)
        xT = np.ascontiguousarray(np.concatenate([inp["x"][b, s * NTL:(s + 1) * NTL].T, inp["ctx"][b].T], axis=1))
        maps.append(dict(xT=xT, modt=mod_table(m, 0, b), normg=normg_table(inp["norm_g"][0]),
                         wg=inp["ffn_w_gate"][0, 0], wu=inp["ffn_w_up"][0, 0], wd=inp["ffn_w_down"][0, 0],
                         w_in=inp["even_w_in"][0], w_uq=inp["mla_w_uq"][0], w_ukk=w_ukk, w_ukv=w_ukv,
                         cqg=fm(inp["mla_cq_g"][0]), ckvg=fm(inp["mla_ckv_g"][0]), c96=c96,
                         oblk=oblk, rm=rm, e32=e32, cosT=cosT, sinT=sinT))
    return maps


NK = NTC + 4 * NTL
NCH = NK // 128


def psplit(P, name, dt, cols, n):
    ph = P.stack.enter_context(P.nc.psum_tensor(uname(name), [128, n * cols], dt))
    bank_bytes = 2048
    esz = 4 if dt == F32 else 2
    bufs = {}
    out = []
    for i in range(n):
        bk = (i * cols * esz) // bank_bytes
        if bk not in bufs:
            bufs[bk] = Buf(f"{name}_b{bk}", excl=True)
        out.append(T(ph[:, i * cols:(i + 1) * cols], bufs[bk]))
    return out


def build_B(P, dr):
    QT = dr.inp("QT", [8, 96, NT], BF16)
    KT = dr.inp("KTa", [8, 96, NK], BF16)
    Va = dr.inp("Va", [NK, 512], BF16)
    mq_d = dr.inp("mq", [NK, 128], BF16)
    mk_d = dr.inp("mk", [NK, 128], BF16)
    mv_d = dr.inp("mv", [NK, 128], BF16)
    g4_d = dr.inp("g4", [NK, 4])
    gb_d = dr.inp("gb", [128, 4])
    triF_d = dr.inp("triF", [128, 128])
    triB_d = dr.inp("triB", [128, 128])
    id_d = dr.inp("ident", [128, 128])
    attT = dr.out("attT", [512, NT])
    HT = dr.out("HT", [128, NK])

    blocks = [(s, 512, False) for s in range(0, NTL, 512)] + [(NTL, NTC, True)]
    psS = psplit(P, "psS", F32, 512, 2)
    psO = psplit(P, "psO", F32, 512, 2)
    psT = psplit(P, "psT", BF16, 512, 2)
    psG = psplit(P, "psG", F32, 128, 4)
    psD = psplit(P, "psD", F32, 256, 2)
    psN = psplit(P, "psN", F32, 256, 2)

    ones_bf = sbt(P, "ones_bf", [128, 128], BF16)
    P.memset("pool", ones_bf, 1.0)
    ones_f = sbt(P, "ones_f", [128, 128], F32)
    P.memset("pool", ones_f, 1.0)
    triF = wload(P, "triF", triF_d.ap, [128, 128], F32)
    triB = wload(P, "triB", triB_d.ap, [128, 128], F32)
    ident = wload(P, "ident", id_d.ap, [128, 128], BF16)
    tri = [triF, triB]

    kth = [sbt(P, f"kth{i}", [96, NK], BF16) for i in range(2)]
    vah = [sbt(P, f"vah{i}", [128, NCH, 128], BF16) for i in range(2)]
    qh = [sbt(P, f"qh{i}", [96, NT], BF16) for i in range(2)]
    for i in range(2):
        P.memset("pool", vah[i][:, :, 64:128], 1.0)
    Et = [sbt(P, f"Et{i}", [128, 512], BF16) for i in range(3)]
    rc = [sbt(P, f"rc{i}", [128, 512], F32) for i in range(2)]
    ob = [sbt(P, f"ob{i}", [64, 512], F32) for i in range(2)]
    vv = Va.ap.rearrange("(t p) c -> p t c", p=128)
    SC = 1.0 / np.sqrt(96.0)

    def load_head(h):
        i = h % 2
        P.dma("sp", kth[i], KT.with_ap(KT.ap[h]), sem=f"kh{i}")
        P.dma("sp", qh[i], QT.with_ap(QT.ap[h]), sem=f"kh{i}")
        for c0 in range(0, NCH, 6):
            P.dma("sp", vah[i][:, c0:c0 + 6, 0:64], Va.with_ap(vv[:, c0:c0 + 6, h * 64:(h + 1) * 64]), sem=f"kh{i}")

    def att_gen():
        it = 0
        load_head(0)
        for h in range(8):
            if h + 1 < 8:
                load_head(h + 1)
            i = h % 2
            for bi, (s, n, isc) in enumerate(blocks):
                po = psO[bi % 2]
                tiles = range(NTC // 128) if isc else range(NCH)
                last = tiles[-1]
                for kt in tiles:
                    pS = psS[it % 2]
                    P.mm(pS[:, :n], kth[i][:, kt * 128:(kt + 1) * 128], qh[i][:, s:s + n])
                    E = Et[it % 3]
                    P.act(E[:, :n], pS[:, :n], AF.Exp, scale=SC)
                    P.mm(po[:, :n], vah[i][:, kt, :], E[:, :n], start=(kt == 0), stop=(kt == last))
                    it += 1
                    yield
                r = rc[bi % 2]
                P.recip(r[64:128, :n], po[64:128, :n])
                o = ob[bi % 2]
                P.tt("dve", o[:, :n], po[0:64, :n], r[64:128, :n], ALU.mult)
                P.dma("sp", attT.with_ap(attT.ap[h * 64:(h + 1) * 64, s:s + n]), o[:, :n], sem="ao")

    gt = sbt(P, "gt", [128, NCH, 4], F32)
    gb = wload(P, "gb", gb_d.ap, [128, 4], F32)
    g4v = g4_d.ap.rearrange("(c p) g -> p c g", p=128)
    for c0 in range(0, NCH, 6):
        P.dma("sp", gt[:, c0:c0 + 6, :], g4_d.with_ap(g4v[:, c0:c0 + 6, :]), sem="c")
    for g in range(4):
        P.ts("dve", gt[:, :, g], gt[:, :, g], gb[:, g:g + 1], None, ALU.add)
    mq = sbt(P, "mq", [128, NCH, 128], BF16)
    mk = sbt(P, "mk", [128, NCH, 128], BF16)
    mv = sbt(P, "mv", [128, NCH, 128], BF16)
    for t_, d_ in ((mq, mq_d), (mk, mk_d), (mv, mv_d)):
        dv_ = d_.ap.rearrange("(c p) d -> p c d", p=128)
        for c0 in range(0, NCH, 6):
            P.dma("sp", t_[:, c0:c0 + 6, :], d_.with_ap(dv_[:, c0:c0 + 6, :]), sem="c")
    HS = sbt(P, "HS", [128, NK], F32)
    HSb = [Buf(f"HS{c}") for c in range(NCH)]
    EQ, EK, EL = [], [], []
    for d in range(2):
        lf = sbt(P, f"lf{d}", [128, NCH], F32)
        P.act(lf, gt[:, :, 2 * d + 1], AF.Exp, scale=-1.0)
        P.ts("dve", lf, lf, 1.0, None, ALU.add)
        P.act(lf, lf, AF.Ln)
        P.ts("dve", lf, lf, -1.0, None, ALU.mult)
        pB = psG[2]
        pTt = psG[3]
        P.mm(pB[:, :NCH], tri[d], lf)
        P.mm(pTt[:, :NCH], ones_f, lf)
        eq = sbt(P, f"eq{d}", [128, NCH], F32)
        ek = sbt(P, f"ek{d}", [128, NCH], F32)
        el = sbt(P, f"el{d}", [128, NCH], F32)
        P.act(eq, pB[:, :NCH], AF.Exp)
        P.tt("dve", ek, gt[:, :, 2 * d], pB[:, :NCH], ALU.subtract)
        P.act(ek, ek, AF.Exp)
        P.ts("dve", ek, ek, float(128.0 ** -0.5), None, ALU.mult)
        P.act(el, pTt[:, :NCH], AF.Exp)
        EQ.append(eq)
        EK.append(ek)
        EL.append(el)
    CN = [sbt(P, f"CN{d}", [128, 256], F32) for d in range(2)]
    CNb = [sbt(P, f"CNb{d}", [128, 256], BF16) for d in range(2)]
    for d in range(2):
        P.memset("pool", CN[d], 0.0)
        P.memset("pool", CNb[d], 0.0)
    qs = [[sbt(P, f"qs{d}{i}", [128, 128], BF16) for i in range(2)] for d in range(2)]
    ks = [[sbt(P, f"ks{d}{i}", [128, 128], BF16) for i in range(2)] for d in range(2)]
    qkT = [[sbt(P, f"qkT{d}{i}", [128, 256], BF16) for i in range(2)] for d in range(2)]
    Sm = [[sbt(P, f"Sm{d}{i}", [128, 128], BF16) for i in range(2)] for d in range(2)]
    dn = [sbt(P, f"dn{d}", [128, 128], F32) for d in range(2)]
    hc = [sbt(P, f"hc{d}", [128, 128], F32) for d in range(2)]
    tmpc = [sbt(P, f"tmpc{d}", [128, 256], F32) for d in range(2)]
    order = [list(range(NCH)), [1, 0] + list(range(NCH - 1, 1, -1))]
    written = set()

    def pre(d, c, i):
        P.ts("dve", qs[d][i], mq[:, c, :], EQ[d][:, c:c + 1], None, ALU.mult)
        P.act(ks[d][i], mk[:, c, :], AF.Identity, scale=EK[d][:, c:c + 1])
        pT = psT[d]
        P.transpose(pT[:, 0:128], qs[d][i], ident)
        P.transpose(pT[:, 128:256], ks[d][i], ident)
        P.copy("act", qkT[d][i], pT[:, 0:256])
        pS = psG[d]
        P.mm(pS, qkT[d][i][:, 128:256], qkT[d][i][:, 0:128])
        P.tt("dve", Sm[d][i], pS, tri[d], ALU.mult)

    def post(d, c, i):
        pD = psD[d]
        P.mm(pD[:, 0:128], ks[d][i], mv[:, c, :])
        P.mm(pD[:, 128:256], ks[d][i], ones_bf)
        pN = psN[d]
        P.mm(pN[:, 0:128], mv[:, c, :], Sm[d][i], start=True, stop=False)
        P.mm(pN[:, 0:128], CNb[d][:, 0:128], qkT[d][i][:, 0:128], start=False, stop=True)
        P.mm(pN[:, 128:256], ones_bf, Sm[d][i], start=True, stop=False)
        P.mm(pN[:, 128:256], CNb[d][:, 128:256], qkT[d][i][:, 0:128], start=False, stop=True)
        P.act(dn[d], pN[:, 128:256], AF.Abs)
        P.ts("dve", dn[d], dn[d], 1.0, None, ALU.max)
        P.recip(dn[d], dn[d])
        hsl = T(HS.ap[:, c * 128:(c + 1) * 128], HSb[c])
        if c in written:
            P.tt("dve", hc[d], pN[:, 0:128], dn[d], ALU.mult)
            P.tt("pool", hsl, hsl, hc[d], ALU.add)
            P.dma("sp", HT.with_ap(HT.ap[:, c * 128:(c + 1) * 128]), hsl, sem="ho")
        else:
            P.tt("dve", hsl, pN[:, 0:128], dn[d], ALU.mult)
            written.add(c)
        P.tt("dve", tmpc[d], CN[d], psD[d], ALU.add)
        P.ts("dve", CN[d], tmpc[d], EL[d][:, c:c + 1], None, ALU.mult)
        P.copy("act", CNb[d], CN[d])

    def ml_gen():
        for d in range(2):
            pre(d, order[d][0], 0)
        for st in range(NCH):
            for d in range(2):
                if st + 1 < NCH:
                    pre(d, order[d][st + 1], (st + 1) % 2)
                post(d, order[d][st], st % 2)
            yield

    a = att_gen() if B_ATT else iter(())
    m = ml_gen() if B_ML else iter(())
    for i, _ in enumerate(a):
        if i % 30 == 5:
            next(m, None)
    for _ in m:
        pass


B_ATT = True
B_ML = True


def consts_B():
    s = np.arange(128)[:, None]
    t = np.arange(128)[None, :]
    triF = (s <= t).astype(np.float32)
    triB = (s >= t).astype(np.float32)
    return triF, triB, np.eye(128, dtype=np.float32)


def maps_B(inp, resA):
    triF, triB, ident = consts_B()
    maps = []
    cat = {}
    for b in range(2):
        rs = [resA[4 * b + s] for s in range(4)]
        cat[b] = dict(
            KT=np.concatenate([rs[0]["KT"][:, :, NTL:]] + [r["KT"][:, :, :NTL] for r in rs], axis=2),
            V=np.concatenate([rs[0]["V"][NTL:]] + [r["V"][:NTL] for r in rs], axis=0),
            MQ=np.concatenate([rs[0]["MQ"][NTL:]] + [r["MQ"][:NTL] for r in rs], axis=0),
            MK=np.concatenate([rs[0]["MK"][NTL:]] + [r["MK"][:NTL] for r in rs], axis=0),
            MV=np.concatenate([rs[0]["MV"][NTL:]] + [r["MV"][:NTL] for r in rs], axis=0),
            GT=np.concatenate([rs[0]["GT"][NTL:]] + [r["GT"][:NTL] for r in rs], axis=0))
    gbias = inp["mlstm_gate_b"][0].reshape(4, 4)
    for r in range(8):
        b, h = r // 4, r % 4
        cb = cat[b]
        hs = slice(h * 128, (h + 1) * 128)
        g4 = np.ascontiguousarray(cb["GT"].reshape(NK, 4, 4)[:, :, h])
        gb = np.ascontiguousarray(np.broadcast_to(gbias[:, h][None, :], (128, 4))).astype(np.float32)
        maps.append(dict(QT=resA[r]["QT"], KTa=np.ascontiguousarray(cb["KT"]), Va=np.ascontiguousarray(cb["V"]),
                         mq=np.ascontiguousarray(cb["MQ"][:, hs]), mk=np.ascontiguousarray(cb["MK"][:, hs]),
                         mv=np.ascontiguousarray(cb["MV"][:, hs]), g4=g4, gb=gb,
                         triF=triF, triB=triB, ident=ident))
    return maps


def mix_out(P, C, wout, mixf, Gsel):
    for bi, (s, n, isc) in enumerate(C.blocks):
        mix = mixf(bi)
        G = Gsel[isc][2]
        for m_ in range(8):
            py = C.ps[4 + m_ % 2]
            for k in range(8):
                P.mm(py[:, :n], wout[:, k, m_ * 128:(m_ + 1) * 128], mix[k][:, :n], start=(k == 0), stop=(k == 7))
            P.stt("dve", C.X[bi][m_], py[:, :n], G[:, m_:m_ + 1], C.X[bi][m_], ALU.mult, ALU.add)


def build_C(P, dr):
    C = setup_common(P, NTL, NTC)
    xin = dr.inp("xT", [1024, NT])
    attT = dr.inp("attT", [512, NT])
    hmT = dr.inp("hmT", [512, NT])
    oT = dr.inp("oT", [512, NT])
    wout_d = dr.inp("w_out", [1024, 1024])
    outg_d = dr.inp("outg", [128, 4])
    mods0 = load_mods(P, dr, C, "0")
    mods1 = load_mods(P, dr, C, "1")
    fw02 = ffn_w(dr, "02")
    fw11 = ffn_w(dr, "11")
    win_d = dr.inp("w_in1", [1024, 2048])
    xo = dr.out("xTo", [1024, NTL])
    ggT = dr.out("ggT", [1024, NTL])
    xrT = dr.out("xrT", [1024, NT])

    load_X(P, C, xin)
    wslab = C.wh[:, :, :].rearrange("p s c -> p (s c)")
    allw = [b for t in C.wslots for b in t.bufs]
    wout = T(wslab[:, 0:8192].rearrange("p (k c) -> p k c", k=8), allw)
    wv = wout_d.ap.rearrange("(k p) c -> p k c", p=128)
    for k in range(8):
        P.dma("pool", wout[:, k, :], T(wv[:, k, :], WDRAM), sem="win")
    outg = wload(P, "outg", outg_d.ap, [128, 4], F32)
    mixb = [sbt(P, f"mix{k}", [128, 512], BF16) for k in range(8)]
    hmt = [sbt(P, f"hmt{i}", [128, 512], F32) for i in range(2)]
    ott = [sbt(P, f"ott{i}", [128, 512], F32) for i in range(2)]
    rr = [sbt(P, f"rrc{i}", [128, 512], F32) for i in range(2)]

    def mixf(bi):
        s, n, isc = C.blocks[bi]
        for j in range(4):
            P.dma("pool", mixb[j][:, :n], attT.with_ap(attT.ap[j * 128:(j + 1) * 128, s:s + n]), sem="mx")
            hm = hmt[j % 2]
            ot = ott[j % 2]
            P.dma("sp", hm[:, :n], hmT.with_ap(hmT.ap[j * 128:(j + 1) * 128, s:s + n]), sem="mx2")
            P.dma("sp", ot[:, :n], oT.with_ap(oT.ap[j * 128:(j + 1) * 128, s:s + n]), sem="mx2")
            sq = C.sq[j % 2]
            P.act(sq[:, :n], hm[:, :n], AF.Square)
            pss = C.ps[6]
            P.mm(pss[:, :n], C.ones_bf, sq[:, :n])
            r = rr[j % 2]
            P.ts("dve", r[:, :n], pss[:, :n], 1.0 / 128, EPS, ALU.mult, ALU.add)
            P.act(r[:, :n], r[:, :n], AF.Sqrt)
            P.recip(r[:, :n], r[:, :n])
            P.tt("dve", hm[:, :n], hm[:, :n], r[:, :n], ALU.mult)
            P.act(ot[:, :n], ot[:, :n], AF.Sigmoid)
            P.stt("dve", mixb[4 + j][:, :n], ot[:, :n], outg[:, j:j + 1], hm[:, :n], ALU.mult, ALU.mult)
        return mixb

    mix_out(P, C, wout, mixf, mods0[1])
    P.barrier()
    ffn(P, C, fw02[0], fw02[1], fw02[2], mods0[2])
    ffn(P, C, fw11[0], fw11[1], fw11[2], mods1[0])
    xv = xo.ap.rearrange("(k p) t -> p k t", p=128)
    for bi, (s, n, isc) in enumerate(C.blocks):
        if not isc:
            for k in range(8):
                P.dma("sp", xo.with_ap(xv[:, k, s:s + n]), C.X[bi][k], sem="xout")
    for bi in range(len(C.blocks)):
        isc = C.blocks[bi][2]
        norm_mod(P, C, bi, mods1[1][isc][0], mods1[1][isc][1])
    P.barrier()
    win = T(wslab[:, 0:8 * 2048].rearrange("p (k c) -> p k c", k=8), allw)
    wv = win_d.ap.rearrange("(k p) c -> p k c", p=128)
    for k in range(8):
        P.dma("pool", win[:, k, :], T(wv[:, k, :], WDRAM), sem="win")
    ga = hmt
    gb_ = ott
    it = 0
    for bi, (s, n, isc) in enumerate(C.blocks):
        H = C.H[bi]
        if not isc:
            for j in range(8):
                pp = C.ps[j % 2]
                for k in range(8):
                    P.mm(pp[:, :n], win[:, k, j * 128:(j + 1) * 128], H[k], start=(k == 0), stop=(k == 7))
                a = ga[it % 2]
                b = gb_[it % 2]
                it += 1
                P.act(a[:, :n], pp[:, :n], AF.Square)
                P.ts("dve", a[:, :n], a[:, :n], 0.044715, 1.0, ALU.mult, ALU.add)
                P.tt("dve", a[:, :n], a[:, :n], pp[:, :n], ALU.mult)
                P.act(b[:, :n], a[:, :n], AF.Sigmoid, scale=1.5957691216057308)
                P.tt("dve", b[:, :n], b[:, :n], pp[:, :n], ALU.mult)
                P.dma("sp", ggT.with_ap(ggT.ap[j * 128:(j + 1) * 128, s:s + n]), b[:, :n], sem="go")
        for j in range(8):
            pp = C.ps[2 + j % 2]
            for k in range(8):
                P.mm(pp[:, :n], win[:, k, 1024 + j * 128:1024 + (j + 1) * 128], H[k], start=(k == 0), stop=(k == 7))
            a = ga[it % 2]
            it += 1
            P.copy("act", a[:, :n], pp[:, :n])
            P.dma("sp", xrT.with_ap(xrT.ap[j * 128:(j + 1) * 128, s:s + n]), a[:, :n], sem="go")


def maps_C(inp, m, resA, resB):
    maps = []
    for r in range(8):
        b, s = r // 4, r % 4
        hm = np.zeros((512, NT), np.float32)
        for h in range(4):
            HTb = resB[4 * b + h]["HT"]
            hm[h * 128:(h + 1) * 128, :NTL] = HTb[:, NTC + s * NTL:NTC + (s + 1) * NTL]
            hm[h * 128:(h + 1) * 128, NTL:] = HTb[:, :NTC]
        maps.append(dict(xT=resA[r]["xTo"], attT=resB[r]["attT"], hmT=hm, oT=resA[r]["OT"],
                         w_out=inp["even_w_out"][0], outg=fm(inp["mlstm_out_g"][0]),
                         modt0=mod_table(m, 0, b), normg0=normg_table(inp["norm_g"][0]),
                         modt1=mod_table(m, 1, b), normg1=normg_table(inp["norm_g"][1]),
                         wg02=inp["ffn_w_gate"][0, 1], wu02=inp["ffn_w_up"][0, 1], wd02=inp["ffn_w_down"][0, 1],
                         wg11=inp["ffn_w_gate"][1, 0], wu11=inp["ffn_w_up"][1, 0], wd11=inp["ffn_w_down"][1, 0],
                         w_in1=inp["odd_w_in"][0]))
    return maps


TS = 2048


def build_D(P, dr):
    u_d = dr.inp("u", [2, 128, NK])
    ur_d = dr.inp("urev", [2, 128, NK])
    cw_d = dr.inp("cw", [128, 4])
    cb_d = dr.inp("cb", [128, 1])
    wa_d = dr.inp("wa", [2, 128, 128])
    wx_d = dr.inp("wx", [2, 128, 128])
    bab_d = dr.inp("bab", [128, 6])
    hf = dr.out("hf", [2, 128, 4 * NTL])
    hb = dr.out("hb", [2, 128, 4 * NTL])
    cw = wload(P, "cw", cw_d.ap, [128, 4], F32)
    cb = wload(P, "cb", cb_d.ap, [128, 1], F32)
    wa = [wload(P, f"wa{d}", wa_d.ap[d], [128, 128], F32) for d in range(2)]
    wx = [wload(P, f"wx{d}", wx_d.ap[d], [128, 128], F32) for d in range(2)]
    bab = wload(P, "bab", bab_d.ap, [128, 6], F32)
    cc = sbt(P, "cc", [128, 2], F32)
    P.act(cc, bab[:, 4:6], AF.Exp, scale=-1.0)
    P.ts("dve", cc, cc, 1.0, None, ALU.add)
    P.act(cc, cc, AF.Ln)
    P.ts("dve", cc, cc, -8.0, None, ALU.mult)
    psA = psplit(P, "psA", F32, 512, 2)
    psX = psplit(P, "psX", F32, 512, 2)
    ub = [sbt(P, f"ub{i}", [128, TS + 4], F32) for i in range(2)]
    xc = [sbt(P, f"xc{i}", [128, TS], F32) for i in range(2)]
    rt = sbt(P, "rt", [128, TS], F32)
    ig = sbt(P, "ig", [128, TS], F32)
    at = sbt(P, "at", [128, TS], F32)
    om = sbt(P, "om", [128, TS], F32)
    ip = sbt(P, "ip", [128, TS], F32)
    ht = [sbt(P, f"ht{i}", [128, TS], F32) for i in range(2)]
    it = 0
    for b in range(2):
        for d in range(2):
            src = u_d if d == 0 else ur_d
            dst = hf if d == 0 else hb
            offs = [j - 2 for j in range(4)] if d == 0 else [2 - j for j in range(4)]
            prev = None
            for (seg0, segn) in ((0, NTC), (NTC, 4 * NTL)):
                for c0 in range(0, segn, TS):
                    n = min(TS, segn - c0)
                    u = ub[it % 2]
                    x = xc[it % 2]
                    h = ht[it % 2]
                    it += 1
                    lo = max(c0 - 2, 0)
                    hi = min(c0 + n + 2, segn)
                    if lo > c0 - 2:
                        P.memset("pool", u[:, 0:2], 0.0)
                    if hi < c0 + n + 2:
                        P.memset("pool", u[:, n + 2:n + 4], 0.0)
                    P.dma("sp", u[:, 2 + (lo - c0):2 + (hi - c0)], src.with_ap(src.ap[b, :, seg0 + lo:seg0 + hi]), sem=f"u{it % 2}")
                    P.act(x[:, :n], u[:, 2 + offs[0]:2 + offs[0] + n], AF.Identity, bias=cb[:, 0:1], scale=cw[:, 0:1])
                    for j in range(1, 4):
                        P.stt("dve", x[:, :n], u[:, 2 + offs[j]:2 + offs[j] + n], cw[:, j:j + 1], x[:, :n], ALU.mult, ALU.add)
                    for q0 in range(0, n, 512):
                        qn_ = min(512, n - q0)
                        pa = psA[(q0 // 512) % 2]
                        px = psX[(q0 // 512) % 2]
                        P.mm(pa[:, :qn_], wa[d], x[:, q0:q0 + qn_])
                        P.act(rt[:, q0:q0 + qn_], pa[:, :qn_], AF.Sigmoid, bias=bab[:, d:d + 1])
                        P.mm(px[:, :qn_], wx[d], x[:, q0:q0 + qn_])
                        P.act(ig[:, q0:q0 + qn_], px[:, :qn_], AF.Sigmoid, bias=bab[:, 2 + d:3 + d])
                    P.act(at[:, :n], rt[:, :n], AF.Exp, scale=cc[:, d:d + 1])
                    P.tt("pool", om[:, :n], at[:, :n], at[:, :n], ALU.mult)
                    P.ts("dve", om[:, :n], om[:, :n], -1.0, 1.0, ALU.mult, ALU.add)
                    P.act(om[:, :n], om[:, :n], AF.Sqrt)
                    P.tt("pool", ip[:, :n], ig[:, :n], x[:, :n], ALU.mult)
                    P.tt("dve", ip[:, :n], ip[:, :n], om[:, :n], ALU.mult)
                    init = 0.0 if prev is None else prev
                    P.scan(h[:, :n], at[:, :n], ip[:, :n], init, ALU.mult, ALU.add)
                    prev = h[:, n - 1:n]
                    if seg0 > 0:
                        P.dma("sp", dst.with_ap(dst.ap[b, :, c0:c0 + n]), h[:, :n], sem="ho")


def maps_D(inp, resC):
    maps = []
    xr = []
    for b in range(2):
        rs = [resC[4 * b + s]["xrT"] for s in range(4)]
        xr.append(np.concatenate([rs[0][:, NTL:]] + [r[:, :NTL] for r in rs], axis=1))
    for n in range(8):
        cs = slice(n * 128, (n + 1) * 128)
        u = np.stack([xr[b][cs] for b in range(2)], 0)
        urev = np.concatenate([u[:, :, :NTC][:, :, ::-1], u[:, :, NTC:][:, :, ::-1]], axis=2)
        bab = np.stack([inp["lru_b_a"][0, 0, cs], inp["lru_b_a"][0, 1, cs], inp["lru_b_x"][0, 0, cs],
                        inp["lru_b_x"][0, 1, cs], inp["lru_lam"][0, 0, cs], inp["lru_lam"][0, 1, cs]], axis=1)
        maps.append(dict(u=np.ascontiguousarray(u), urev=np.ascontiguousarray(urev),
                         cw=np.ascontiguousarray(inp["odd_conv_w"][0][:, cs].T), cb=np.ascontiguousarray(inp["odd_conv_b"][0][cs][:, None]),
                         wa=np.ascontiguousarray(inp["lru_w_a"][0, :, n]), wx=np.ascontiguousarray(inp["lru_w_x"][0, :, n]),
                         bab=np.ascontiguousarray(bab.astype(np.float32))))
    return maps


def build_E(P, dr):
    C = setup_common(P, NTL, 0)
    xin = dr.inp("xT", [1024, NTL])
    ggT = dr.inp("ggT", [1024, NTL])
    hfT = dr.inp("hfT", [1024, NTL])
    hbT = dr.inp("hbT", [1024, NTL])
    wout_d = dr.inp("w_out", [1024, 1024])
    mods1 = load_mods(P, dr, C, "1")
    fw12 = ffn_w(dr, "12")
    xo = dr.out("xTo", [1024, NTL])
    load_X(P, C, xin)
    wslab = C.wh[:, :, :].rearrange("p s c -> p (s c)")
    allw = [b for t in C.wslots for b in t.bufs]
    wout = T(wslab[:, 0:8192].rearrange("p (k c) -> p k c", k=8), allw)
    wv = wout_d.ap.rearrange("(k p) c -> p k c", p=128)
    for k in range(8):
        P.dma("pool", wout[:, k, :], T(wv[:, k, :], WDRAM), sem="win")
    mixb = [sbt(P, f"mix{k}", [128, 512], BF16) for k in range(8)]
    ta = [sbt(P, f"ta{i}", [128, 512], F32) for i in range(2)]
    tb = [sbt(P, f"tb{i}", [128, 512], F32) for i in range(2)]
    tg = [sbt(P, f"tg{i}", [128, 512], F32) for i in range(2)]

    def mixf(bi):
        s, n, isc = C.blocks[bi]
        for k in range(8):
            a, b_, g = ta[k % 2], tb[k % 2], tg[k % 2]
            rows = slice(k * 128, (k + 1) * 128)
            P.dma("sp", a[:, :n], hfT.with_ap(hfT.ap[rows, s:s + n]), sem="mx")
            P.dma("sp", b_[:, :n], hbT.with_ap(hbT.ap[rows, s:s + n]), sem="mx")
            P.dma("sp", g[:, :n], ggT.with_ap(ggT.ap[rows, s:s + n]), sem="mx")
            P.tt("pool", a[:, :n], a[:, :n], b_[:, :n], ALU.add)
            P.tt("dve", mixb[k][:, :n], a[:, :n], g[:, :n], ALU.mult)
        return mixb

    mix_out(P, C, wout, mixf, mods1[1])
    P.barrier()
    ffn(P, C, fw12[0], fw12[1], fw12[2], mods1[2])
    store_X(P, C, xo)


def maps_E(inp, m, resC, resD):
    maps = []
    for r in range(8):
        b, s = r // 4, r % 4
        hf = np.concatenate([resD[n]["hf"][b][:, s * NTL:(s + 1) * NTL] for n in range(8)], axis=0)
        hbfull = [resD[n]["hb"][b][:, ::-1] for n in range(8)]
        hb = np.concatenate([h[:, s * NTL:(s + 1) * NTL] for h in hbfull], axis=0)
        maps.append(dict(xT=resC[r]["xTo"], ggT=resC[r]["ggT"], hfT=np.ascontiguousarray(hf), hbT=np.ascontiguousarray(hb),
                         w_out=inp["odd_w_out"][0], modt1=mod_table(m, 1, b), normg1=normg_table(inp["norm_g"][1]),
                         wg12=inp["ffn_w_gate"][1, 1], wu12=inp["ffn_w_up"][1, 1], wd12=inp["ffn_w_down"][1, 1]))
    return maps


NSEG = 4


def seg_blocks(seg):
    bl = [(s, 512, False, NTC + seg * NTL + s) for s in range(0, NTL, 512)]
    if seg == 0:
        bl.append((NTL, NTC, True, 0))
    return bl


def setup_seg(P, seg):
    C = setup_common(P, NTL, NTC if seg == 0 else 0)
    C.gcol = [b[3] for b in seg_blocks(seg)]
    return C


def load_Xg(P, C, xin):
    xv = xin.ap.rearrange("(k p) t -> p k t", p=128)
    for bi, (s, n, c) in enumerate(C.blocks):
        g = C.gcol[bi]
        for k in range(8):
            P.dma("sp", C.X[bi][k], xin.with_ap(xv[:, k, g:g + n]), sem="xin")


def store_Xg(P, C, xout, goff=0, latent_only=False):
    xv = xout.ap.rearrange("(k p) t -> p k t", p=128)
    for bi, (s, n, c) in enumerate(C.blocks):
        if latent_only and c:
            continue
        g = C.gcol[bi] - goff
        for k in range(8):
            P.dma("sp", xout.with_ap(xv[:, k, g:g + n]), C.X[bi][k], sem="xout")


def phase_M(P, dr):
    modw = dr.inp("mod_w", [2, 1024, 9216])
    modb = dr.inp("modb", [128, 144])
    cv = dr.inp("cv", [128, 16])
    MODS = dr.tmp("MODS", [128, 288])
    cvt = sbt(P, "cvt", [128, 8, 2], F32)
    scv = sbt(P, "scv", [128, 8, 2], F32)
    mb = sbt(P, "mb", [128, 144], F32)
    res = sbt(P, "res", [128, 144, 2], F32)
    P.dma("sp", cvt, cv.with_ap(cv.ap.rearrange("p (k v) -> p k v", v=2)), sem="c")
    P.dma("sp", mb, modb, sem="c")
    P.act(scv, cvt, AF.Silu)
    ph = P.stack.enter_context(P.nc.psum_tensor(uname("psm"), [128, 144, 2], F32))
    ps = T(ph[:, :, :], Buf("psm", excl=True))
    ws = [sbt(P, f"mw{i}", [128, 8, 384], F32) for i in range(2)]
    n = 0
    for l in range(2):
        wv = modw.ap[l].rearrange("(k p) c -> p k c", p=128)
        for pc in range(24):
            w = ws[n % 2]
            P.dma("sp", w, modw.with_ap(wv[:, :, pc * 384:(pc + 1) * 384]), sem=f"mw{n % 2}")
            n += 1
            for jj in range(3):
                j = l * 72 + pc * 3 + jj
                for k in range(8):
                    P.mm(ps[:, j, :], w[:, k, jj * 128:(jj + 1) * 128], scv[:, k, :], start=(k == 0), stop=(k == 7))
    for v in range(2):
        P.tt("dve", res[:, :, v], ps[:, :, v], mb, ALU.add)
    P.dma("sp", MODS, res.with_ap(res.ap.rearrange("p j v -> p (j v)")), sem="o")


def load_modsF(P, dr, C, l):
    MODS = dr.tmp("MODS", [128, 288])
    ng_d = dr.inp(f"normg{l}", [128, 24])
    mt = sbt(P, "mt", [128, 9, 8, 2], F32)
    ng = sbt(P, "ng", [128, 3, 8], F32)
    mv = MODS.ap.rearrange("p (l j v) -> p l j v", l=2, v=2)[:, l].rearrange("p (i k) v -> p i k v", i=9)
    P.dma("sp", mt, MODS.with_ap(mv), sem="c")
    P.dma("sp", ng, ng_d.with_ap(ng_d.ap.rearrange("p (i k) -> p i k", i=3)), sem="c")
    ab = sbt(P, "modAB", [128, 3, 2, 3, 8], F32)
    mods = []
    for i in range(3):
        row = {}
        for xc in range(2):
            A = ab[:, i, xc, 0, :]
            B = ab[:, i, xc, 1, :]
            G = ab[:, i, xc, 2, :]
            P.ts("dve", A, mt[:, 3 * i + 1, :, xc], 1.0, None, ALU.add)
            P.tt("dve", A, A, ng[:, i, :], ALU.mult)
            P.copy("dve", B, mt[:, 3 * i, :, xc])
            P.ts("dve", G, mt[:, 3 * i + 2, :, xc], 0.5 if i != 1 else 1.0, None, ALU.mult)
            row[bool(xc)] = (A, B, G)
        mods.append(row)
    return mods


def ffn_wF(dr, l, i):
    wg = dr.inp("ffn_w_gate", [2, 2, 1024, 2816])
    wu = dr.inp("ffn_w_up", [2, 2, 1024, 2816])
    wd = dr.inp("ffn_w_down", [2, 2, 2816, 1024])
    return (wg.ap[l, i].rearrange("(k p) c -> p k c", p=128), wu.ap[l, i].rearrange("(k p) c -> p k c", p=128),
            wd.ap[l, i].rearrange("(j p) c -> p j c", p=128))


def scratch(dr):
    D_ = {}
    D_["XA"] = dr.tmp("XA", [1024, NK])
    D_["XC"] = dr.tmp("XC", [1024, NK])
    D_["QT"] = dr.tmp("QT", [8, 96, NK], BF16)
    D_["KT"] = dr.tmp("KT", [8, 96, NK], BF16)
    D_["V"] = dr.tmp("V", [NK, 512], BF16)
    D_["MQ"] = dr.tmp("MQ", [NK, 512], BF16)
    D_["MK"] = dr.tmp("MK", [NK, 512], BF16)
    D_["MV"] = dr.tmp("MV", [NK, 512], BF16)
    D_["OT"] = dr.tmp("OT", [512, NK])
    D_["GT"] = dr.tmp("GT", [NK, 16])
    D_["attT"] = dr.tmp("attT", [512, NK])
    D_["HT"] = dr.tmp("HT", [512, NK])
    D_["xrT"] = dr.tmp("xrT", [1024, NK])
    D_["ggT"] = dr.tmp("ggT", [1024, NK])
    D_["hf"] = dr.tmp("hf", [1024, 4 * NTL])
    D_["hb"] = dr.tmp("hb", [1024, 4 * NTL])
    return D_


def phase_A(P, dr, seg):
    S = scratch(dr)
    C = setup_seg(P, seg)
    xin = dr.inp("xT", [1024, NK])
    mods = load_modsF(P, dr, C, 0)
    fw = ffn_wF(dr, 0, 0)
    w_in = dr.inp("w_in", [1024, 2736])
    w_uq = dr.inp("w_uq", [384, 768])
    w_ukk = dr.inp("w_ukk", [256, 768])
    w_ukv = dr.inp("w_ukv", [256, 512])
    cqg_d = dr.inp("cqg", [128, 3])
    ckvg_d = dr.inp("ckvg", [128, 2])
    c96_d = dr.inp("c96", [96, 3])
    oblk_d = dr.inp("oblk", [96, 96])
    rm_d = dr.inp("rm", [96, 96])
    e32_d = dr.inp("e32", [32, 96])
    cos_d = dr.inp("cosT", [96, NK])
    sin_d = dr.inp("sinT", [96, NK])
    xo, QT, KT, Vo, MQ, MK, MV, OT, GT = (S[k] for k in ("XA", "QT", "KT", "V", "MQ", "MK", "MV", "OT", "GT"))

    load_Xg(P, C, xin)
    ffn(P, C, fw[0], fw[1], fw[2], mods[0])
    store_Xg(P, C, xo)
    for bi in range(len(C.blocks)):
        isc = C.blocks[bi][2]
        norm_mod(P, C, bi, mods[1][isc][0], mods[1][isc][1])
    P.barrier()
    wslab = C.wh
    win = T(wslab[:, :, :].rearrange("p s c -> p (s c)")[:, 0:8 * 2736].rearrange("p (k c) -> p k c", k=8),
            [b for t in C.wslots for b in t.bufs])
    winv = w_in.ap.rearrange("(k p) c -> p k c", p=128)
    for k in range(8):
        P.dma("pool", win[:, k, :], T(winv[:, k, :], WDRAM), sem="win")
    wuq = wload(P, "wuq", w_uq.ap.rearrange("(k p) c -> p k c", p=128), [128, 3, 768])
    wukk = wload(P, "wukk", w_ukk.ap.rearrange("(k p) c -> p k c", p=128), [128, 2, 768])
    wukv = wload(P, "wukv", w_ukv.ap.rearrange("(k p) c -> p k c", p=128), [128, 2, 512])
    cqg = wload(P, "cqg", cqg_d.ap, [128, 3], F32)
    ckvg = wload(P, "ckvg", ckvg_d.ap, [128, 2], F32)
    c96 = wload(P, "c96", c96_d.ap, [96, 3], F32)
    oblk = wload(P, "oblk", oblk_d.ap, [96, 96])
    rm = wload(P, "rm", rm_d.ap, [96, 96], F32)
    e32 = wload(P, "e32", e32_d.ap, [32, 96])
    xh = C.xh
    XS = xh[:, :, :].rearrange("p k t -> p (k t)")
    off = [0]

    def xs(name, parts, n):
        a = XS[0:parts, off[0]:off[0] + n]
        off[0] += n
        return T(a, Buf(name))

    cq = [xs(f"cq{k}", 128, 512) for k in range(3)]
    ckv = [xs(f"ckv{k}", 128, 512) for k in range(2)]
    cosb = xs("cosb", 96, 512)
    sinb = xs("sinb", 96, 512)
    qn = [xs(f"qn{i}", 96, 512) for i in range(2)]
    t1 = [xs(f"t1{i}", 96, 512) for i in range(2)]
    t2 = [xs(f"t2{i}", 96, 512) for i in range(2)]
    rr = [xs(f"rr{i}", 128, 512) for i in range(2)]
    otile = [xs(f"ot{i}", 128, 512) for i in range(2)]
    gts = [xs(f"gts{i}", 128, 16) for i in range(2)]
    cqn = [sbt(P, f"cqn{k}", [128, 512], BF16) for k in range(3)]
    ckvn = [sbt(P, f"ckvn{k}", [128, 512], BF16) for k in range(2)]
    krr = sbt(P, "krr", [32, 512], BF16)
    sqb = [sbt(P, f"sqb{i}", [128, 512], BF16) for i in range(2)]
    qo = [sbt(P, f"qo{i}", [96, 512], BF16) for i in range(2)]
    tok = [sbt(P, f"tok{i}", [128, 512], BF16) for i in range(3)]
    ps = C.ps
    cnt = [0]

    def nxt():
        cnt[0] += 1
        return cnt[0]

    def rstd_from(pss, n, parts, invn, out):
        P.ts("dve", out[:parts, :n], pss[:parts, :n], invn, EPS, ALU.mult, ALU.add)
        P.act(out[:parts, :n], out[:parts, :n], AF.Sqrt)
        P.recip(out[:parts, :n], out[:parts, :n])

    def lowrank_norm(raw, nk, gains, outs, n):
        pss = ps[6]
        for k in range(nk):
            q = sqb[k % 2]
            P.act(q[:, :n], raw[k][:, :n], AF.Square)
            P.mm(pss[:, :n], C.ones_bf, q[:, :n], start=(k == 0), stop=(k == nk - 1))
        r = rr[nxt() % 2]
        rstd_from(pss, n, 128, 1.0 / (nk * 128), r)
        for k in range(nk):
            t = C.tmpf[k % 3]
            P.tt("dve", t[:, :n], raw[k][:, :n], r[:, :n], ALU.mult)
            P.act(outs[k][:, :n], t[:, :n], AF.Identity, scale=gains[:, k:k + 1])

    def head_norm_rope(praw, n, gcol, dst):
        i = nxt()
        sq = sqb[i % 2]
        P.act(sq[:96, :n], praw[:96, :n], AF.Square)
        pss = ps[6]
        P.mm(pss[:96, :n], oblk, sq[:96, :n])
        r = rr[i % 2]
        rstd_from(pss, n, 96, c96[:, 2:3], r)
        q = qn[i % 2]
        P.tt("dve", q[:, :n], praw[:96, :n], r[:96, :n], ALU.mult)
        P.act(q[:, :n], q[:, :n], AF.Identity, scale=c96[:, gcol:gcol + 1])
        prot = ps[7]
        P.mm(prot[:96, :n], rm, q[:, :n])
        a = t1[i % 2]
        b = t2[i % 2]
        P.tt("pool", a[:, :n], q[:, :n], cosb[:, :n], ALU.mult)
        P.tt("dve", b[:, :n], prot[:96, :n], sinb[:, :n], ALU.mult)
        o = qo[i % 2]
        P.tt("dve", o[:, :n], a[:, :n], b[:, :n], ALU.add)
        P.dma("sp", dst, o[:, :n], sem="qk")

    COL = dict(cq=0, ckv=384, kr=640, mq=672, mk=1184, mv=1696, mo=2208, mg=2720)
    for bi, (s, n, isc) in enumerate(C.blocks):
        g = C.gcol[bi]
        H = C.H[bi]
        P.dma("sp", cosb[:, :n], cos_d.with_ap(cos_d.ap[:, g:g + n]), sem="cs")
        P.dma("sp", sinb[:, :n], sin_d.with_ap(sin_d.ap[:, g:g + n]), sem="cs")
        for j in range(3):
            pp = ps[j % 2]
            for k in range(8):
                P.mm(pp[:, :n], win[:, k, COL["cq"] + j * 128:COL["cq"] + (j + 1) * 128], H[k], start=(k == 0), stop=(k == 7))
            P.copy("act", cq[j][:, :n], pp[:, :n])
        for j in range(2):
            pp = ps[j % 2]
            for k in range(8):
                P.mm(pp[:, :n], win[:, k, COL["ckv"] + j * 128:COL["ckv"] + (j + 1) * 128], H[k], start=(k == 0), stop=(k == 7))
            P.copy("act", ckv[j][:, :n], pp[:, :n])
        pp = ps[2]
        for k in range(8):
            P.mm(pp[:32, :n], win[:, k, COL["kr"]:COL["kr"] + 32], H[k], start=(k == 0), stop=(k == 7))
        P.copy("act", krr[:, :n], pp[:32, :n])
        lowrank_norm(cq, 3, cqg, cqn, n)
        lowrank_norm(ckv, 2, ckvg, ckvn, n)
        for j in range(4):
            pp = ps[j % 2]
            for k in range(8):
                P.mm(pp[:, :n], win[:, k, COL["mo"] + j * 128:COL["mo"] + (j + 1) * 128], H[k], start=(k == 0), stop=(k == 7))
            ot = otile[j % 2]
            P.copy("act", ot[:, :n], pp[:, :n])
            P.dma("sp", OT.with_ap(OT.ap[j * 128:(j + 1) * 128, g:g + n]), ot[:, :n], sem="oo")
        for tt_ in range(n // 128):
            ts_ = slice(tt_ * 128, (tt_ + 1) * 128)
            rows = slice(g + tt_ * 128, g + (tt_ + 1) * 128)
            for nm, dst in (("mq", MQ), ("mk", MK), ("mv", MV)):
                pp = ps[3 + nxt() % 2]
                for k in range(8):
                    P.mm(pp[:, :], H[k][:, ts_], win[:, k, COL[nm]:COL[nm] + 512], start=(k == 0), stop=(k == 7))
                tk = tok[nxt() % 3]
                P.copy("act", tk, pp)
                P.dma("sp", dst.with_ap(dst.ap[rows, :]), tk, sem="tok")
            pp = ps[5]
            for k in range(8):
                P.mm(pp[:, 0:16], H[k][:, ts_], win[:, k, COL["mg"]:COL["mg"] + 16], start=(k == 0), stop=(k == 7))
            gg_ = gts[nxt() % 2]
            P.copy("act", gg_, pp[:, 0:16])
            P.dma("sp", GT.with_ap(GT.ap[rows, :]), gg_, sem="tok")
            pp = ps[3 + nxt() % 2]
            for k in range(2):
                P.mm(pp[:, :], ckvn[k][:, ts_], wukv[:, k, :], start=(k == 0), stop=(k == 1))
            tk = tok[nxt() % 3]
            P.copy("act", tk, pp)
            P.dma("sp", Vo.with_ap(Vo.ap[rows, :]), tk, sem="tok")
        for h in range(8):
            pp = ps[h % 2]
            for k in range(3):
                P.mm(pp[:96, :n], wuq[:, k, h * 96:(h + 1) * 96], cqn[k][:, :n], start=(k == 0), stop=(k == 2))
            head_norm_rope(pp, n, 0, QT.with_ap(QT.ap[h, :, g:g + n]))
            pp = ps[2 + h % 2]
            for k in range(2):
                P.mm(pp[:96, :n], wukk[:, k, h * 96:(h + 1) * 96], ckvn[k][:, :n], start=(k == 0), stop=False)
            P.mm(pp[:96, :n], e32, krr[:, :n], start=False, stop=True)
            head_norm_rope(pp, n, 1, KT.with_ap(KT.ap[h, :, g:g + n]))


def phase_B(P, dr):
    S = scratch(dr)
    QT, KT, Va, MQd, MKd, MVd, GT, attT, HT = (S[k] for k in ("QT", "KT", "V", "MQ", "MK", "MV", "GT", "attT", "HT"))
    gb_d = dr.inp("gb16", [128, 16])
    triF_d = dr.inp("triF", [128, 128])
    triB_d = dr.inp("triB", [128, 128])
    id_d = dr.inp("ident", [128, 128])
    blocks = [(NTC + s, 512, False) for s in range(0, 4 * NTL, 512)] + [(0, NTC, True)]
    psS = psplit(P, "psS", F32, 512, 2)
    psO = psplit(P, "psO", F32, 512, 2)
    psT = psplit(P, "psT", BF16, 512, 2)
    psG = psplit(P, "psG", F32, 128, 4)
    psD = psplit(P, "psD", F32, 256, 2)
    psN = psplit(P, "psN", F32, 256, 2)
    ones_bf = sbt(P, "ones_bf", [128, 128], BF16)
    P.memset("pool", ones_bf, 1.0)
    ones_f = sbt(P, "ones_f", [128, 128], F32)
    P.memset("pool", ones_f, 1.0)
    triF = wload(P, "triF", triF_d.ap, [128, 128], F32)
    triB = wload(P, "triB", triB_d.ap, [128, 128], F32)
    ident = wload(P, "ident", id_d.ap, [128, 128], BF16)
    gb16 = wload(P, "gb16", gb_d.ap, [128, 16], F32)
    tri = [triF, triB]
    kth = [sbt(P, f"kth{i}", [96, NK], BF16) for i in range(2)]
    vah = [sbt(P, f"vah{i}", [128, NCH, 128], BF16) for i in range(2)]
    qh1 = sbt(P, "qh", [96, NK], BF16)
    qh = [qh1, qh1]
    for i in range(2):
        P.memset("pool", vah[i][:, :, 64:128], 1.0)
    Et = [sbt(P, f"Et{i}", [128, 512], BF16) for i in range(3)]
    rc = [sbt(P, f"rc{i}", [128, 512], F32) for i in range(2)]
    ob = [sbt(P, f"ob{i}", [64, 512], F32) for i in range(2)]
    vv = Va.ap.rearrange("(t p) c -> p t c", p=128)
    SC = 1.0 / np.sqrt(96.0)

    def load_head(h):
        i = h % 2
        P.dma("sp", kth[i], KT.with_ap(KT.ap[h]), sem=f"kh{i}")
        for c0 in range(0, NCH, 6):
            P.dma("sp", vah[i][:, c0:c0 + 6, 0:64], Va.with_ap(vv[:, c0:c0 + 6, h * 64:(h + 1) * 64]), sem=f"kh{i}")

    def att_gen():
        work = []
        for h in range(8):
            for (g, n, isc) in blocks:
                tiles = list(range(NTC // 128) if isc else range(NCH))
                for kt in tiles:
                    work.append((h, g, n, kt, kt == tiles[0], kt == tiles[-1]))
        load_head(0)
        nb = 0
        pend = None
        for it, (h, g, n, kt, first, last) in enumerate(work):
            i = h % 2
            if first and g == blocks[0][0]:
                P.dma("sp", qh1, QT.with_ap(QT.ap[h]), sem="qh")
                if h + 1 < 8:
                    load_head(h + 1)
            pS = psS[it % 2]
            P.mm(pS[:, :n], kth[i][:, kt * 128:(kt + 1) * 128], qh1[:, g:g + n])
            E = Et[it % 3]
            P.act(E[:, :n], pS[:, :n], AF.Exp, scale=SC)
            if pend is not None:
                pend()
            def fin(h=h, g=g, n=n, kt=kt, first=first, last=last, i=i, E=E):
                nonlocal nb
                po = psO[nb % 2]
                P.mm(po[:, :n], vah[i][:, kt, :], E[:, :n], start=first, stop=last)
                if last:
                    r = rc[nb % 2]
                    P.recip(r[64:128, :n], po[64:128, :n])
                    o = ob[nb % 2]
                    P.tt("dve", o[:, :n], po[0:64, :n], r[64:128, :n], ALU.mult)
                    P.dma("sp", attT.with_ap(attT.ap[h * 64:(h + 1) * 64, g:g + n]), o[:, :n], sem="ao")
                    nb += 1
            pend = fin
            yield
        pend()

    gt = sbt(P, "gt", [128, NCH, 4], F32)
    mq = sbt(P, "mq", [128, NCH, 128], BF16)
    mk = sbt(P, "mk", [128, NCH, 128], BF16)
    mv = sbt(P, "mv", [128, NCH, 128], BF16)
    HS = sbt(P, "HS", [128, NK], F32)
    HSb = [Buf(f"HS{c}") for c in range(NCH)]
    lf = [sbt(P, f"lf{d}", [128, NCH], F32) for d in range(2)]
    EQ = [sbt(P, f"eq{d}", [128, NCH], F32) for d in range(2)]
    EK = [sbt(P, f"ek{d}", [128, NCH], F32) for d in range(2)]
    EL = [sbt(P, f"el{d}", [128, NCH], F32) for d in range(2)]
    CN = [sbt(P, f"CN{d}", [128, 256], F32) for d in range(2)]
    CNb = [sbt(P, f"CNb{d}", [128, 256], BF16) for d in range(2)]
    qs = [[sbt(P, f"qs{d}{i}", [128, 128], BF16) for i in range(2)] for d in range(2)]
    ks = [[sbt(P, f"ks{d}{i}", [128, 128], BF16) for i in range(2)] for d in range(2)]
    qkT = [[sbt(P, f"qkT{d}{i}", [128, 256], BF16) for i in range(2)] for d in range(2)]
    Sm = [[sbt(P, f"Sm{d}{i}", [128, 128], BF16) for i in range(2)] for d in range(2)]
    dn = [sbt(P, f"dn{d}", [128, 128], F32) for d in range(2)]
    hc = [sbt(P, f"hc{d}", [128, 128], F32) for d in range(2)]
    tmpc = [sbt(P, f"tmpc{d}", [128, 256], F32) for d in range(2)]
    order = [list(range(NCH)), [1, 0] + list(range(NCH - 1, 1, -1))]
    g4v = GT.ap.rearrange("(c p) (g h) -> p c g h", p=128, h=4)

    def ml_setup(h):
        for c0 in range(0, NCH, 6):
            for g_ in range(4):
                P.dma("sp", gt[:, c0:c0 + 6, g_], GT.with_ap(g4v[:, c0:c0 + 6, g_, h]), sem="mlc", allow_slow_non_contiguous=True)
        for g in range(4):
            P.ts("dve", gt[:, :, g], gt[:, :, g], gb16[:, g * 4 + h:g * 4 + h + 1], None, ALU.add)
        for t_, d_ in ((mq, MQd), (mk, MKd), (mv, MVd)):
            dv_ = d_.ap.rearrange("(c p) d -> p c d", p=128)
            for c0 in range(0, NCH, 6):
                P.dma("sp", t_[:, c0:c0 + 6, :], d_.with_ap(dv_[:, c0:c0 + 6, h * 128:(h + 1) * 128]), sem="mlc")
        for d in range(2):
            P.act(lf[d], gt[:, :, 2 * d + 1], AF.Exp, scale=-1.0)
            P.ts("dve", lf[d], lf[d], 1.0, None, ALU.add)
            P.act(lf[d], lf[d], AF.Ln)
            P.ts("dve", lf[d], lf[d], -1.0, None, ALU.mult)
            pB = psG[2]
            pTt = psG[3]
            P.mm(pB[:, :NCH], tri[d], lf[d])
            P.mm(pTt[:, :NCH], ones_f, lf[d])
            P.act(EQ[d], pB[:, :NCH], AF.Exp)
            P.tt("dve", EK[d], gt[:, :, 2 * d], pB[:, :NCH], ALU.subtract)
            P.act(EK[d], EK[d], AF.Exp)
            P.ts("dve", EK[d], EK[d], float(128.0 ** -0.5), None, ALU.mult)
            P.act(EL[d], pTt[:, :NCH], AF.Exp)
            P.memset("pool", CN[d], 0.0)
            P.memset("pool", CNb[d], 0.0)

    def pre(d, c, i):
        P.ts("dve", qs[d][i], mq[:, c, :], EQ[d][:, c:c + 1], None, ALU.mult)
        P.act(ks[d][i], mk[:, c, :], AF.Identity, scale=EK[d][:, c:c + 1])
        yield
        pT = psT[d]
        P.transpose(pT[:, 0:128], qs[d][i], ident)
        P.transpose(pT[:, 128:256], ks[d][i], ident)
        yield
        P.copy("act", qkT[d][i], pT[:, 0:256])
        yield
        pS = psG[d]
        P.mm(pS, qkT[d][i][:, 128:256], qkT[d][i][:, 0:128])
        yield
        P.tt("dve", Sm[d][i], pS, tri[d], ALU.mult)
        yield

    def post(h, written, d, c, i):
        pD = psD[d]
        P.mm(pD[:, 0:128], ks[d][i], mv[:, c, :])
        P.mm(pD[:, 128:256], ks[d][i], ones_bf)
        pN = psN[d]
        P.mm(pN[:, 0:128], mv[:, c, :], Sm[d][i], start=True, stop=False)
        P.mm(pN[:, 0:128], CNb[d][:, 0:128], qkT[d][i][:, 0:128], start=False, stop=True)
        P.mm(pN[:, 128:256], ones_bf, Sm[d][i], start=True, stop=False)
        P.mm(pN[:, 128:256], CNb[d][:, 128:256], qkT[d][i][:, 0:128], start=False, stop=True)
        yield
        P.act(dn[d], pN[:, 128:256], AF.Abs)
        P.tt("dve", tmpc[d], CN[d], psD[d], ALU.add)
        P.ts("dve", CN[d], tmpc[d], EL[d][:, c:c + 1], None, ALU.mult)
        P.copy("act", CNb[d], CN[d])
        P.ts("dve", dn[d], dn[d], 1.0, None, ALU.max)
        P.recip(dn[d], dn[d])
        hsl = T(HS.ap[:, c * 128:(c + 1) * 128], HSb[c])
        if c in written:
            P.tt("dve", hc[d], pN[:, 0:128], dn[d], ALU.mult)
            P.tt("pool", hsl, hsl, hc[d], ALU.add)
            P.dma("sp", HT.with_ap(HT.ap[h * 128:(h + 1) * 128, c * 128:(c + 1) * 128]), hsl, sem="ho")
        else:
            P.tt("dve", hsl, pN[:, 0:128], dn[d], ALU.mult)
            written.add(c)
        yield

    def ml_gen():
        for h in range(4):
            ml_setup(h)
            yield
            written = set()
            for d in range(2):
                yield from pre(d, order[d][0], 0)
            for st in range(NCH):
                for d in range(2):
                    if st + 1 < NCH:
                        yield from pre(d, order[d][st + 1], (st + 1) % 2)
                    yield from post(h, written, d, order[d][st], st % 2)

    a = att_gen()
    m = ml_gen()
    for i, _ in enumerate(a):
        if i % 2 == 1:
            next(m, None)
    for _ in m:
        pass


def phase_C(P, dr, seg):
    S = scratch(dr)
    C = setup_seg(P, seg)
    xin, attT, hmT, oT, xo, ggT, xrT = (S[k] for k in ("XA", "attT", "HT", "OT", "XC", "ggT", "xrT"))
    wout_d = dr.inp("even_w_out", [1024, 1024])
    outg_d = dr.inp("outg", [128, 4])
    mods0 = load_modsF(P, dr, C, 0)
    mods1 = load_modsF(P, dr, C, 1)
    fw02 = ffn_wF(dr, 0, 1)
    fw11 = ffn_wF(dr, 1, 0)
    win_d = dr.inp("odd_w_in", [1024, 2048])
    load_Xg(P, C, xin)
    wslab = C.wh[:, :, :].rearrange("p s c -> p (s c)")
    allw = [b for t in C.wslots for b in t.bufs]
    wout = T(wslab[:, 0:8192].rearrange("p (k c) -> p k c", k=8), allw)
    wv = wout_d.ap.rearrange("(k p) c -> p k c", p=128)
    for k in range(8):
        P.dma("pool", wout[:, k, :], T(wv[:, k, :], WDRAM), sem="win")
    outg = wload(P, "outg", outg_d.ap, [128, 4], F32)
    mixb = [sbt(P, f"mix{k}", [128, 512], BF16) for k in range(8)]
    hmt = [sbt(P, f"hmt{i}", [128, 512], F32) for i in range(2)]
    ott = [sbt(P, f"ott{i}", [128, 512], F32) for i in range(2)]
    rr = [sbt(P, f"rrc{i}", [128, 512], F32) for i in range(2)]

    def mixf(bi):
        s, n, isc = C.blocks[bi]
        g = C.gcol[bi]
        for j in range(4):
            P.dma("pool", mixb[j][:, :n], attT.with_ap(attT.ap[j * 128:(j + 1) * 128, g:g + n]), sem="mx")
            hm = hmt[j % 2]
            ot = ott[j % 2]
            P.dma("sp", hm[:, :n], hmT.with_ap(hmT.ap[j * 128:(j + 1) * 128, g:g + n]), sem="mx2")
            P.dma("sp", ot[:, :n], oT.with_ap(oT.ap[j * 128:(j + 1) * 128, g:g + n]), sem="mx2")
            sq = C.sq[j % 2]
            P.act(sq[:, :n], hm[:, :n], AF.Square)
            pss = C.ps[6]
            P.mm(pss[:, :n], C.ones_bf, sq[:, :n])
            r = rr[j % 2]
            P.ts("dve", r[:, :n], pss[:, :n], 1.0 / 128, EPS, ALU.mult, ALU.add)
            P.act(r[:, :n], r[:, :n], AF.Sqrt)
            P.recip(r[:, :n], r[:, :n])
            P.tt("dve", hm[:, :n], hm[:, :n], r[:, :n], ALU.mult)
            P.act(ot[:, :n], ot[:, :n], AF.Sigmoid)
            P.stt("dve", mixb[4 + j][:, :n], ot[:, :n], outg[:, j:j + 1], hm[:, :n], ALU.mult, ALU.mult)
        return mixb

    mix_out(P, C, wout, mixf, mods0[1])
    P.barrier()
    ffn(P, C, fw02[0], fw02[1], fw02[2], mods0[2])
    ffn(P, C, fw11[0], fw11[1], fw11[2], mods1[0])
    store_Xg(P, C, xo, latent_only=True)
    for bi in range(len(C.blocks)):
        isc = C.blocks[bi][2]
        norm_mod(P, C, bi, mods1[1][isc][0], mods1[1][isc][1])
    P.barrier()
    win = T(wslab[:, 0:8 * 2048].rearrange("p (k c) -> p k c", k=8), allw)
    wv = win_d.ap.rearrange("(k p) c -> p k c", p=128)
    for k in range(8):
        P.dma("pool", win[:, k, :], T(wv[:, k, :], WDRAM), sem="win")
    ga = hmt
    gb_ = ott
    it = 0
    for bi, (s, n, isc) in enumerate(C.blocks):
        g = C.gcol[bi]
        H = C.H[bi]
        if not isc:
            for j in range(8):
                pp = C.ps[j % 2]
                for k in range(8):
                    P.mm(pp[:, :n], win[:, k, j * 128:(j + 1) * 128], H[k], start=(k == 0), stop=(k == 7))
                a = ga[it % 2]
                b = gb_[it % 2]
                it += 1
                P.act(a[:, :n], pp[:, :n], AF.Square)
                P.ts("dve", a[:, :n], a[:, :n], 0.044715, 1.0, ALU.mult, ALU.add)
                P.tt("dve", a[:, :n], a[:, :n], pp[:, :n], ALU.mult)
                P.act(b[:, :n], a[:, :n], AF.Sigmoid, scale=1.5957691216057308)
                P.tt("dve", b[:, :n], b[:, :n], pp[:, :n], ALU.mult)
                P.dma("sp", ggT.with_ap(ggT.ap[j * 128:(j + 1) * 128, g:g + n]), b[:, :n], sem="go")
        for j in range(8):
            pp = C.ps[2 + j % 2]
            for k in range(8):
                P.mm(pp[:, :n], win[:, k, 1024 + j * 128:1024 + (j + 1) * 128], H[k], start=(k == 0), stop=(k == 7))
            a = ga[it % 2]
            it += 1
            P.copy("act", a[:, :n], pp[:, :n])
            P.dma("sp", xrT.with_ap(xrT.ap[j * 128:(j + 1) * 128, g:g + n]), a[:, :n], sem="go")


def phase_D(P, dr):
    S = scratch(dr)
    xrT, hf, hb = S["xrT"], S["hf"], S["hb"]
    cw_d = dr.inp("cw", [8, 128, 4])
    cb_d = dr.inp("cb", [8, 128, 1])
    wa_d = dr.inp("lru_w_a", [1, 2, 8, 128, 128])
    wx_d = dr.inp("lru_w_x", [1, 2, 8, 128, 128])
    bab_d = dr.inp("bab", [8, 128, 6])
    psA = psplit(P, "psA", F32, 512, 2)
    psX = psplit(P, "psX", F32, 512, 2)
    ub = [sbt(P, f"ub{i}", [128, TS + 4], F32) for i in range(2)]
    xc = [sbt(P, f"xc{i}", [128, TS], F32) for i in range(2)]
    rt = sbt(P, "rt", [128, TS], F32)
    ig = sbt(P, "ig", [128, TS], F32)
    at = sbt(P, "at", [128, TS], F32)
    om = sbt(P, "om", [128, TS], F32)
    ip = sbt(P, "ip", [128, TS], F32)
    ht = [sbt(P, f"ht{i}", [128, TS], F32) for i in range(2)]
    cw = sbt(P, "cw", [128, 4], F32)
    cb = sbt(P, "cb", [128, 1], F32)
    wa = [sbt(P, f"wa{d}", [128, 128], F32) for d in range(2)]
    wx = [sbt(P, f"wx{d}", [128, 128], F32) for d in range(2)]
    bab = sbt(P, "bab", [128, 6], F32)
    cc = sbt(P, "cc", [128, 2], F32)
    it = 0
    offs = [j - 2 for j in range(4)]
    for nb in range(8):
        rows = slice(nb * 128, (nb + 1) * 128)
        P.dma("sp", cw, cw_d.with_ap(cw_d.ap[nb]), sem="dc")
        P.dma("sp", cb, cb_d.with_ap(cb_d.ap[nb]), sem="dc")
        P.dma("sp", bab, bab_d.with_ap(bab_d.ap[nb]), sem="dc")
        for d in range(2):
            P.dma("sp", wa[d], wa_d.with_ap(wa_d.ap[0, d, nb]), sem="dc")
            P.dma("sp", wx[d], wx_d.with_ap(wx_d.ap[0, d, nb]), sem="dc")
        P.act(cc, bab[:, 4:6], AF.Exp, scale=-1.0)
        P.ts("dve", cc, cc, 1.0, None, ALU.add)
        P.act(cc, cc, AF.Ln)
        P.ts("dve", cc, cc, -8.0, None, ALU.mult)
        for d in range(2):
            dst = hf if d == 0 else hb
            lat = [(NTC, 4 * NTL, c0) for c0 in range(0, 4 * NTL, TS)]
            if d == 1:
                lat = lat[::-1]
            tiles = [(0, NTC, 0)] + lat
            prev = None
            for (seg0, segn, c0) in tiles:
                n = min(TS, segn - c0)
                u = ub[it % 2]
                x = xc[it % 2]
                h = ht[it % 2]
                it += 1
                lo = max(c0 - 2, 0)
                hi = min(c0 + n + 2, segn)
                if lo > c0 - 2:
                    P.memset("pool", u[:, 0:2], 0.0)
                if hi < c0 + n + 2:
                    P.memset("pool", u[:, n + 2:n + 4], 0.0)
                P.dma("sp", u[:, 2 + (lo - c0):2 + (hi - c0)], xrT.with_ap(xrT.ap[rows, seg0 + lo:seg0 + hi]), sem=f"u{it % 2}")
                P.act(x[:, :n], u[:, 2 + offs[0]:2 + offs[0] + n], AF.Identity, bias=cb[:, 0:1], scale=cw[:, 0:1])
                for j in range(1, 4):
                    P.stt("dve", x[:, :n], u[:, 2 + offs[j]:2 + offs[j] + n], cw[:, j:j + 1], x[:, :n], ALU.mult, ALU.add)
                for q0 in range(0, n, 512):
                    qn_ = min(512, n - q0)
                    pa = psA[(q0 // 512) % 2]
                    px = psX[(q0 // 512) % 2]
                    P.mm(pa[:, :qn_], wa[d], x[:, q0:q0 + qn_])
                    P.act(rt[:, q0:q0 + qn_], pa[:, :qn_], AF.Sigmoid, bias=bab[:, d:d + 1])
                    P.mm(px[:, :qn_], wx[d], x[:, q0:q0 + qn_])
                    P.act(ig[:, q0:q0 + qn_], px[:, :qn_], AF.Sigmoid, bias=bab[:, 2 + d:3 + d])
                P.act(at[:, :n], rt[:, :n], AF.Exp, scale=cc[:, d:d + 1])
                P.tt("pool", om[:, :n], at[:, :n], at[:, :n], ALU.mult)
                P.ts("dve", om[:, :n], om[:, :n], -1.0, 1.0, ALU.mult, ALU.add)
                P.act(om[:, :n], om[:, :n], AF.Sqrt)
                P.tt("pool", ip[:, :n], ig[:, :n], x[:, :n], ALU.mult)
                P.tt("dve", ip[:, :n], ip[:, :n], om[:, :n], ALU.mult)
                init = 0.0 if prev is None else prev
                if d == 0:
                    P.scan(h[:, :n], at[:, :n], ip[:, :n], init, ALU.mult, ALU.add)
                    prev = h[:, n - 1:n]
                else:
                    P.scan(h.with_ap(h.ap[:, :n][:, ::-1]), at.with_ap(at.ap[:, :n][:, ::-1]),
                           ip.with_ap(ip.ap[:, :n][:, ::-1]), init, ALU.mult, ALU.add)
                    prev = h[:, 0:1]
                if seg0 > 0:
                    P.dma("sp", dst.with_ap(dst.ap[rows, c0:c0 + n]), h[:, :n], sem="ho")


def phase_E(P, dr, seg):
    S = scratch(dr)
    C = setup_common(P, NTL, 0)
    C.gcol = [NTC + seg * NTL + s for s in range(0, NTL, 512)]
    xin, ggT, hfT, hbT = S["XC"], S["ggT"], S["hf"], S["hb"]
    wout_d = dr.inp("odd_w_out", [1024, 1024])
    mods1 = load_modsF(P, dr, C, 1)
    fw12 = ffn_wF(dr, 1, 1)
    xo = dr.out("out", [1024, 4 * NTL])
    load_Xg(P, C, xin)
    wslab = C.wh[:, :, :].rearrange("p s c -> p (s c)")
    allw = [b for t in C.wslots for b in t.bufs]
    wout = T(wslab[:, 0:8192].rearrange("p (k c) -> p k c", k=8), allw)
    wv = wout_d.ap.rearrange("(k p) c -> p k c", p=128)
    for k in range(8):
        P.dma("pool", wout[:, k, :], T(wv[:, k, :], WDRAM), sem="win")
    mixb = [sbt(P, f"mix{k}", [128, 512], BF16) for k in range(8)]
    ta = [sbt(P, f"ta{i}", [128, 512], F32) for i in range(2)]
    tb = [sbt(P, f"tb{i}", [128, 512], F32) for i in range(2)]
    tg = [sbt(P, f"tg{i}", [128, 512], F32) for i in range(2)]

    def mixf(bi):
        s, n, isc = C.blocks[bi]
        g = C.gcol[bi]
        for k in range(8):
            a, b_, gg_ = ta[k % 2], tb[k % 2], tg[k % 2]
            rows = slice(k * 128, (k + 1) * 128)
            P.dma("sp", a[:, :n], hfT.with_ap(hfT.ap[rows, g - NTC:g - NTC + n]), sem="mx")
            P.dma("sp", b_[:, :n], hbT.with_ap(hbT.ap[rows, g - NTC:g - NTC + n]), sem="mx")
            P.dma("sp", gg_[:, :n], ggT.with_ap(ggT.ap[rows, g:g + n]), sem="mx")
            P.tt("pool", a[:, :n], a[:, :n], b_[:, :n], ALU.add)
            P.tt("dve", mixb[k][:, :n], a[:, :n], gg_[:, :n], ALU.mult)
        return mixb

    mix_out(P, C, wout, mixf, mods1[1])
    P.barrier()
    ffn(P, C, fw12[0], fw12[1], fw12[2], mods1[2])
    store_Xg(P, C, xo, goff=NTC)


PHASES = None


def build_fused(nc):
    with contextlib.ExitStack() as semstack:
        P = Prog(nc, semstack)
        dr = DR(nc)
        plan = [("M", None)] + [("A", s) for s in range(NSEG)] + [("B", None)] + \
               [("C", s) for s in range(NSEG)] + [("D", None)] + [("E", s) for s in range(NSEG)]
        for nm, seg in plan:
            if PHASES is not None and nm not in PHASES:
                continue
            with contextlib.ExitStack() as pstack:
                P.stack = pstack
                if nm == "M":
                    phase_M(P, dr)
                elif nm == "A":
                    phase_A(P, dr, seg)
                elif nm == "B":
                    phase_B(P, dr)
                elif nm == "C":
                    phase_C(P, dr, seg)
                elif nm == "D":
                    phase_D(P, dr)
                else:
                    phase_E(P, dr, seg)
                P.barrier()
                P.emit()


def rope_tables_g():
    pos = np.arange(4 * NTL)
    rows = (pos // 64).astype(np.float32)
    cols = (pos % 64).astype(np.float32)
    inv = (np.float32(10000.0) ** (-np.arange(0, 16, 2, dtype=np.float32) / np.float32(16))).astype(np.float32)
    cosT = np.ones((96, NK), np.float32)
    sinT = np.zeros((96, NK), np.float32)
    for a, pp in enumerate((rows, cols)):
        ang = (pp[None, :] * inv[:, None]).astype(np.float32)
        for half in range(2):
            r0 = 64 + a * 16 + half * 8
            cosT[r0:r0 + 8, NTC:] = np.cos(ang)
            sinT[r0:r0 + 8, NTC:] = np.sin(ang)
    return cosT, sinT


def maps_fused(inp):
    oblk, rm, e32 = consts_A()
    triF, triB, ident = consts_B()
    cosT, sinT = rope_tables_g()
    ukv = inp["mla_w_ukv"][0].reshape(256, 8, 128)
    w_ukk = np.zeros((256, 8, 96), np.float32)
    w_ukk[:, :, :64] = ukv[:, :, :64]
    w_ukk = w_ukk.reshape(256, 768)
    w_ukv = np.ascontiguousarray(ukv[:, :, 64:].reshape(256, 512))
    c96 = np.zeros((96, 3), np.float32)
    c96[:, 0] = inp["mla_q_g"][0]
    c96[:, 1] = inp["mla_k_g"][0]
    c96[:64, 2] = 1.0 / 64
    c96[64:, 2] = 1.0 / 32
    modb = np.ascontiguousarray(inp["mod_b"].reshape(2, 72, 128).transpose(2, 0, 1).reshape(128, 144))
    gb16 = np.ascontiguousarray(np.broadcast_to(inp["mlstm_gate_b"][0][None, :], (128, 16))).astype(np.float32)
    cw = np.ascontiguousarray(inp["odd_conv_w"][0].reshape(4, 8, 128).transpose(1, 2, 0))
    cb = np.ascontiguousarray(inp["odd_conv_b"][0].reshape(8, 128, 1))
    bab = np.stack([inp["lru_b_a"][0, 0], inp["lru_b_a"][0, 1], inp["lru_b_x"][0, 0],
                    inp["lru_b_x"][0, 1], inp["lru_lam"][0, 0], inp["lru_lam"][0, 1]], axis=1)
    bab = np.ascontiguousarray(bab.reshape(8, 128, 6).astype(np.float32))
    common = dict(mod_w=inp["mod_w"], modb=modb, normg0=normg_table(inp["norm_g"][0]), normg1=normg_table(inp["norm_g"][1]),
                  ffn_w_gate=inp["ffn_w_gate"], ffn_w_up=inp["ffn_w_up"], ffn_w_down=inp["ffn_w_down"],
                  w_in=inp["even_w_in"][0], w_uq=inp["mla_w_uq"][0], w_ukk=w_ukk, w_ukv=w_ukv,
                  cqg=fm(inp["mla_cq_g"][0]), ckvg=fm(inp["mla_ckv_g"][0]), c96=c96, oblk=oblk, rm=rm, e32=e32,
                  cosT=cosT, sinT=sinT, gb16=gb16, triF=triF, triB=triB, ident=ident,
                  even_w_out=inp["even_w_out"][0], outg=fm(inp["mlstm_out_g"][0]), odd_w_in=inp["odd_w_in"][0],
                  cw=cw, cb=cb, lru_w_a=inp["lru_w_a"], lru_w_x=inp["lru_w_x"], bab=bab, odd_w_out=inp["odd_w_out"][0])
    maps = []
    for r in range(8):
        b = r // 4
        xT = np.ascontiguousarray(np.concatenate([inp["ctx"][b].T, inp["x"][b].T], axis=1))
        vecs = np.stack([inp["c"][b], inp["c_ctx"]], 0)
        cv = np.ascontiguousarray(vecs.reshape(2, 8, 128).transpose(2, 1, 0).reshape(128, 16))
        mp = dict(common)
        mp["xT"] = xT
        mp["cv"] = cv
        maps.append(mp)
    return maps


def kernel(**inp):
    inp = {k: np.asarray(v) for k, v in inp.items()}
    nc = bass.Bass("TRN2", target_bir_lowering=False)
    build_fused(nc)
    res = run_bass_kernel_spmd(nc, maps_fused(inp), core_ids=list(range(8))).results
    out = np.zeros((2, 4 * NTL, 1024), np.float32)
    for b in range(2):
        out[b] = np.asarray(res[4 * b]["out"], np.float32).T
    return out
```

```python
import contextlib
import numpy as np
import concourse.bass as bass
import concourse.mybir as mybir
from concourse.bass_utils import run_bass_kernel_spmd

F32 = mybir.dt.float32
BF16 = mybir.dt.bfloat16
AF = mybir.ActivationFunctionType
ALU = mybir.AluOpType
AX = mybir.AxisListType


class Buf:
    __slots__ = ("name", "writer", "readers", "excl", "nowaw")

    def __init__(self, name="", excl=False, nowaw=False):
        self.name = name
        self.nowaw = nowaw
        self.writer = None
        self.readers = []
        self.excl = excl


class Sem:
    __slots__ = ("h", "total", "dma", "name")

    def __init__(self, h, dma, name):
        self.h = h
        self.total = 0
        self.dma = dma
        self.name = name


class T:
    __slots__ = ("ap", "bufs")

    def __init__(self, ap, bufs):
        self.ap = ap
        self.bufs = bufs if isinstance(bufs, (list, tuple)) else [bufs]

    def __getitem__(self, idx):
        return T(self.ap[idx], self.bufs)

    def with_ap(self, ap):
        return T(ap, self.bufs)


ENGS = ("pe", "act", "dve", "pool", "sp")


class Prog:
    def __init__(self, nc, stack):
        self.nc = nc
        self.stack = stack
        self.sem_stack = stack
        self.ops = {e: [] for e in ENGS}
        self.esem = {e: self.new_sem("e_" + e, False) for e in ENGS}
        self.dsems = {}
        self.seen = {e: {} for e in ENGS}
        self.nops = 0

    def new_sem(self, name, dma):
        h = self.sem_stack.enter_context(self.nc.semaphore(name))
        return Sem(h, dma, name)

    def dsem(self, name):
        if name not in self.dsems:
            self.dsems[name] = self.new_sem("d_" + name, True)
        return self.dsems[name]

    def _record(self, eng, fn, reads, writes, sem=None):
        waits = {}

        def need(tok, kind):
            if tok is None:
                return
            s, v, peng = tok
            if not s.dma and peng == eng:
                if kind != "raw":
                    return
                if eng == "pe":
                    return
            if s.dma:
                v = max(v, s.total)
            if self.seen[eng].get(s, 0) >= v:
                return
            if waits.get(s, 0) < v:
                waits[s] = v

        rb = []
        xb = []
        for t in reads:
            if t is None or isinstance(t, (int, float)):
                continue
            for b in t.bufs:
                need(b.writer, "raw")
                if b.excl:
                    for r in b.readers:
                        need(r, "war")
                    xb.append(b)
                else:
                    rb.append(b)
        wb = []
        for t in writes:
            if t is None:
                continue
            for b in t.bufs:
                wb.append(b)
                if not b.nowaw:
                    need(b.writer, "waw")
                for r in b.readers:
                    need(r, "war")
        for s, v in waits.items():
            self.seen[eng][s] = v
        if sem is None:
            s = self.esem[eng]
            s.total += 1
        else:
            s = sem
            s.total += 16
        tok = (s, s.total, eng)
        for b in rb:
            b.readers.append(tok)
        for b in wb:
            b.writer = tok
            b.readers = []
        for b in xb:
            if b.writer is not tok:
                b.readers.append(tok)
                b.writer = (tok[0], tok[1], tok[2]) if False else b.writer
                b.writer = tok
                b.readers = []
        self.ops[eng].append((list(waits.items()), fn, s, 16 if s.dma else 1))
        self.nops += 1

    def barrier(self):
        sems = list(self.esem.values()) + list(self.dsems.values())
        for e in ENGS:
            waits = []
            for s in sems:
                if s.total > self.seen[e].get(s, 0) and not (s is self.esem[e]):
                    waits.append((s, s.total))
                    self.seen[e][s] = s.total
            if waits:
                self.ops[e].append((waits, None, None, 0))

    def emit(self):
        nc = self.nc
        ops = self.ops

        def run(eng, lst):
            for waits, fn, s, inc in lst:
                for ws, wv in waits:
                    eng.wait_ge(ws.h, wv)
                if fn is not None:
                    fn(eng).then_inc(s.h, inc)

        with nc.Block() as block:
            @block.tensor
            def _(e):
                run(e, ops["pe"])

            @block.scalar
            def _(e):
                run(e, ops["act"])

            @block.vector
            def _(e):
                run(e, ops["dve"])

            @block.gpsimd
            def _(e):
                run(e, ops["pool"])

            @block.sync
            def _(e):
                run(e, ops["sp"])

        self.ops = {e: [] for e in ENGS}

    def dma(self, q, out, in_, sem="ld", **kw):
        s = self.dsem(sem)
        self._record(q, lambda e: e.dma_start(out=out.ap, in_=in_.ap, **kw), [in_], [out], sem=s)

    def mm(self, out, lhsT, rhs, start=True, stop=True):
        self._record("pe", lambda e: e.matmul(out.ap, lhsT.ap, rhs.ap, start=start, stop=stop),
                     [lhsT, rhs], [out])

    def transpose(self, out, in_, ident):
        self._record("pe", lambda e: e.transpose(out.ap, in_.ap, ident.ap), [in_, ident], [out])

    def act(self, out, in_, func, bias=None, scale=1.0, accum=None, eng="act"):
        kw = {}
        if bias is not None:
            kw["bias"] = bias.ap if isinstance(bias, T) else bias
        kw["scale"] = scale.ap if isinstance(scale, T) else scale
        if accum is not None:
            kw["accum_out"] = accum.ap
        self._record("act", lambda e: e.activation(out.ap, in_.ap, func, **kw),
                     [in_, bias if isinstance(bias, T) else None, scale if isinstance(scale, T) else None],
                     [out, accum])

    def tt(self, eng, out, in0, in1, op):
        self._record(eng, lambda e: e.tensor_tensor(out.ap, in0.ap, in1.ap, op), [in0, in1], [out])

    def ts(self, eng, out, in0, s1, s2, op0, op1=None):
        a1 = s1.ap if isinstance(s1, T) else s1
        a2 = s2.ap if isinstance(s2, T) else s2
        if op1 is None:
            f = lambda e: e.tensor_scalar(out.ap, in0.ap, a1, None, op0)
        else:
            f = lambda e: e.tensor_scalar(out.ap, in0.ap, a1, a2, op0, op1)
        self._record(eng, f, [in0, s1 if isinstance(s1, T) else None, s2 if isinstance(s2, T) else None], [out])

    def stt(self, eng, out, in0, scalar, in1, op0, op1):
        a = scalar.ap if isinstance(scalar, T) else scalar
        self._record(eng, lambda e: e.scalar_tensor_tensor(out.ap, in0.ap, a, in1.ap, op0, op1),
                     [in0, in1, scalar if isinstance(scalar, T) else None], [out])

    def copy(self, eng, out, in_):
        if eng == "act":
            self._record(eng, lambda e: e.copy(out.ap, in_.ap), [in_], [out])
        else:
            self._record(eng, lambda e: e.tensor_copy(out.ap, in_.ap), [in_], [out])

    def memset(self, eng, out, val):
        self._record(eng, lambda e: e.memset(out.ap, val), [], [out])

    def recip(self, out, in_):
        self._record("dve", lambda e: e.reciprocal(out.ap, in_.ap), [in_], [out])

    def scan(self, out, d0, d1, init, op0, op1):
        a = init.ap if isinstance(init, T) else init
        self._record("dve", lambda e: e.tensor_tensor_scan(out.ap, d0.ap, d1.ap, a, op0, op1),
                     [d0, d1, init if isinstance(init, T) else None], [out])


D = 1024
DFF = 2816
NFF = 22
EPS = 1e-6


class Ctx:
    pass


WDRAM = Buf("wdram")


_UID = [0]


def uname(name):
    _UID[0] += 1
    return f"{name}_u{_UID[0]}"


def sbt(P, name, shape, dt, nb=None):
    h = P.stack.enter_context(P.nc.sbuf_tensor(uname("s_" + name), shape, dt))
    return T(h[tuple(slice(None) for _ in shape)], Buf(name))


def setup_common(P, NTL, NTC):
    nc = P.nc
    C = Ctx()
    C.NTL, C.NTC = NTL, NTC
    NT = NTL + NTC
    C.NT = NT
    C.blocks = [(s, 512, False) for s in range(0, NTL, 512)] + ([(NTL, NTC, True)] if NTC else [])
    xh = P.stack.enter_context(nc.sbuf_tensor(uname("X"), [128, 8, NT], F32))
    C.xh = xh
    hh = P.stack.enter_context(nc.sbuf_tensor(uname("hT"), [128, 8, NT], BF16))
    C.X = [[T(xh[:, k, s:s + n], Buf(f"X{bi}_{k}")) for k in range(8)] for bi, (s, n, c) in enumerate(C.blocks)]
    C.H = [[T(hh[:, k, s:s + n], Buf(f"H{bi}_{k}")) for k in range(8)] for bi, (s, n, c) in enumerate(C.blocks)]
    C.ps = []
    for i in range(8):
        ph = P.stack.enter_context(nc.psum_tensor(uname(f"ps{i}"), [128, 512], F32))
        C.ps.append(T(ph[:, :], Buf(f"ps{i}", excl=True)))
    C.ones_bf = sbt(P, "ones_bf", [128, 128], BF16)
    P.memset("pool", C.ones_bf, 1.0)
    C.NSLOT = 6
    C.SLOTSZ = 4096
    wh = P.stack.enter_context(nc.sbuf_tensor(uname("wslots"), [128, C.NSLOT, C.SLOTSZ], BF16))
    C.wh = wh
    C.wslots = [T(wh[:, i, :], Buf(f"wslot{i}")) for i in range(C.NSLOT)]
    C.wi = 0
    C.wq = []
    C.wloaded = []
    C.sq = [sbt(P, f"sq{i}", [128, 512], BF16) for i in range(2)]
    C.rstd = [sbt(P, f"rstd{i}", [128, 512], F32) for i in range(2)]
    C.tmpf = [sbt(P, f"tmpf{i}", [128, 512], F32) for i in range(3)]
    C.sg = [sbt(P, f"sg{i}", [128, 512], F32) for i in range(2)]
    C.actb = [[sbt(P, f"act{i}_{j}", [128, 512], BF16) for j in range(4)] for i in range(2)]
    C.cnt = 0
    return C


def wq_push(C, dram_ap, K, cols):
    C.wq.append((dram_ap, K, cols))


def wq_issue(P, C):
    if not C.wq:
        return
    dram_ap, K, cols = C.wq.pop(0)
    si = C.wi % C.NSLOT
    C.wi += 1
    slot = C.wslots[si]
    view = slot.with_ap(slot.ap[:, 0:K * cols].rearrange("p (k c) -> p k c", k=K))
    P.dma("pool", view, T(dram_ap, WDRAM), sem=f"w{si}")
    C.wloaded.append(view)


def wq_prefetch(P, C, n):
    for _ in range(n):
        wq_issue(P, C)


def wq_get(P, C):
    return C.wloaded.pop(0)


def norm_mod(P, C, bi, A, B):
    s, n, isc = C.blocks[bi]
    i = C.cnt
    C.cnt += 1
    pss = C.ps[6]
    sq = C.sq
    for k in range(8):
        q = sq[k % 2]
        P.act(q[:, :n], C.X[bi][k], AF.Square)
        P.mm(pss[:, :n], C.ones_bf, q[:, :n], start=(k == 0), stop=(k == 7))
    r = C.rstd[i % 2]
    P.ts("dve", r[:, :n], pss[:, :n], 1.0 / D, EPS, ALU.mult, ALU.add)
    P.act(r[:, :n], r[:, :n], AF.Sqrt)
    P.recip(r[:, :n], r[:, :n])
    for k in range(8):
        t = C.tmpf[k % 3]
        P.tt("dve", t[:, :n], C.X[bi][k], r[:, :n], ALU.mult)
        P.act(C.H[bi][k], t[:, :n], AF.Identity, bias=B[:, k:k + 1], scale=A[:, k:k + 1])


def ffn(P, C, wg_d, wu_d, wd_d, mods, groups=(4, 4, 4, 4, 4, 2)):
    for bi in range(len(C.blocks)):
        isc = C.blocks[bi][2]
        norm_mod(P, C, bi, mods[isc][0], mods[isc][1])
    j0 = 0
    for g in groups:
        wq_push(C, wg_d[:, :, j0 * 128:(j0 + g) * 128], 8, g * 128)
        wq_push(C, wu_d[:, :, j0 * 128:(j0 + g) * 128], 8, g * 128)
        wq_push(C, wd_d[:, j0:j0 + g, :], g, 1024)
        j0 += g
    items = []
    for g in groups:
        for bi in range(len(C.blocks)):
            items.append((g, bi))
    state = {}

    def gateup(it, idx):
        g, bi = it
        s, n, isc = C.blocks[bi]
        if bi == 0:
            state["w"] = (wq_get(P, C), wq_get(P, C), wq_get(P, C))
        wg, wu, wd = state["w"]
        acts = C.actb[idx % 2]
        for jj in range(g):
            pg = C.ps[(2 * jj) % 4]
            pu = C.ps[(2 * jj + 1) % 4]
            for k in range(8):
                P.mm(pg[:, :n], wg[:, k, jj * 128:(jj + 1) * 128], C.H[bi][k], start=(k == 0), stop=(k == 7))
            for k in range(8):
                P.mm(pu[:, :n], wu[:, k, jj * 128:(jj + 1) * 128], C.H[bi][k], start=(k == 0), stop=(k == 7))
            sg = C.sg[jj % 2]
            P.act(sg[:, :n], pg[:, :n], AF.Silu)
            P.tt("dve", acts[jj][:, :n], sg[:, :n], pu[:, :n], ALU.mult)
        return (g, bi, wd, acts)

    def down(st):
        g, bi, wd, acts = st
        s, n, isc = C.blocks[bi]
        G = mods[isc][2]
        for m in range(8):
            py = C.ps[4 + m % 2]
            for jj in range(g):
                P.mm(py[:, :n], wd[:, jj, m * 128:(m + 1) * 128], acts[jj][:, :n], start=(jj == 0), stop=(jj == g - 1))
            P.stt("dve", C.X[bi][m], py[:, :n], G[:, m:m + 1], C.X[bi][m], ALU.mult, ALU.add)

    prev = None
    nb = len(C.blocks)
    wq_prefetch(P, C, 6)
    for idx, it in enumerate(items):
        cur = gateup(it, idx)
        if prev is not None:
            down(prev)
            if prev[1] == nb - 1:
                wq_prefetch(P, C, 3)
        prev = cur
    down(prev)


import ml_dtypes

NPBF = ml_dtypes.bfloat16


class DR:
    def __init__(self, nc):
        self.nc = nc
        self.bufs = {}

    def inp(self, name, shape, dt=F32):
        if name not in self.bufs:
            ap = self.nc.dram_tensor(name, list(shape), dt, kind="ExternalInput").ap()
            self.bufs[name] = T(ap, Buf(name))
        return self.bufs[name]

    def out(self, name, shape, dt=F32):
        if name not in self.bufs:
            ap = self.nc.dram_tensor(name, list(shape), dt, kind="ExternalOutput").ap()
            self.bufs[name] = T(ap, Buf(name, nowaw=True))
        return self.bufs[name]

    def tmp(self, name, shape, dt=F32):
        if name not in self.bufs:
            ap = self.nc.dram_tensor(name, list(shape), dt).ap()
            self.bufs[name] = T(ap, Buf(name, nowaw=True))
        return self.bufs[name]


def launch(build, in_maps):
    nc = bass.Bass("TRN2", target_bir_lowering=False)
    with contextlib.ExitStack() as stack:
        P = Prog(nc, stack)
        build(P, DR(nc))
        P.barrier()
        P.emit()
    res = run_bass_kernel_spmd(nc, in_maps, core_ids=list(range(8)))
    return res.results


def fm(v):
    v = np.asarray(v, np.float32)
    return np.ascontiguousarray(v.reshape(-1, 128).T)


def build_M(P, dr):
    modw = dr.inp("modw", [2, 1024, 1152])
    modb = dr.inp("modb", [128, 18])
    cv = dr.inp("cv", [128, 24])
    outd = dr.out("mod_out", [128, 54])
    cvt = sbt(P, "cvt", [128, 8, 3], F32)
    scv = sbt(P, "scv", [128, 8, 3], F32)
    mb = sbt(P, "mb", [128, 18], F32)
    res = sbt(P, "res", [128, 18, 3], F32)
    P.dma("sp", cvt, cv.with_ap(cv.ap.rearrange("p (k v) -> p k v", v=3)), sem="c")
    P.dma("sp", mb, modb, sem="c")
    P.act(scv, cvt, AF.Silu)
    ph = P.stack.enter_context(P.nc.psum_tensor(uname("psm"), [128, 18, 3], F32))
    ps = T(ph[:, :, :], Buf("psm", excl=True))
    ws = [sbt(P, f"mw{i}", [128, 8, 384], F32) for i in range(2)]
    n = 0
    for l in range(2):
        wv = modw.ap[l].rearrange("(k p) c -> p k c", p=128)
        for pc in range(3):
            w = ws[n % 2]
            P.dma("sp", w, modw.with_ap(wv[:, :, pc * 384:(pc + 1) * 384]), sem=f"mw{n % 2}")
            n += 1
            for jj in range(3):
                j = l * 9 + pc * 3 + jj
                for k in range(8):
                    P.mm(ps[:, j, :], w[:, k, jj * 128:(jj + 1) * 128], scv[:, k, :], start=(k == 0), stop=(k == 7))
    for v in range(3):
        P.tt("dve", res[:, :, v], ps[:, :, v], mb, ALU.add)
    P.dma("sp", outd, res.with_ap(res.ap.rearrange("p j v -> p (j v)")), sem="o")


def run_M(inp):
    c, c_ctx, mod_w, mod_b = inp["c"], inp["c_ctx"], inp["mod_w"], inp["mod_b"]
    vecs = np.stack([c[0], c[1], c_ctx], 0)
    cv = np.ascontiguousarray(vecs.reshape(3, 8, 128).transpose(2, 1, 0).reshape(128, 24))
    maps = []
    for r in range(8):
        cs = slice(1152 * r, 1152 * (r + 1))
        mbr = mod_b[:, cs].reshape(2, 9, 128).transpose(2, 0, 1).reshape(128, 18)
        maps.append({"modw": np.ascontiguousarray(mod_w[:, :, cs]), "modb": np.ascontiguousarray(mbr), "cv": cv})
    res = launch(build_M, maps)
    m = np.zeros((2, 3, 9216), np.float32)
    for r in range(8):
        o = res[r]["mod_out"].reshape(128, 2, 9, 3)
        for l in range(2):
            blk = o[:, l].transpose(2, 1, 0).reshape(3, 1152)
            m[l, :, 1152 * r:1152 * (r + 1)] = blk
    return m


def mod_table(m, l, b):
    t = np.zeros((128, 9, 8, 2), np.float32)
    for xc, v in enumerate((b, 2)):
        t[:, :, :, xc] = m[l, v].reshape(9, 8, 128).transpose(2, 0, 1)
    return np.ascontiguousarray(t.reshape(128, 144))


NTL = 2048
NTC = 256
NT = NTL + NTC


def load_mods(P, dr, C, tag=""):
    modt_d = dr.inp("modt" + tag, [128, 144])
    ng_d = dr.inp("normg" + tag, [128, 24])
    mt = sbt(P, "mt" + tag, [128, 9, 8, 2], F32)
    ng = sbt(P, "ng" + tag, [128, 3, 8], F32)
    P.dma("sp", mt, modt_d.with_ap(modt_d.ap.rearrange("p (i k x) -> p i k x", i=9, k=8)), sem="c")
    P.dma("sp", ng, ng_d.with_ap(ng_d.ap.rearrange("p (i k) -> p i k", i=3)), sem="c")
    ab = sbt(P, "modAB" + tag, [128, 3, 2, 3, 8], F32)
    mods = []
    for i in range(3):
        row = {}
        for xc in range(2):
            A = ab[:, i, xc, 0, :]
            B = ab[:, i, xc, 1, :]
            G = ab[:, i, xc, 2, :]
            P.ts("dve", A, mt[:, 3 * i + 1, :, xc], 1.0, None, ALU.add)
            P.tt("dve", A, A, ng[:, i, :], ALU.mult)
            P.copy("dve", B, mt[:, 3 * i, :, xc])
            P.ts("dve", G, mt[:, 3 * i + 2, :, xc], 0.5 if i != 1 else 1.0, None, ALU.mult)
            row[bool(xc)] = (A, B, G)
        mods.append(row)
    return mods


def ffn_w(dr, tag):
    wg = dr.inp("wg" + tag, [1024, 2816])
    wu = dr.inp("wu" + tag, [1024, 2816])
    wd = dr.inp("wd" + tag, [2816, 1024])
    return (wg.ap.rearrange("(k p) c -> p k c", p=128), wu.ap.rearrange("(k p) c -> p k c", p=128),
            wd.ap.rearrange("(j p) c -> p j c", p=128))


def load_X(P, C, xin, nblk=None):
    xv = xin.ap.rearrange("(k p) t -> p k t", p=128)
    for bi, (s, n, c) in enumerate(C.blocks):
        for k in range(8):
            P.dma("sp", C.X[bi][k], xin.with_ap(xv[:, k, s:s + n]), sem="xin")


def store_X(P, C, xout):
    xv = xout.ap.rearrange("(k p) t -> p k t", p=128)
    for bi, (s, n, c) in enumerate(C.blocks):
        for k in range(8):
            P.dma("sp", xout.with_ap(xv[:, k, s:s + n]), C.X[bi][k], sem="xout")


def wload(P, name, dram_ap, shape, dt=BF16, q=None):
    t = sbt(P, name, shape, dt)
    P.dma("pool" if dt == BF16 else "sp", t, T(dram_ap, WDRAM), sem="c")
    return t


def build_A(P, dr):
    C = setup_common(P, NTL, NTC)
    xin = dr.inp("xT", [1024, NT])
    mods = load_mods(P, dr, C)
    fw = ffn_w(dr, "")
    w_in = dr.inp("w_in", [1024, 2736])
    w_uq = dr.inp("w_uq", [384, 768])
    w_ukk = dr.inp("w_ukk", [256, 768])
    w_ukv = dr.inp("w_ukv", [256, 512])
    cqg_d = dr.inp("cqg", [128, 3])
    ckvg_d = dr.inp("ckvg", [128, 2])
    c96_d = dr.inp("c96", [96, 3])
    oblk_d = dr.inp("oblk", [96, 96])
    rm_d = dr.inp("rm", [96, 96])
    e32_d = dr.inp("e32", [32, 96])
    cos_d = dr.inp("cosT", [96, NT])
    sin_d = dr.inp("sinT", [96, NT])
    xo = dr.out("xTo", [1024, NT])
    QT = dr.out("QT", [8, 96, NT], BF16)
    KT = dr.out("KT", [8, 96, NT], BF16)
    Vo = dr.out("V", [NT, 512], BF16)
    MQ = dr.out("MQ", [NT, 512], BF16)
    MK = dr.out("MK", [NT, 512], BF16)
    MV = dr.out("MV", [NT, 512], BF16)
    OT = dr.out("OT", [512, NT], F32)
    GT = dr.out("GT", [NT, 16], F32)

    load_X(P, C, xin)
    ffn(P, C, fw[0], fw[1], fw[2], mods[0])
    store_X(P, C, xo)
    for bi in range(len(C.blocks)):
        isc = C.blocks[bi][2]
        norm_mod(P, C, bi, mods[1][isc][0], mods[1][isc][1])
    P.barrier()
    wslab = C.wh
    win = T(wslab[:, :, :].rearrange("p s c -> p (s c)")[:, 0:8 * 2736].rearrange("p (k c) -> p k c", k=8),
            [b for t in C.wslots for b in t.bufs])
    winv = w_in.ap.rearrange("(k p) c -> p k c", p=128)
    for k in range(8):
        P.dma("pool", win[:, k, :], T(winv[:, k, :], WDRAM), sem="win")
    wuq = wload(P, "wuq", w_uq.ap.rearrange("(k p) c -> p k c", p=128), [128, 3, 768])
    wukk = wload(P, "wukk", w_ukk.ap.rearrange("(k p) c -> p k c", p=128), [128, 2, 768])
    wukv = wload(P, "wukv", w_ukv.ap.rearrange("(k p) c -> p k c", p=128), [128, 2, 512])
    cqg = wload(P, "cqg", cqg_d.ap, [128, 3], F32)
    ckvg = wload(P, "ckvg", ckvg_d.ap, [128, 2], F32)
    c96 = wload(P, "c96", c96_d.ap, [96, 3], F32)
    oblk = wload(P, "oblk", oblk_d.ap, [96, 96])
    rm = wload(P, "rm", rm_d.ap, [96, 96], F32)
    e32 = wload(P, "e32", e32_d.ap, [32, 96])
    xh = C.xh
    XS = xh[:, :, :].rearrange("p k t -> p (k t)")
    off = [0]

    def xs(name, parts, n):
        a = XS[0:parts, off[0]:off[0] + n]
        off[0] += n
        return T(a, Buf(name))

    cq = [xs(f"cq{k}", 128, 512) for k in range(3)]
    ckv = [xs(f"ckv{k}", 128, 512) for k in range(2)]
    cosb = xs("cosb", 96, 512)
    sinb = xs("sinb", 96, 512)
    qn = [xs(f"qn{i}", 96, 512) for i in range(2)]
    t1 = [xs(f"t1{i}", 96, 512) for i in range(2)]
    t2 = [xs(f"t2{i}", 96, 512) for i in range(2)]
    rr = [xs(f"rr{i}", 128, 512) for i in range(2)]
    otile = [xs(f"ot{i}", 128, 512) for i in range(2)]
    gts = [xs(f"gts{i}", 128, 16) for i in range(2)]
    cqn = [sbt(P, f"cqn{k}", [128, 512], BF16) for k in range(3)]
    ckvn = [sbt(P, f"ckvn{k}", [128, 512], BF16) for k in range(2)]
    krr = sbt(P, "krr", [32, 512], BF16)
    sqb = [sbt(P, f"sqb{i}", [128, 512], BF16) for i in range(2)]
    qo = [sbt(P, f"qo{i}", [96, 512], BF16) for i in range(2)]
    tok = [sbt(P, f"tok{i}", [128, 512], BF16) for i in range(3)]
    ps = C.ps
    cnt = [0]

    def nxt():
        cnt[0] += 1
        return cnt[0]

    def rstd_from(pss, n, parts, invn, out):
        P.ts("dve", out[:parts, :n], pss[:parts, :n], invn, EPS, ALU.mult, ALU.add)
        P.act(out[:parts, :n], out[:parts, :n], AF.Sqrt)
        P.recip(out[:parts, :n], out[:parts, :n])

    def lowrank_norm(raw, nk, gains, outs, n):
        pss = ps[6]
        for k in range(nk):
            q = sqb[k % 2]
            P.act(q[:, :n], raw[k][:, :n], AF.Square)
            P.mm(pss[:, :n], C.ones_bf, q[:, :n], start=(k == 0), stop=(k == nk - 1))
        r = rr[nxt() % 2]
        rstd_from(pss, n, 128, 1.0 / (nk * 128), r)
        for k in range(nk):
            t = C.tmpf[k % 3]
            P.tt("dve", t[:, :n], raw[k][:, :n], r[:, :n], ALU.mult)
            P.act(outs[k][:, :n], t[:, :n], AF.Identity, scale=gains[:, k:k + 1])

    def head_norm_rope(praw, n, gcol, dst):
        i = nxt()
        sq = sqb[i % 2]
        P.act(sq[:96, :n], praw[:96, :n], AF.Square)
        pss = ps[6]
        P.mm(pss[:96, :n], oblk, sq[:96, :n])
        r = rr[i % 2]
        rstd_from(pss, n, 96, c96[:, 2:3], r)
        q = qn[i % 2]
        P.tt("dve", q[:, :n], praw[:96, :n], r[:96, :n], ALU.mult)
        P.act(q[:, :n], q[:, :n], AF.Identity, scale=c96[:, gcol:gcol + 1])
        prot = ps[7]
        P.mm(prot[:96, :n], rm, q[:, :n])
        a = t1[i % 2]
        b = t2[i % 2]
        P.tt("pool", a[:, :n], q[:, :n], cosb[:, :n], ALU.mult)
        P.tt("dve", b[:, :n], prot[:96, :n], sinb[:, :n], ALU.mult)
        o = qo[i % 2]
        P.tt("dve", o[:, :n], a[:, :n], b[:, :n], ALU.add)
        P.dma("sp", dst, o[:, :n], sem="qk")

    COL = dict(cq=0, ckv=384, kr=640, mq=672, mk=1184, mv=1696, mo=2208, mg=2720)
    for bi, (s, n, isc) in enumerate(C.blocks):
        H = C.H[bi]
        P.dma("sp", cosb[:, :n], cos_d.with_ap(cos_d.ap[:, s:s + n]), sem="cs")
        P.dma("sp", sinb[:, :n], sin_d.with_ap(sin_d.ap[:, s:s + n]), sem="cs")
        for j in range(3):
            pp = ps[j % 2]
            for k in range(8):
                P.mm(pp[:, :n], win[:, k, COL["cq"] + j * 128:COL["cq"] + (j + 1) * 128], H[k], start=(k == 0), stop=(k == 7))
            P.copy("act", cq[j][:, :n], pp[:, :n])
        for j in range(2):
            pp = ps[j % 2]
            for k in range(8):
                P.mm(pp[:, :n], win[:, k, COL["ckv"] + j * 128:COL["ckv"] + (j + 1) * 128], H[k], start=(k == 0), stop=(k == 7))
            P.copy("act", ckv[j][:, :n], pp[:, :n])
        pp = ps[2]
        for k in range(8):
            P.mm(pp[:32, :n], win[:, k, COL["kr"]:COL["kr"] + 32], H[k], start=(k == 0), stop=(k == 7))
        P.copy("act", krr[:, :n], pp[:32, :n])
        lowrank_norm(cq, 3, cqg, cqn, n)
        lowrank_norm(ckv, 2, ckvg, ckvn, n)
        for j in range(4):
            pp = ps[j % 2]
            for k in range(8):
                P.mm(pp[:, :n], win[:, k, COL["mo"] + j * 128:COL["mo"] + (j + 1) * 128], H[k], start=(k == 0), stop=(k == 7))
            ot = otile[j % 2]
            P.copy("act", ot[:, :n], pp[:, :n])
            P.dma("sp", OT.with_ap(OT.ap[j * 128:(j + 1) * 128, s:s + n]), ot[:, :n], sem="oo")
        for tt_ in range(n // 128):
            ts_ = slice(tt_ * 128, (tt_ + 1) * 128)
            rows = slice(s + tt_ * 128, s + (tt_ + 1) * 128)
            for nm, dst in (("mq", MQ), ("mk", MK), ("mv", MV)):
                pp = ps[3 + nxt() % 2]
                for k in range(8):
                    P.mm(pp[:, :], H[k][:, ts_], win[:, k, COL[nm]:COL[nm] + 512], start=(k == 0), stop=(k == 7))
                tk = tok[nxt() % 3]
                P.copy("act", tk, pp)
                P.dma("sp", dst.with_ap(dst.ap[rows, :]), tk, sem="tok")
            pp = ps[5]
            for k in range(8):
                P.mm(pp[:, 0:16], H[k][:, ts_], win[:, k, COL["mg"]:COL["mg"] + 16], start=(k == 0), stop=(k == 7))
            g = gts[nxt() % 2]
            P.copy("act", g, pp[:, 0:16])
            P.dma("sp", GT.with_ap(GT.ap[rows, :]), g, sem="tok")
            pp = ps[3 + nxt() % 2]
            for k in range(2):
                P.mm(pp[:, :], ckvn[k][:, ts_], wukv[:, k, :], start=(k == 0), stop=(k == 1))
            tk = tok[nxt() % 3]
            P.copy("act", tk, pp)
            P.dma("sp", Vo.with_ap(Vo.ap[rows, :]), tk, sem="tok")
        for h in range(8):
            pp = ps[h % 2]
            for k in range(3):
                P.mm(pp[:96, :n], wuq[:, k, h * 96:(h + 1) * 96], cqn[k][:, :n], start=(k == 0), stop=(k == 2))
            head_norm_rope(pp, n, 0, QT.with_ap(QT.ap[h, :, s:s + n]))
            pp = ps[2 + h % 2]
            for k in range(2):
                P.mm(pp[:96, :n], wukk[:, k, h * 96:(h + 1) * 96], ckvn[k][:, :n], start=(k == 0), stop=False)
            P.mm(pp[:96, :n], e32, krr[:, :n], start=False, stop=True)
            head_norm_rope(pp, n, 1, KT.with_ap(KT.ap[h, :, s:s + n]))


def rope_tables(s):
    pos = np.arange(s * NTL, (s + 1) * NTL)
    rows = (pos // 64).astype(np.float32)
    cols = (pos % 64).astype(np.float32)
    inv = (np.float32(10000.0) ** (-np.arange(0, 16, 2, dtype=np.float32) / np.float32(16))).astype(np.float32)
    cosT = np.ones((96, NT), np.float32)
    sinT = np.zeros((96, NT), np.float32)
    for a, pp in enumerate((rows, cols)):
        ang = (pp[None, :] * inv[:, None]).astype(np.float32)
        for half in range(2):
            r0 = 64 + a * 16 + half * 8
            cosT[r0:r0 + 8, :NTL] = np.cos(ang)
            sinT[r0:r0 + 8, :NTL] = np.sin(ang)
    return cosT, sinT


def consts_A():
    oblk = np.zeros((96, 96), np.float32)
    oblk[:64, :64] = 1
    oblk[64:, 64:] = 1
    rm = np.zeros((96, 96), np.float32)
    for a in range(2):
        for f in range(8):
            p1 = 64 + a * 16 + f
            p2 = 64 + a * 16 + 8 + f
            rm[p2, p1] = -1.0
            rm[p1, p2] = 1.0
    e32 = np.zeros((32, 96), np.float32)
    for k in range(32):
        e32[k, 64 + k] = 1
    return oblk, rm, e32


def normg_table(norm_g_l):
    return np.ascontiguousarray(np.concatenate([fm(norm_g_l[i]) for i in range(3)], axis=1))


def maps_A(inp, m):
    oblk, rm, e32 = consts_A()
    ukv = inp["mla_w_ukv"][0].reshape(256, 8, 128)
    w_ukk = np.zeros((256, 8, 96), np.float32)
    w_ukk[:, :, :64] = ukv[:, :, :64]
    w_ukk = w_ukk.reshape(256, 768)
    w_ukv = np.ascontiguousarray(ukv[:, :, 64:].reshape(256, 512))
    c96 = np.zeros((96, 3), np.float32)
    c96[:, 0] = inp["mla_q_g"][0]
    c96[:, 1] = inp["mla_k_g"][0]
    c96[:64, 2] = 1.0 / 64
    c96[64:, 2] = 1.0 / 32
    maps = []
    for r in range(8):
        b, s = r // 4, r % 4
        cosT, sinT = rope_tables(s)
        xT = np.ascontiguousarray(np.concatenate([inp["x"][b, s * NTL:(s + 1) * NTL].T, inp["ctx"][b].T], axis=1))
        maps.append(dict(xT=xT, modt=mod_table(m, 0, b), normg=normg_table(inp["norm_g"][0]),
                         wg=inp["ffn_w_gate"][0, 0], wu=inp["ffn_w_up"][0, 0], wd=inp["ffn_w_down"][0, 0],
                         w_in=inp["even_w_in"][0], w_uq=inp["mla_w_uq"][0], w_ukk=w_ukk, w_ukv=w_ukv,
                         cqg=fm(inp["mla_cq_g"][0]), ckvg=fm(inp["mla_ckv_g"][0]), c96=c96,
                         oblk=oblk, rm=rm, e32=e32, cosT=cosT, sinT=sinT))
    return maps


NK = NTC + 4 * NTL
NCH = NK // 128


def psplit(P, name, dt, cols, n):
    ph = P.stack.enter_context(P.nc.psum_tensor(uname(name), [128, n * cols], dt))
    bank_bytes = 2048
    esz = 4 if dt == F32 else 2
    bufs = {}
    out = []
    for i in range(n):
        bk = (i * cols * esz) // bank_bytes
        if bk not in bufs:
            bufs[bk] = Buf(f"{name}_b{bk}", excl=True)
        out.append(T(ph[:, i * cols:(i + 1) * cols], bufs[bk]))
    return out


def build_B(P, dr):
    QT = dr.inp("QT", [8, 96, NT], BF16)
    KT = dr.inp("KTa", [8, 96, NK], BF16)
    Va = dr.inp("Va", [NK, 512], BF16)
    mq_d = dr.inp("mq", [NK, 128], BF16)
    mk_d = dr.inp("mk", [NK, 128], BF16)
    mv_d = dr.inp("mv", [NK, 128], BF16)
    g4_d = dr.inp("g4", [NK, 4])
    gb_d = dr.inp("gb", [128, 4])
    triF_d = dr.inp("triF", [128, 128])
    triB_d = dr.inp("triB", [128, 128])
    id_d = dr.inp("ident", [128, 128])
    attT = dr.out("attT", [512, NT])
    HT = dr.out("HT", [128, NK])

    blocks = [(s, 512, False) for s in range(0, NTL, 512)] + [(NTL, NTC, True)]
    psS = psplit(P, "psS", F32, 512, 2)
    psO = psplit(P, "psO", F32, 512, 2)
    psT = psplit(P, "psT", BF16, 512, 2)
    psG = psplit(P, "psG", F32, 128, 4)
    psD = psplit(P, "psD", F32, 256, 2)
    psN = psplit(P, "psN", F32, 256, 2)

    ones_bf = sbt(P, "ones_bf", [128, 128], BF16)
    P.memset("pool", ones_bf, 1.0)
    ones_f = sbt(P, "ones_f", [128, 128], F32)
    P.memset("pool", ones_f, 1.0)
    triF = wload(P, "triF", triF_d.ap, [128, 128], F32)
    triB = wload(P, "triB", triB_d.ap, [128, 128], F32)
    ident = wload(P, "ident", id_d.ap, [128, 128], BF16)
    tri = [triF, triB]

    kth = [sbt(P, f"kth{i}", [96, NK], BF16) for i in range(2)]
    vah = [sbt(P, f"vah{i}", [128, NCH, 128], BF16) for i in range(2)]
    qh = [sbt(P, f"qh{i}", [96, NT], BF16) for i in range(2)]
    for i in range(2):
        P.memset("pool", vah[i][:, :, 64:128], 1.0)
    Et = [sbt(P, f"Et{i}", [128, 512], BF16) for i in range(3)]
    rc = [sbt(P, f"rc{i}", [128, 512], F32) for i in range(2)]
    ob = [sbt(P, f"ob{i}", [64, 512], F32) for i in range(2)]
    vv = Va.ap.rearrange("(t p) c -> p t c", p=128)
    SC = 1.0 / np.sqrt(96.0)

    def load_head(h):
        i = h % 2
        P.dma("sp", kth[i], KT.with_ap(KT.ap[h]), sem=f"kh{i}")
        P.dma("sp", qh[i], QT.with_ap(QT.ap[h]), sem=f"kh{i}")
        for c0 in range(0, NCH, 6):
            P.dma("sp", vah[i][:, c0:c0 + 6, 0:64], Va.with_ap(vv[:, c0:c0 + 6, h * 64:(h + 1) * 64]), sem=f"kh{i}")

    def att_gen():
        it = 0
        load_head(0)
        for h in range(8):
            if h + 1 < 8:
                load_head(h + 1)
            i = h % 2
            for bi, (s, n, isc) in enumerate(blocks):
                po = psO[bi % 2]
                tiles = range(NTC // 128) if isc else range(NCH)
                last = tiles[-1]
                for kt in tiles:
                    pS = psS[it % 2]
                    P.mm(pS[:, :n], kth[i][:, kt * 128:(kt + 1) * 128], qh[i][:, s:s + n])
                    E = Et[it % 3]
                    P.act(E[:, :n], pS[:, :n], AF.Exp, scale=SC)
                    P.mm(po[:, :n], vah[i][:, kt, :], E[:, :n], start=(kt == 0), stop=(kt == last))
                    it += 1
                    yield
                r = rc[bi % 2]
                P.recip(r[64:128, :n], po[64:128, :n])
                o = ob[bi % 2]
                P.tt("dve", o[:, :n], po[0:64, :n], r[64:128, :n], ALU.mult)
                P.dma("sp", attT.with_ap(attT.ap[h * 64:(h + 1) * 64, s:s + n]), o[:, :n], sem="ao")

    gt = sbt(P, "gt", [128, NCH, 4], F32)
    gb = wload(P, "gb", gb_d.ap, [128, 4], F32)
    g4v = g4_d.ap.rearrange("(c p) g -> p c g", p=128)
    for c0 in range(0, NCH, 6):
        P.dma("sp", gt[:, c0:c0 + 6, :], g4_d.with_ap(g4v[:, c0:c0 + 6, :]), sem="c")
    for g in range(4):
        P.ts("dve", gt[:, :, g], gt[:, :, g], gb[:, g:g + 1], None, ALU.add)
    mq = sbt(P, "mq", [128, NCH, 128], BF16)
    mk = sbt(P, "mk", [128, NCH, 128], BF16)
    mv = sbt(P, "mv", [128, NCH, 128], BF16)
    for t_, d_ in ((mq, mq_d), (mk, mk_d), (mv, mv_d)):
        dv_ = d_.ap.rearrange("(c p) d -> p c d", p=128)
        for c0 in range(0, NCH, 6):
            P.dma("sp", t_[:, c0:c0 + 6, :], d_.with_ap(dv_[:, c0:c0 + 6, :]), sem="c")
    HS = sbt(P, "HS", [128, NK], F32)
    HSb = [Buf(f"HS{c}") for c in range(NCH)]
    EQ, EK, EL = [], [], []
    for d in range(2):
        lf = sbt(P, f"lf{d}", [128, NCH], F32)
        P.act(lf, gt[:, :, 2 * d + 1], AF.Exp, scale=-1.0)
        P.ts("dve", lf, lf, 1.0, None, ALU.add)
        P.act(lf, lf, AF.Ln)
        P.ts("dve", lf, lf, -1.0, None, ALU.mult)
        pB = psG[2]
        pTt = psG[3]
        P.mm(pB[:, :NCH], tri[d], lf)
        P.mm(pTt[:, :NCH], ones_f, lf)
        eq = sbt(P, f"eq{d}", [128, NCH], F32)
        ek = sbt(P, f"ek{d}", [128, NCH], F32)
        el = sbt(P, f"el{d}", [128, NCH], F32)
        P.act(eq, pB[:, :NCH], AF.Exp)
        P.tt("dve", ek, gt[:, :, 2 * d], pB[:, :NCH], ALU.subtract)
        P.act(ek, ek, AF.Exp)
        P.ts("dve", ek, ek, float(128.0 ** -0.5), None, ALU.mult)
        P.act(el, pTt[:, :NCH], AF.Exp)
        EQ.append(eq)
        EK.append(ek)
        EL.append(el)
    CN = [sbt(P, f"CN{d}", [128, 256], F32) for d in range(2)]
    CNb = [sbt(P, f"CNb{d}", [128, 256], BF16) for d in range(2)]
    for d in range(2):
        P.memset("pool", CN[d], 0.0)
        P.memset("pool", CNb[d], 0.0)
    qs = [[sbt(P, f"qs{d}{i}", [128, 128], BF16) for i in range(2)] for d in range(2)]
    ks = [[sbt(P, f"ks{d}{i}", [128, 128], BF16) for i in range(2)] for d in range(2)]
    qkT = [[sbt(P, f"qkT{d}{i}", [128, 256], BF16) for i in range(2)] for d in range(2)]
    Sm = [[sbt(P, f"Sm{d}{i}", [128, 128], BF16) for i in range(2)] for d in range(2)]
    dn = [sbt(P, f"dn{d}", [128, 128], F32) for d in range(2)]
    hc = [sbt(P, f"hc{d}", [128, 128], F32) for d in range(2)]
    tmpc = [sbt(P, f"tmpc{d}", [128, 256], F32) for d in range(2)]
    order = [list(range(NCH)), [1, 0] + list(range(NCH - 1, 1, -1))]
    written = set()

    def pre(d, c, i):
        P.ts("dve", qs[d][i], mq[:, c, :], EQ[d][:, c:c + 1], None, ALU.mult)
        P.act(ks[d][i], mk[:, c, :], AF.Identity, scale=EK[d][:, c:c + 1])
        pT = psT[d]
        P.transpose(pT[:, 0:128], qs[d][i], ident)
        P.transpose(pT[:, 128:256], ks[d][i], ident)
        P.copy("act", qkT[d][i], pT[:, 0:256])
        pS = psG[d]
        P.mm(pS, qkT[d][i][:, 128:256], qkT[d][i][:, 0:128])
        P.tt("dve", Sm[d][i], pS, tri[d], ALU.mult)

    def post(d, c, i):
        pD = psD[d]
        P.mm(pD[:, 0:128], ks[d][i], mv[:, c, :])
        P.mm(pD[:, 128:256], ks[d][i], ones_bf)
        pN = psN[d]
        P.mm(pN[:, 0:128], mv[:, c, :], Sm[d][i], start=True, stop=False)
        P.mm(pN[:, 0:128], CNb[d][:, 0:128], qkT[d][i][:, 0:128], start=False, stop=True)
        P.mm(pN[:, 128:256], ones_bf, Sm[d][i], start=True, stop=False)
        P.mm(pN[:, 128:256], CNb[d][:, 128:256], qkT[d][i][:, 0:128], start=False, stop=True)
        P.act(dn[d], pN[:, 128:256], AF.Abs)
        P.ts("dve", dn[d], dn[d], 1.0, None, ALU.max)
        P.recip(dn[d], dn[d])
        hsl = T(HS.ap[:, c * 128:(c + 1) * 128], HSb[c])
        if c in written:
            P.tt("dve", hc[d], pN[:, 0:128], dn[d], ALU.mult)
            P.tt("pool", hsl, hsl, hc[d], ALU.add)
            P.dma("sp", HT.with_ap(HT.ap[:, c * 128:(c + 1) * 128]), hsl, sem="ho")
        else:
            P.tt("dve", hsl, pN[:, 0:128], dn[d], ALU.mult)
            written.add(c)
        P.tt("dve", tmpc[d], CN[d], psD[d], ALU.add)
        P.ts("dve", CN[d], tmpc[d], EL[d][:, c:c + 1], None, ALU.mult)
        P.copy("act", CNb[d], CN[d])

    def ml_gen():
        for d in range(2):
            pre(d, order[d][0], 0)
        for st in range(NCH):
            for d in range(2):
                if st + 1 < NCH:
                    pre(d, order[d][st + 1], (st + 1) % 2)
                post(d, order[d][st], st % 2)
            yield

    a = att_gen() if B_ATT else iter(())
    m = ml_gen() if B_ML else iter(())
    for i, _ in enumerate(a):
        if i % 30 == 5:
            next(m, None)
    for _ in m:
        pass


B_ATT = True
B_ML = True


def consts_B():
    s = np.arange(128)[:, None]
    t = np.arange(128)[None, :]
    triF = (s <= t).astype(np.float32)
    triB = (s >= t).astype(np.float32)
    return triF, triB, np.eye(128, dtype=np.float32)


def maps_B(inp, resA):
    triF, triB, ident = consts_B()
    maps = []
    cat = {}
    for b in range(2):
        rs = [resA[4 * b + s] for s in range(4)]
        cat[b] = dict(
            KT=np.concatenate([rs[0]["KT"][:, :, NTL:]] + [r["KT"][:, :, :NTL] for r in rs], axis=2),
            V=np.concatenate([rs[0]["V"][NTL:]] + [r["V"][:NTL] for r in rs], axis=0),
            MQ=np.concatenate([rs[0]["MQ"][NTL:]] + [r["MQ"][:NTL] for r in rs], axis=0),
            MK=np.concatenate([rs[0]["MK"][NTL:]] + [r["MK"][:NTL] for r in rs], axis=0),
            MV=np.concatenate([rs[0]["MV"][NTL:]] + [r["MV"][:NTL] for r in rs], axis=0),
            GT=np.concatenate([rs[0]["GT"][NTL:]] + [r["GT"][:NTL] for r in rs], axis=0))
    gbias = inp["mlstm_gate_b"][0].reshape(4, 4)
    for r in range(8):
        b, h = r // 4, r % 4
        cb = cat[b]
        hs = slice(h * 128, (h + 1) * 128)
        g4 = np.ascontiguousarray(cb["GT"].reshape(NK, 4, 4)[:, :, h])
        gb = np.ascontiguousarray(np.broadcast_to(gbias[:, h][None, :], (128, 4))).astype(np.float32)
        maps.append(dict(QT=resA[r]["QT"], KTa=np.ascontiguousarray(cb["KT"]), Va=np.ascontiguousarray(cb["V"]),
                         mq=np.ascontiguousarray(cb["MQ"][:, hs]), mk=np.ascontiguousarray(cb["MK"][:, hs]),
                         mv=np.ascontiguousarray(cb["MV"][:, hs]), g4=g4, gb=gb,
                         triF=triF, triB=triB, ident=ident))
    return maps


def mix_out(P, C, wout, mixf, Gsel):
    for bi, (s, n, isc) in enumerate(C.blocks):
        mix = mixf(bi)
        G = Gsel[isc][2]
        for m_ in range(8):
            py = C.ps[4 + m_ % 2]
            for k in range(8):
                P.mm(py[:, :n], wout[:, k, m_ * 128:(m_ + 1) * 128], mix[k][:, :n], start=(k == 0), stop=(k == 7))
            P.stt("dve", C.X[bi][m_], py[:, :n], G[:, m_:m_ + 1], C.X[bi][m_], ALU.mult, ALU.add)


def build_C(P, dr):
    C = setup_common(P, NTL, NTC)
    xin = dr.inp("xT", [1024, NT])
    attT = dr.inp("attT", [512, NT])
    hmT = dr.inp("hmT", [512, NT])
    oT = dr.inp("oT", [512, NT])
    wout_d = dr.inp("w_out", [1024, 1024])
    outg_d = dr.inp("outg", [128, 4])
    mods0 = load_mods(P, dr, C, "0")
    mods1 = load_mods(P, dr, C, "1")
    fw02 = ffn_w(dr, "02")
    fw11 = ffn_w(dr, "11")
    win_d = dr.inp("w_in1", [1024, 2048])
    xo = dr.out("xTo", [1024, NTL])
    ggT = dr.out("ggT", [1024, NTL])
    xrT = dr.out("xrT", [1024, NT])

    load_X(P, C, xin)
    wslab = C.wh[:, :, :].rearrange("p s c -> p (s c)")
    allw = [b for t in C.wslots for b in t.bufs]
    wout = T(wslab[:, 0:8192].rearrange("p (k c) -> p k c", k=8), allw)
    wv = wout_d.ap.rearrange("(k p) c -> p k c", p=128)
    for k in range(8):
        P.dma("pool", wout[:, k, :], T(wv[:, k, :], WDRAM), sem="win")
    outg = wload(P, "outg", outg_d.ap, [128, 4], F32)
    mixb = [sbt(P, f"mix{k}", [128, 512], BF16) for k in range(8)]
    hmt = [sbt(P, f"hmt{i}", [128, 512], F32) for i in range(2)]
    ott = [sbt(P, f"ott{i}", [128, 512], F32) for i in range(2)]
    rr = [sbt(P, f"rrc{i}", [128, 512], F32) for i in range(2)]

    def mixf(bi):
        s, n, isc = C.blocks[bi]
        for j in range(4):
            P.dma("pool", mixb[j][:, :n], attT.with_ap(attT.ap[j * 128:(j + 1) * 128, s:s + n]), sem="mx")
            hm = hmt[j % 2]
            ot = ott[j % 2]
            P.dma("sp", hm[:, :n], hmT.with_ap(hmT.ap[j * 128:(j + 1) * 128, s:s + n]), sem="mx2")
            P.dma("sp", ot[:, :n], oT.with_ap(oT.ap[j * 128:(j + 1) * 128, s:s + n]), sem="mx2")
            sq = C.sq[j % 2]
            P.act(sq[:, :n], hm[:, :n], AF.Square)
            pss = C.ps[6]
            P.mm(pss[:, :n], C.ones_bf, sq[:, :n])
            r = rr[j % 2]
            P.ts("dve", r[:, :n], pss[:, :n], 1.0 / 128, EPS, ALU.mult, ALU.add)
            P.act(r[:, :n], r[:, :n], AF.Sqrt)
            P.recip(r[:, :n], r[:, :n])
            P.tt("dve", hm[:, :n], hm[:, :n], r[:, :n], ALU.mult)
            P.act(ot[:, :n], ot[:, :n], AF.Sigmoid)
            P.stt("dve", mixb[4 + j][:, :n], ot[:, :n], outg[:, j:j + 1], hm[:, :n], ALU.mult, ALU.mult)
        return mixb

    mix_out(P, C, wout, mixf, mods0[1])
    P.barrier()
    ffn(P, C, fw02[0], fw02[1], fw02[2], mods0[2])
    ffn(P, C, fw11[0], fw11[1], fw11[2], mods1[0])
    xv = xo.ap.rearrange("(k p) t -> p k t", p=128)
    for bi, (s, n, isc) in enumerate(C.blocks):
        if not isc:
            for k in range(8):
                P.dma("sp", xo.with_ap(xv[:, k, s:s + n]), C.X[bi][k], sem="xout")
    for bi in range(len(C.blocks)):
        isc = C.blocks[bi][2]
        norm_mod(P, C, bi, mods1[1][isc][0], mods1[1][isc][1])
    P.barrier()
    win = T(wslab[:, 0:8 * 2048].rearrange("p (k c) -> p k c", k=8), allw)
    wv = win_d.ap.rearrange("(k p) c -> p k c", p=128)
    for k in range(8):
        P.dma("pool", win[:, k, :], T(wv[:, k, :], WDRAM), sem="win")
    ga = hmt
    gb_ = ott
    it = 0
    for bi, (s, n, isc) in enumerate(C.blocks):
        H = C.H[bi]
        if not isc:
            for j in range(8):
                pp = C.ps[j % 2]
                for k in range(8):
                    P.mm(pp[:, :n], win[:, k, j * 128:(j + 1) * 128], H[k], start=(k == 0), stop=(k == 7))
                a = ga[it % 2]
                b = gb_[it % 2]
                it += 1
                P.act(a[:, :n], pp[:, :n], AF.Square)
                P.ts("dve", a[:, :n], a[:, :n], 0.044715, 1.0, ALU.mult, ALU.add)
                P.tt("dve", a[:, :n], a[:, :n], pp[:, :n], ALU.mult)
                P.act(b[:, :n], a[:, :n], AF.Sigmoid, scale=1.5957691216057308)
                P.tt("dve", b[:, :n], b[:, :n], pp[:, :n], ALU.mult)
                P.dma("sp", ggT.with_ap(ggT.ap[j * 128:(j + 1) * 128, s:s + n]), b[:, :n], sem="go")
        for j in range(8):
            pp = C.ps[2 + j % 2]
            for k in range(8):
                P.mm(pp[:, :n], win[:, k, 1024 + j * 128:1024 + (j + 1) * 128], H[k], start=(k == 0), stop=(k == 7))
            a = ga[it % 2]
            it += 1
            P.copy("act", a[:, :n], pp[:, :n])
            P.dma("sp", xrT.with_ap(xrT.ap[j * 128:(j + 1) * 128, s:s + n]), a[:, :n], sem="go")


def maps_C(inp, m, resA, resB):
    maps = []
    for r in range(8):
        b, s = r // 4, r % 4
        hm = np.zeros((512, NT), np.float32)
        for h in range(4):
            HTb = resB[4 * b + h]["HT"]
            hm[h * 128:(h + 1) * 128, :NTL] = HTb[:, NTC + s * NTL:NTC + (s + 1) * NTL]
            hm[h * 128:(h + 1) * 128, NTL:] = HTb[:, :NTC]
        maps.append(dict(xT=resA[r]["xTo"], attT=resB[r]["attT"], hmT=hm, oT=resA[r]["OT"],
                         w_out=inp["even_w_out"][0], outg=fm(inp["mlstm_out_g"][0]),
                         modt0=mod_table(m, 0, b), normg0=normg_table(inp["norm_g"][0]),
                         modt1=mod_table(m, 1, b), normg1=normg_table(inp["norm_g"][1]),
                         wg02=inp["ffn_w_gate"][0, 1], wu02=inp["ffn_w_up"][0, 1], wd02=inp["ffn_w_down"][0, 1],
                         wg11=inp["ffn_w_gate"][1, 0], wu11=inp["ffn_w_up"][1, 0], wd11=inp["ffn_w_down"][1, 0],
                         w_in1=inp["odd_w_in"][0]))
    return maps


TS = 2048


def build_D(P, dr):
    u_d = dr.inp("u", [2, 128, NK])
    ur_d = dr.inp("urev", [2, 128, NK])
    cw_d = dr.inp("cw", [128, 4])
    cb_d = dr.inp("cb", [128, 1])
    wa_d = dr.inp("wa", [2, 128, 128])
    wx_d = dr.inp("wx", [2, 128, 128])
    bab_d = dr.inp("bab", [128, 6])
    hf = dr.out("hf", [2, 128, 4 * NTL])
    hb = dr.out("hb", [2, 128, 4 * NTL])
    cw = wload(P, "cw", cw_d.ap, [128, 4], F32)
    cb = wload(P, "cb", cb_d.ap, [128, 1], F32)
    wa = [wload(P, f"wa{d}", wa_d.ap[d], [128, 128], F32) for d in range(2)]
    wx = [wload(P, f"wx{d}", wx_d.ap[d], [128, 128], F32) for d in range(2)]
    bab = wload(P, "bab", bab_d.ap, [128, 6], F32)
    cc = sbt(P, "cc", [128, 2], F32)
    P.act(cc, bab[:, 4:6], AF.Exp, scale=-1.0)
    P.ts("dve", cc, cc, 1.0, None, ALU.add)
    P.act(cc, cc, AF.Ln)
    P.ts("dve", cc, cc, -8.0, None, ALU.mult)
    psA = psplit(P, "psA", F32, 512, 2)
    psX = psplit(P, "psX", F32, 512, 2)
    ub = [sbt(P, f"ub{i}", [128, TS + 4], F32) for i in range(2)]
    xc = [sbt(P, f"xc{i}", [128, TS], F32) for i in range(2)]
    rt = sbt(P, "rt", [128, TS], F32)
    ig = sbt(P, "ig", [128, TS], F32)
    at = sbt(P, "at", [128, TS], F32)
    om = sbt(P, "om", [128, TS], F32)
    ip = sbt(P, "ip", [128, TS], F32)
    ht = [sbt(P, f"ht{i}", [128, TS], F32) for i in range(2)]
    it = 0
    for b in range(2):
        for d in range(2):
            src = u_d if d == 0 else ur_d
            dst = hf if d == 0 else hb
            offs = [j - 2 for j in range(4)] if d == 0 else [2 - j for j in range(4)]
            prev = None
            for (seg0, segn) in ((0, NTC), (NTC, 4 * NTL)):
                for c0 in range(0, segn, TS):
                    n = min(TS, segn - c0)
                    u = ub[it % 2]
                    x = xc[it % 2]
                    h = ht[it % 2]
                    it += 1
                    lo = max(c0 - 2, 0)
                    hi = min(c0 + n + 2, segn)
                    if lo > c0 - 2:
                        P.memset("pool", u[:, 0:2], 0.0)
                    if hi < c0 + n + 2:
                        P.memset("pool", u[:, n + 2:n + 4], 0.0)
                    P.dma("sp", u[:, 2 + (lo - c0):2 + (hi - c0)], src.with_ap(src.ap[b, :, seg0 + lo:seg0 + hi]), sem=f"u{it % 2}")
                    P.act(x[:, :n], u[:, 2 + offs[0]:2 + offs[0] + n], AF.Identity, bias=cb[:, 0:1], scale=cw[:, 0:1])
                    for j in range(1, 4):
                        P.stt("dve", x[:, :n], u[:, 2 + offs[j]:2 + offs[j] + n], cw[:, j:j + 1], x[:, :n], ALU.mult, ALU.add)
                    for q0 in range(0, n, 512):
                        qn_ = min(512, n - q0)
                        pa = psA[(q0 // 512) % 2]
                        px = psX[(q0 // 512) % 2]
                        P.mm(pa[:, :qn_], wa[d], x[:, q0:q0 + qn_])
                        P.act(rt[:, q0:q0 + qn_], pa[:, :qn_], AF.Sigmoid, bias=bab[:, d:d + 1])
                        P.mm(px[:, :qn_], wx[d], x[:, q0:q0 + qn_])
                        P.act(ig[:, q0:q0 + qn_], px[:, :qn_], AF.Sigmoid, bias=bab[:, 2 + d:3 + d])
                    P.act(at[:, :n], rt[:, :n], AF.Exp, scale=cc[:, d:d + 1])
                    P.tt("pool", om[:, :n], at[:, :n], at[:, :n], ALU.mult)
                    P.ts("dve", om[:, :n], om[:, :n], -1.0, 1.0, ALU.mult, ALU.add)
                    P.act(om[:, :n], om[:, :n], AF.Sqrt)
                    P.tt("pool", ip[:, :n], ig[:, :n], x[:, :n], ALU.mult)
                    P.tt("dve", ip[:, :n], ip[:, :n], om[:, :n], ALU.mult)
                    init = 0.0 if prev is None else prev
                    P.scan(h[:, :n], at[:, :n], ip[:, :n], init, ALU.mult, ALU.add)
                    prev = h[:, n - 1:n]
                    if seg0 > 0:
                        P.dma("sp", dst.with_ap(dst.ap[b, :, c0:c0 + n]), h[:, :n], sem="ho")


def maps_D(inp, resC):
    maps = []
    xr = []
    for b in range(2):
        rs = [resC[4 * b + s]["xrT"] for s in range(4)]
        xr.append(np.concatenate([rs[0][:, NTL:]] + [r[:, :NTL] for r in rs], axis=1))
    for n in range(8):
        cs = slice(n * 128, (n + 1) * 128)
        u = np.stack([xr[b][cs] for b in range(2)], 0)
        urev = np.concatenate([u[:, :, :NTC][:, :, ::-1], u[:, :, NTC:][:, :, ::-1]], axis=2)
        bab = np.stack([inp["lru_b_a"][0, 0, cs], inp["lru_b_a"][0, 1, cs], inp["lru_b_x"][0, 0, cs],
                        inp["lru_b_x"][0, 1, cs], inp["lru_lam"][0, 0, cs], inp["lru_lam"][0, 1, cs]], axis=1)
        maps.append(dict(u=np.ascontiguousarray(u), urev=np.ascontiguousarray(urev),
                         cw=np.ascontiguousarray(inp["odd_conv_w"][0][:, cs].T), cb=np.ascontiguousarray(inp["odd_conv_b"][0][cs][:, None]),
                         wa=np.ascontiguousarray(inp["lru_w_a"][0, :, n]), wx=np.ascontiguousarray(inp["lru_w_x"][0, :, n]),
                         bab=np.ascontiguousarray(bab.astype(np.float32))))
    return maps


def build_E(P, dr):
    C = setup_common(P, NTL, 0)
    xin = dr.inp("xT", [1024, NTL])
    ggT = dr.inp("ggT", [1024, NTL])
    hfT = dr.inp("hfT", [1024, NTL])
    hbT = dr.inp("hbT", [1024, NTL])
    wout_d = dr.inp("w_out", [1024, 1024])
    mods1 = load_mods(P, dr, C, "1")
    fw12 = ffn_w(dr, "12")
    xo = dr.out("xTo", [1024, NTL])
    load_X(P, C, xin)
    wslab = C.wh[:, :, :].rearrange("p s c -> p (s c)")
    allw = [b for t in C.wslots for b in t.bufs]
    wout = T(wslab[:, 0:8192].rearrange("p (k c) -> p k c", k=8), allw)
    wv = wout_d.ap.rearrange("(k p) c -> p k c", p=128)
    for k in range(8):
        P.dma("pool", wout[:, k, :], T(wv[:, k, :], WDRAM), sem="win")
    mixb = [sbt(P, f"mix{k}", [128, 512], BF16) for k in range(8)]
    ta = [sbt(P, f"ta{i}", [128, 512], F32) for i in range(2)]
    tb = [sbt(P, f"tb{i}", [128, 512], F32) for i in range(2)]
    tg = [sbt(P, f"tg{i}", [128, 512], F32) for i in range(2)]

    def mixf(bi):
        s, n, isc = C.blocks[bi]
        for k in range(8):
            a, b_, g = ta[k % 2], tb[k % 2], tg[k % 2]
            rows = slice(k * 128, (k + 1) * 128)
            P.dma("sp", a[:, :n], hfT.with_ap(hfT.ap[rows, s:s + n]), sem="mx")
            P.dma("sp", b_[:, :n], hbT.with_ap(hbT.ap[rows, s:s + n]), sem="mx")
            P.dma("sp", g[:, :n], ggT.with_ap(ggT.ap[rows, s:s + n]), sem="mx")
            P.tt("pool", a[:, :n], a[:, :n], b_[:, :n], ALU.add)
            P.tt("dve", mixb[k][:, :n], a[:, :n], g[:, :n], ALU.mult)
        return mixb

    mix_out(P, C, wout, mixf, mods1[1])
    P.barrier()
    ffn(P, C, fw12[0], fw12[1], fw12[2], mods1[2])
    store_X(P, C, xo)


def maps_E(inp, m, resC, resD):
    maps = []
    for r in range(8):
        b, s = r // 4, r % 4
        hf = np.concatenate([resD[n]["hf"][b][:, s * NTL:(s + 1) * NTL] for n in range(8)], axis=0)
        hbfull = [resD[n]["hb"][b][:, ::-1] for n in range(8)]
        hb = np.concatenate([h[:, s * NTL:(s + 1) * NTL] for h in hbfull], axis=0)
        maps.append(dict(xT=resC[r]["xTo"], ggT=resC[r]["ggT"], hfT=np.ascontiguousarray(hf), hbT=np.ascontiguousarray(hb),
                         w_out=inp["odd_w_out"][0], modt1=mod_table(m, 1, b), normg1=normg_table(inp["norm_g"][1]),
                         wg12=inp["ffn_w_gate"][1, 1], wu12=inp["ffn_w_up"][1, 1], wd12=inp["ffn_w_down"][1, 1]))
    return maps


NSEG = 4


def seg_blocks(seg):
    bl = [(s, 512, False, NTC + seg * NTL + s) for s in range(0, NTL, 512)]
    if seg == 0:
        bl.append((NTL, NTC, True, 0))
    return bl


def setup_seg(P, seg):
    C = setup_common(P, NTL, NTC if seg == 0 else 0)
    C.gcol = [b[3] for b in seg_blocks(seg)]
    return C


def load_Xg(P, C, xin):
    xv = xin.ap.rearrange("(k p) t -> p k t", p=128)
    for bi, (s, n, c) in enumerate(C.blocks):
        g = C.gcol[bi]
        for k in range(8):
            P.dma("sp", C.X[bi][k], xin.with_ap(xv[:, k, g:g + n]), sem="xin")


def store_Xg(P, C, xout, goff=0, latent_only=False):
    xv = xout.ap.rearrange("(k p) t -> p k t", p=128)
    for bi, (s, n, c) in enumerate(C.blocks):
        if latent_only and c:
            continue
        g = C.gcol[bi] - goff
        for k in range(8):
            P.dma("sp", xout.with_ap(xv[:, k, g:g + n]), C.X[bi][k], sem="xout")


def phase_M(P, dr):
    modw = dr.inp("mod_w", [2, 1024, 9216])
    modb = dr.inp("modb", [128, 144])
    cv = dr.inp("cv", [128, 16])
    MODS = dr.tmp("MODS", [128, 288])
    cvt = sbt(P, "cvt", [128, 8, 2], F32)
    scv = sbt(P, "scv", [128, 8, 2], F32)
    mb = sbt(P, "mb", [128, 144], F32)
    res = sbt(P, "res", [128, 144, 2], F32)
    P.dma("sp", cvt, cv.with_ap(cv.ap.rearrange("p (k v) -> p k v", v=2)), sem="c")
    P.dma("sp", mb, modb, sem="c")
    P.act(scv, cvt, AF.Silu)
    ph = P.stack.enter_context(P.nc.psum_tensor(uname("psm"), [128, 144, 2], F32))
    ps = T(ph[:, :, :], Buf("psm", excl=True))
    ws = [sbt(P, f"mw{i}", [128, 8, 384], F32) for i in range(2)]
    n = 0
    for l in range(2):
        wv = modw.ap[l].rearrange("(k p) c -> p k c", p=128)
        for pc in range(24):
            w = ws[n % 2]
            P.dma("sp", w, modw.with_ap(wv[:, :, pc * 384:(pc + 1) * 384]), sem=f"mw{n % 2}")
            n += 1
            for jj in range(3):
                j = l * 72 + pc * 3 + jj
                for k in range(8):
                    P.mm(ps[:, j, :], w[:, k, jj * 128:(jj + 1) * 128], scv[:, k, :], start=(k == 0), stop=(k == 7))
    for v in range(2):
        P.tt("dve", res[:, :, v], ps[:, :, v], mb, ALU.add)
    P.dma("sp", MODS, res.with_ap(res.ap.rearrange("p j v -> p (j v)")), sem="o")


def load_modsF(P, dr, C, l):
    MODS = dr.tmp("MODS", [128, 288])
    ng_d = dr.inp(f"normg{l}", [128, 24])
    mt = sbt(P, "mt", [128, 9, 8, 2], F32)
    ng = sbt(P, "ng", [128, 3, 8], F32)
    mv = MODS.ap.rearrange("p (l j v) -> p l j v", l=2, v=2)[:, l].rearrange("p (i k) v -> p i k v", i=9)
    P.dma("sp", mt, MODS.with_ap(mv), sem="c")
    P.dma("sp", ng, ng_d.with_ap(ng_d.ap.rearrange("p (i k) -> p i k", i=3)), sem="c")
    ab = sbt(P, "modAB", [128, 3, 2, 3, 8], F32)
    mods = []
    for i in range(3):
        row = {}
        for xc in range(2):
            A = ab[:, i, xc, 0, :]
            B = ab[:, i, xc, 1, :]
            G = ab[:, i, xc, 2, :]
            P.ts("dve", A, mt[:, 3 * i + 1, :, xc], 1.0, None, ALU.add)
            P.tt("dve", A, A, ng[:, i, :], ALU.mult)
            P.copy("dve", B, mt[:, 3 * i, :, xc])
            P.ts("dve", G, mt[:, 3 * i + 2, :, xc], 0.5 if i != 1 else 1.0, None, ALU.mult)
            row[bool(xc)] = (A, B, G)
        mods.append(row)
    return mods


def ffn_wF(dr, l, i):
    wg = dr.inp("ffn_w_gate", [2, 2, 1024, 2816])
    wu = dr.inp("ffn_w_up", [2, 2, 1024, 2816])
    wd = dr.inp("ffn_w_down", [2, 2, 2816, 1024])
    return (wg.ap[l, i].rearrange("(k p) c -> p k c", p=128), wu.ap[l, i].rearrange("(k p) c -> p k c", p=128),
            wd.ap[l, i].rearrange("(j p) c -> p j c", p=128))


def scratch(dr):
    D_ = {}
    D_["XA"] = dr.tmp("XA", [1024, NK])
    D_["XC"] = dr.tmp("XC", [1024, NK])
    D_["QT"] = dr.tmp("QT", [8, 96, NK], BF16)
    D_["KT"] = dr.tmp("KT", [8, 96, NK], BF16)
    D_["V"] = dr.tmp("V", [NK, 512], BF16)
    D_["MQ"] = dr.tmp("MQ", [NK, 512], BF16)
    D_["MK"] = dr.tmp("MK", [NK, 512], BF16)
    D_["MV"] = dr.tmp("MV", [NK, 512], BF16)
    D_["OT"] = dr.tmp("OT", [512, NK])
    D_["GT"] = dr.tmp("GT", [NK, 16])
    D_["attT"] = dr.tmp("attT", [512, NK])
    D_["HT"] = dr.tmp("HT", [512, NK])
    D_["xrT"] = dr.tmp("xrT", [1024, NK])
    D_["ggT"] = dr.tmp("ggT", [1024, NK])
    D_["hf"] = dr.tmp("hf", [1024, 4 * NTL])
    D_["hb"] = dr.tmp("hb", [1024, 4 * NTL])
    return D_


def phase_A(P, dr, seg):
    S = scratch(dr)
    C = setup_seg(P, seg)
    xin = dr.inp("xT", [1024, NK])
    mods = load_modsF(P, dr, C, 0)
    fw = ffn_wF(dr, 0, 0)
    w_in = dr.inp("w_in", [1024, 2736])
    w_uq = dr.inp("w_uq", [384, 768])
    w_ukk = dr.inp("w_ukk", [256, 768])
    w_ukv = dr.inp("w_ukv", [256, 512])
    cqg_d = dr.inp("cqg", [128, 3])
    ckvg_d = dr.inp("ckvg", [128, 2])
    c96_d = dr.inp("c96", [96, 3])
    oblk_d = dr.inp("oblk", [96, 96])
    rm_d = dr.inp("rm", [96, 96])
    e32_d = dr.inp("e32", [32, 96])
    cos_d = dr.inp("cosT", [96, NK])
    sin_d = dr.inp("sinT", [96, NK])
    xo, QT, KT, Vo, MQ, MK, MV, OT, GT = (S[k] for k in ("XA", "QT", "KT", "V", "MQ", "MK", "MV", "OT", "GT"))

    load_Xg(P, C, xin)
    ffn(P, C, fw[0], fw[1], fw[2], mods[0])
    store_Xg(P, C, xo)
    for bi in range(len(C.blocks)):
        isc = C.blocks[bi][2]
        norm_mod(P, C, bi, mods[1][isc][0], mods[1][isc][1])
    P.barrier()
    wslab = C.wh
    win = T(wslab[:, :, :].rearrange("p s c -> p (s c)")[:, 0:8 * 2736].rearrange("p (k c) -> p k c", k=8),
            [b for t in C.wslots for b in t.bufs])
    winv = w_in.ap.rearrange("(k p) c -> p k c", p=128)
    for k in range(8):
        P.dma("pool", win[:, k, :], T(winv[:, k, :], WDRAM), sem="win")
    wuq = wload(P, "wuq", w_uq.ap.rearrange("(k p) c -> p k c", p=128), [128, 3, 768])
    wukk = wload(P, "wukk", w_ukk.ap.rearrange("(k p) c -> p k c", p=128), [128, 2, 768])
    wukv = wload(P, "wukv", w_ukv.ap.rearrange("(k p) c -> p k c", p=128), [128, 2, 512])
    cqg = wload(P, "cqg", cqg_d.ap, [128, 3], F32)
    ckvg = wload(P, "ckvg", ckvg_d.ap, [128, 2], F32)
    c96 = wload(P, "c96", c96_d.ap, [96, 3], F32)
    oblk = wload(P, "oblk", oblk_d.ap, [96, 96])
    rm = wload(P, "rm", rm_d.ap, [96, 96], F32)
    e32 = wload(P, "e32", e32_d.ap, [32, 96])
    xh = C.xh
    XS = xh[:, :, :].rearrange("p k t -> p (k t)")
    off = [0]

    def xs(name, parts, n):
        a = XS[0:parts, off[0]:off[0] + n]
        off[0] += n
        return T(a, Buf(name))

    cq = [xs(f"cq{k}", 128, 512) for k in range(3)]
    ckv = [xs(f"ckv{k}", 128, 512) for k in range(2)]
    cosb = xs("cosb", 96, 512)
    sinb = xs("sinb", 96, 512)
    qn = [xs(f"qn{i}", 96, 512) for i in range(2)]
    t1 = [xs(f"t1{i}", 96, 512) for i in range(2)]
    t2 = [xs(f"t2{i}", 96, 512) for i in range(2)]
    rr = [xs(f"rr{i}", 128, 512) for i in range(2)]
    otile = [xs(f"ot{i}", 128, 512) for i in range(2)]
    gts = [xs(f"gts{i}", 128, 16) for i in range(2)]
    cqn = [sbt(P, f"cqn{k}", [128, 512], BF16) for k in range(3)]
    ckvn = [sbt(P, f"ckvn{k}", [128, 512], BF16) for k in range(2)]
    krr = sbt(P, "krr", [32, 512], BF16)
    sqb = [sbt(P, f"sqb{i}", [128, 512], BF16) for i in range(2)]
    qo = [sbt(P, f"qo{i}", [96, 512], BF16) for i in range(2)]
    tok = [sbt(P, f"tok{i}", [128, 512], BF16) for i in range(3)]
    ps = C.ps
    cnt = [0]

    def nxt():
        cnt[0] += 1
        return cnt[0]

    def rstd_from(pss, n, parts, invn, out):
        P.ts("dve", out[:parts, :n], pss[:parts, :n], invn, EPS, ALU.mult, ALU.add)
        P.act(out[:parts, :n], out[:parts, :n], AF.Sqrt)
        P.recip(out[:parts, :n], out[:parts, :n])

    def lowrank_norm(raw, nk, gains, outs, n):
        pss = ps[6]
        for k in range(nk):
            q = sqb[k % 2]
            P.act(q[:, :n], raw[k][:, :n], AF.Square)
            P.mm(pss[:, :n], C.ones_bf, q[:, :n], start=(k == 0), stop=(k == nk - 1))
        r = rr[nxt() % 2]
        rstd_from(pss, n, 128, 1.0 / (nk * 128), r)
        for k in range(nk):
            t = C.tmpf[k % 3]
            P.tt("dve", t[:, :n], raw[k][:, :n], r[:, :n], ALU.mult)
            P.act(outs[k][:, :n], t[:, :n], AF.Identity, scale=gains[:, k:k + 1])

    def head_norm_rope(praw, n, gcol, dst):
        i = nxt()
        sq = sqb[i % 2]
        P.act(sq[:96, :n], praw[:96, :n], AF.Square)
        pss = ps[6]
        P.mm(pss[:96, :n], oblk, sq[:96, :n])
        r = rr[i % 2]
        rstd_from(pss, n, 96, c96[:, 2:3], r)
        q = qn[i % 2]
        P.tt("dve", q[:, :n], praw[:96, :n], r[:96, :n], ALU.mult)
        P.act(q[:, :n], q[:, :n], AF.Identity, scale=c96[:, gcol:gcol + 1])
        prot = ps[7]
        P.mm(prot[:96, :n], rm, q[:, :n])
        a = t1[i % 2]
        b = t2[i % 2]
        P.tt("pool", a[:, :n], q[:, :n], cosb[:, :n], ALU.mult)
        P.tt("dve", b[:, :n], prot[:96, :n], sinb[:, :n], ALU.mult)
        o = qo[i % 2]
        P.tt("dve", o[:, :n], a[:, :n], b[:, :n], ALU.add)
        P.dma("sp", dst, o[:, :n], sem="qk")

    COL = dict(cq=0, ckv=384, kr=640, mq=672, mk=1184, mv=1696, mo=2208, mg=2720)
    for bi, (s, n, isc) in enumerate(C.blocks):
        g = C.gcol[bi]
        H = C.H[bi]
        P.dma("sp", cosb[:, :n], cos_d.with_ap(cos_d.ap[:, g:g + n]), sem="cs")
        P.dma("sp", sinb[:, :n], sin_d.with_ap(sin_d.ap[:, g:g + n]), sem="cs")
        for j in range(3):
            pp = ps[j % 2]
            for k in range(8):
                P.mm(pp[:, :n], win[:, k, COL["cq"] + j * 128:COL["cq"] + (j + 1) * 128], H[k], start=(k == 0), stop=(k == 7))
            P.copy("act", cq[j][:, :n], pp[:, :n])
        for j in range(2):
            pp = ps[j % 2]
            for k in range(8):
                P.mm(pp[:, :n], win[:, k, COL["ckv"] + j * 128:COL["ckv"] + (j + 1) * 128], H[k], start=(k == 0), stop=(k == 7))
            P.copy("act", ckv[j][:, :n], pp[:, :n])
        pp = ps[2]
        for k in range(8):
            P.mm(pp[:32, :n], win[:, k, COL["kr"]:COL["kr"] + 32], H[k], start=(k == 0), stop=(k == 7))
        P.copy("act", krr[:, :n], pp[:32, :n])
        lowrank_norm(cq, 3, cqg, cqn, n)
        lowrank_norm(ckv, 2, ckvg, ckvn, n)
        for j in range(4):
            pp = ps[j % 2]
            for k in range(8):
                P.mm(pp[:, :n], win[:, k, COL["mo"] + j * 128:COL["mo"] + (j + 1) * 128], H[k], start=(k == 0), stop=(k == 7))
            ot = otile[j % 2]
            P.copy("act", ot[:, :n], pp[:, :n])
            P.dma("sp", OT.with_ap(OT.ap[j * 128:(j + 1) * 128, g:g + n]), ot[:, :n], sem="oo")
        for tt_ in range(n // 128):
            ts_ = slice(tt_ * 128, (tt_ + 1) * 128)
            rows = slice(g + tt_ * 128, g + (tt_ + 1) * 128)
            for nm, dst in (("mq", MQ), ("mk", MK), ("mv", MV)):
                pp = ps[3 + nxt() % 2]
                for k in range(8):
                    P.mm(pp[:, :], H[k][:, ts_], win[:, k, COL[nm]:COL[nm] + 512], start=(k == 0), stop=(k == 7))
                tk = tok[nxt() % 3]
                P.copy("act", tk, pp)
                P.dma("sp", dst.with_ap(dst.ap[rows, :]), tk, sem="tok")
            pp = ps[5]
            for k in range(8):
                P.mm(pp[:, 0:16], H[k][:, ts_], win[:, k, COL["mg"]:COL["mg"] + 16], start=(k == 0), stop=(k == 7))
            gg_ = gts[nxt() % 2]
            P.copy("act", gg_, pp[:, 0:16])
            P.dma("sp", GT.with_ap(GT.ap[rows, :]), gg_, sem="tok")
            pp = ps[3 + nxt() % 2]
            for k in range(2):
                P.mm(pp[:, :], ckvn[k][:, ts_], wukv[:, k, :], start=(k == 0), stop=(k == 1))
            tk = tok[nxt() % 3]
            P.copy("act", tk, pp)
            P.dma("sp", Vo.with_ap(Vo.ap[rows, :]), tk, sem="tok")
        for h in range(8):
            pp = ps[h % 2]
            for k in range(3):
                P.mm(pp[:96, :n], wuq[:, k, h * 96:(h + 1) * 96], cqn[k][:, :n], start=(k == 0), stop=(k == 2))
            head_norm_rope(pp, n, 0, QT.with_ap(QT.ap[h, :, g:g + n]))
            pp = ps[2 + h % 2]
            for k in range(2):
                P.mm(pp[:96, :n], wukk[:, k, h * 96:(h + 1) * 96], ckvn[k][:, :n], start=(k == 0), stop=False)
            P.mm(pp[:96, :n], e32, krr[:, :n], start=False, stop=True)
            head_norm_rope(pp, n, 1, KT.with_ap(KT.ap[h, :, g:g + n]))


def phase_B(P, dr):
    S = scratch(dr)
    QT, KT, Va, MQd, MKd, MVd, GT, attT, HT = (S[k] for k in ("QT", "KT", "V", "MQ", "MK", "MV", "GT", "attT", "HT"))
    gb_d = dr.inp("gb16", [128, 16])
    triF_d = dr.inp("triF", [128, 128])
    triB_d = dr.inp("triB", [128, 128])
    id_d = dr.inp("ident", [128, 128])
    blocks = [(NTC + s, 512, False) for s in range(0, 4 * NTL, 512)] + [(0, NTC, True)]
    psS = psplit(P, "psS", F32, 512, 2)
    psO = psplit(P, "psO", F32, 512, 2)
    psT = psplit(P, "psT", BF16, 512, 2)
    psG = psplit(P, "psG", F32, 128, 4)
    psD = psplit(P, "psD", F32, 256, 2)
    psN = psplit(P, "psN", F32, 256, 2)
    ones_bf = sbt(P, "ones_bf", [128, 128], BF16)
    P.memset("pool", ones_bf, 1.0)
    ones_f = sbt(P, "ones_f", [128, 128], F32)
    P.memset("pool", ones_f, 1.0)
    triF = wload(P, "triF", triF_d.ap, [128, 128], F32)
    triB = wload(P, "triB", triB_d.ap, [128, 128], F32)
    ident = wload(P, "ident", id_d.ap, [128, 128], BF16)
    gb16 = wload(P, "gb16", gb_d.ap, [128, 16], F32)
    tri = [triF, triB]
    kth = [sbt(P, f"kth{i}", [96, NK], BF16) for i in range(2)]
    vah = [sbt(P, f"vah{i}", [128, NCH, 128], BF16) for i in range(2)]
    qh1 = sbt(P, "qh", [96, NK], BF16)
    qh = [qh1, qh1]
    for i in range(2):
        P.memset("pool", vah[i][:, :, 64:128], 1.0)
    Et = [sbt(P, f"Et{i}", [128, 512], BF16) for i in range(3)]
    rc = [sbt(P, f"rc{i}", [128, 512], F32) for i in range(2)]
    ob = [sbt(P, f"ob{i}", [64, 512], F32) for i in range(2)]
    vv = Va.ap.rearrange("(t p) c -> p t c", p=128)
    SC = 1.0 / np.sqrt(96.0)

    def load_head(h):
        i = h % 2
        P.dma("sp", kth[i], KT.with_ap(KT.ap[h]), sem=f"kh{i}")
        for c0 in range(0, NCH, 6):
            P.dma("sp", vah[i][:, c0:c0 + 6, 0:64], Va.with_ap(vv[:, c0:c0 + 6, h * 64:(h + 1) * 64]), sem=f"kh{i}")

    def att_gen():
        work = []
        for h in range(8):
            for (g, n, isc) in blocks:
                tiles = list(range(NTC // 128) if isc else range(NCH))
                for kt in tiles:
                    work.append((h, g, n, kt, kt == tiles[0], kt == tiles[-1]))
        load_head(0)
        nb = 0
        pend = None
        for it, (h, g, n, kt, first, last) in enumerate(work):
            i = h % 2
            if first and g == blocks[0][0]:
                if pend is not None:
                    pend()
                    pend = None
                P.dma("sp", qh1, QT.with_ap(QT.ap[h]), sem="qh")
                if h + 1 < 8:
                    load_head(h + 1)
            pS = psS[it % 2]
            P.mm(pS[:, :n], kth[i][:, kt * 128:(kt + 1) * 128], qh1[:, g:g + n])
            E = Et[it % 3]
            P.act(E[:, :n], pS[:, :n], AF.Exp, scale=SC)
            if pend is not None:
                pend()
            def fin(h=h, g=g, n=n, kt=kt, first=first, last=last, i=i, E=E):
                nonlocal nb
                po = psO[nb % 2]
                P.mm(po[:, :n], vah[i][:, kt, :], E[:, :n], start=first, stop=last)
                if last:
                    r = rc[nb % 2]
                    P.recip(r[64:128, :n], po[64:128, :n])
                    o = ob[nb % 2]
                    P.tt("dve", o[:, :n], po[0:64, :n], r[64:128, :n], ALU.mult)
                    P.dma("sp", attT.with_ap(attT.ap[h * 64:(h + 1) * 64, g:g + n]), o[:, :n], sem="ao")
                    nb += 1
            pend = fin
            yield
        pend()

    gt = sbt(P, "gt", [128, NCH, 4], F32)
    mq = sbt(P, "mq", [128, NCH, 128], BF16)
    mk = sbt(P, "mk", [128, NCH, 128], BF16)
    mv = sbt(P, "mv", [128, NCH, 128], BF16)
    HS = sbt(P, "HS", [128, NK], F32)
    HSb = [Buf(f"HS{c}") for c in range(NCH)]
    lf = [sbt(P, f"lf{d}", [128, NCH], F32) for d in range(2)]
    EQ = [sbt(P, f"eq{d}", [128, NCH], F32) for d in range(2)]
    EK = [sbt(P, f"ek{d}", [128, NCH], F32) for d in range(2)]
    EL = [sbt(P, f"el{d}", [128, NCH], F32) for d in range(2)]
    CN = [sbt(P, f"CN{d}", [128, 256], F32) for d in range(2)]
    CNb = [sbt(P, f"CNb{d}", [128, 256], BF16) for d in range(2)]
    qs = [[sbt(P, f"qs{d}{i}", [128, 128], BF16) for i in range(2)] for d in range(2)]
    ks = [[sbt(P, f"ks{d}{i}", [128, 128], BF16) for i in range(2)] for d in range(2)]
    qkT = [[sbt(P, f"qkT{d}{i}", [128, 256], BF16) for i in range(2)] for d in range(2)]
    Sm = [[sbt(P, f"Sm{d}{i}", [128, 128], BF16) for i in range(2)] for d in range(2)]
    dn = [sbt(P, f"dn{d}", [128, 128], F32) for d in range(2)]
    hc = [sbt(P, f"hc{d}", [128, 128], F32) for d in range(2)]
    tmpc = [sbt(P, f"tmpc{d}", [128, 256], F32) for d in range(2)]
    order = [list(range(NCH)), [1, 0] + list(range(NCH - 1, 1, -1))]
    g4v = GT.ap.rearrange("(c p) (g h) -> p c g h", p=128, h=4)

    def ml_setup(h):
        for c0 in range(0, NCH, 6):
            for g_ in range(4):
                P.dma("sp", gt[:, c0:c0 + 6, g_], GT.with_ap(g4v[:, c0:c0 + 6, g_, h]), sem="mlc", allow_slow_non_contiguous=True)
        for g in range(4):
            P.ts("dve", gt[:, :, g], gt[:, :, g], gb16[:, g * 4 + h:g * 4 + h + 1], None, ALU.add)
        for t_, d_ in ((mq, MQd), (mk, MKd), (mv, MVd)):
            dv_ = d_.ap.rearrange("(c p) d -> p c d", p=128)
            for c0 in range(0, NCH, 6):
                P.dma("sp", t_[:, c0:c0 + 6, :], d_.with_ap(dv_[:, c0:c0 + 6, h * 128:(h + 1) * 128]), sem="mlc")
        for d in range(2):
            P.act(lf[d], gt[:, :, 2 * d + 1], AF.Exp, scale=-1.0)
            P.ts("dve", lf[d], lf[d], 1.0, None, ALU.add)
            P.act(lf[d], lf[d], AF.Ln)
            P.ts("dve", lf[d], lf[d], -1.0, None, ALU.mult)
            pB = psG[2]
            pTt = psG[3]
            P.mm(pB[:, :NCH], tri[d], lf[d])
            P.mm(pTt[:, :NCH], ones_f, lf[d])
            P.act(EQ[d], pB[:, :NCH], AF.Exp)
            P.tt("dve", EK[d], gt[:, :, 2 * d], pB[:, :NCH], ALU.subtract)
            P.act(EK[d], EK[d], AF.Exp)
            P.ts("dve", EK[d], EK[d], float(128.0 ** -0.5), None, ALU.mult)
            P.act(EL[d], pTt[:, :NCH], AF.Exp)
            P.memset("pool", CN[d], 0.0)
            P.memset("pool", CNb[d], 0.0)

    def pre(d, c, i):
        P.ts("dve", qs[d][i], mq[:, c, :], EQ[d][:, c:c + 1], None, ALU.mult)
        P.act(ks[d][i], mk[:, c, :], AF.Identity, scale=EK[d][:, c:c + 1])
        yield
        pT = psT[d]
        P.transpose(pT[:, 0:128], qs[d][i], ident)
        P.transpose(pT[:, 128:256], ks[d][i], ident)
        yield
        P.copy("act", qkT[d][i], pT[:, 0:256])
        yield
        pS = psG[d]
        P.mm(pS, qkT[d][i][:, 128:256], qkT[d][i][:, 0:128])
        yield
        P.tt("dve", Sm[d][i], pS, tri[d], ALU.mult)
        yield

    def post(h, written, d, c, i):
        pD = psD[d]
        P.mm(pD[:, 0:128], ks[d][i], mv[:, c, :])
        P.mm(pD[:, 128:256], ks[d][i], ones_bf)
        pN = psN[d]
        P.mm(pN[:, 0:128], mv[:, c, :], Sm[d][i], start=True, stop=False)
        P.mm(pN[:, 0:128], CNb[d][:, 0:128], qkT[d][i][:, 0:128], start=False, stop=True)
        P.mm(pN[:, 128:256], ones_bf, Sm[d][i], start=True, stop=False)
        P.mm(pN[:, 128:256], CNb[d][:, 128:256], qkT[d][i][:, 0:128], start=False, stop=True)
        yield
        P.act(dn[d], pN[:, 128:256], AF.Abs)
        P.tt("dve", tmpc[d], CN[d], psD[d], ALU.add)
        P.ts("dve", CN[d], tmpc[d], EL[d][:, c:c + 1], None, ALU.mult)
        P.copy("act", CNb[d], CN[d])
        P.ts("dve", dn[d], dn[d], 1.0, None, ALU.max)
        P.recip(dn[d], dn[d])
        hsl = T(HS.ap[:, c * 128:(c + 1) * 128], HSb[c])
        if c in written:
            P.tt("dve", hc[d], pN[:, 0:128], dn[d], ALU.mult)
            P.tt("pool", hsl, hsl, hc[d], ALU.add)
            P.dma("sp", HT.with_ap(HT.ap[h * 128:(h + 1) * 128, c * 128:(c + 1) * 128]), hsl, sem="ho")
        else:
            P.tt("dve", hsl, pN[:, 0:128], dn[d], ALU.mult)
            written.add(c)
        yield

    def ml_gen():
        for h in range(4):
            ml_setup(h)
            yield
            written = set()
            for d in range(2):
                yield from pre(d, order[d][0], 0)
            for st in range(NCH):
                for d in range(2):
                    if st + 1 < NCH:
                        yield from pre(d, order[d][st + 1], (st + 1) % 2)
                    yield from post(h, written, d, order[d][st], st % 2)

    a = att_gen()
    m = ml_gen()
    for i, _ in enumerate(a):
        if i % 2 == 1:
            next(m, None)
    for _ in m:
        pass


def phase_C(P, dr, seg):
    S = scratch(dr)
    C = setup_seg(P, seg)
    xin, attT, hmT, oT, xo, ggT, xrT = (S[k] for k in ("XA", "attT", "HT", "OT", "XC", "ggT", "xrT"))
    wout_d = dr.inp("even_w_out", [1024, 1024])
    outg_d = dr.inp("outg", [128, 4])
    mods0 = load_modsF(P, dr, C, 0)
    mods1 = load_modsF(P, dr, C, 1)
    fw02 = ffn_wF(dr, 0, 1)
    fw11 = ffn_wF(dr, 1, 0)
    win_d = dr.inp("odd_w_in", [1024, 2048])
    load_Xg(P, C, xin)
    wslab = C.wh[:, :, :].rearrange("p s c -> p (s c)")
    allw = [b for t in C.wslots for b in t.bufs]
    wout = T(wslab[:, 0:8192].rearrange("p (k c) -> p k c", k=8), allw)
    wv = wout_d.ap.rearrange("(k p) c -> p k c", p=128)
    for k in range(8):
        P.dma("pool", wout[:, k, :], T(wv[:, k, :], WDRAM), sem="win")
    outg = wload(P, "outg", outg_d.ap, [128, 4], F32)
    mixb = [sbt(P, f"mix{k}", [128, 512], BF16) for k in range(8)]
    hmt = [sbt(P, f"hmt{i}", [128, 512], F32) for i in range(2)]
    ott = [sbt(P, f"ott{i}", [128, 512], F32) for i in range(2)]
    rr = [sbt(P, f"rrc{i}", [128, 512], F32) for i in range(2)]

    def mixf(bi):
        s, n, isc = C.blocks[bi]
        g = C.gcol[bi]
        for j in range(4):
            P.dma("pool", mixb[j][:, :n], attT.with_ap(attT.ap[j * 128:(j + 1) * 128, g:g + n]), sem="mx")
            hm = hmt[j % 2]
            ot = ott[j % 2]
            P.dma("sp", hm[:, :n], hmT.with_ap(hmT.ap[j * 128:(j + 1) * 128, g:g + n]), sem="mx2")
            P.dma("sp", ot[:, :n], oT.with_ap(oT.ap[j * 128:(j + 1) * 128, g:g + n]), sem="mx2")
            sq = C.sq[j % 2]
            P.act(sq[:, :n], hm[:, :n], AF.Square)
            pss = C.ps[6]
            P.mm(pss[:, :n], C.ones_bf, sq[:, :n])
            r = rr[j % 2]
            P.ts("dve", r[:, :n], pss[:, :n], 1.0 / 128, EPS, ALU.mult, ALU.add)
            P.act(r[:, :n], r[:, :n], AF.Sqrt)
            P.recip(r[:, :n], r[:, :n])
            P.tt("dve", hm[:, :n], hm[:, :n], r[:, :n], ALU.mult)
            P.act(ot[:, :n], ot[:, :n], AF.Sigmoid)
            P.stt("dve", mixb[4 + j][:, :n], ot[:, :n], outg[:, j:j + 1], hm[:, :n], ALU.mult, ALU.mult)
        return mixb

    mix_out(P, C, wout, mixf, mods0[1])
    P.barrier()
    ffn(P, C, fw02[0], fw02[1], fw02[2], mods0[2])
    ffn(P, C, fw11[0], fw11[1], fw11[2], mods1[0])
    store_Xg(P, C, xo, latent_only=True)
    for bi in range(len(C.blocks)):
        isc = C.blocks[bi][2]
        norm_mod(P, C, bi, mods1[1][isc][0], mods1[1][isc][1])
    P.barrier()
    win = T(wslab[:, 0:8 * 2048].rearrange("p (k c) -> p k c", k=8), allw)
    wv = win_d.ap.rearrange("(k p) c -> p k c", p=128)
    for k in range(8):
        P.dma("pool", win[:, k, :], T(wv[:, k, :], WDRAM), sem="win")
    ga = hmt
    gb_ = ott
    it = 0
    for bi, (s, n, isc) in enumerate(C.blocks):
        g = C.gcol[bi]
        H = C.H[bi]
        if not isc:
            for j in range(8):
                pp = C.ps[j % 2]
                for k in range(8):
                    P.mm(pp[:, :n], win[:, k, j * 128:(j + 1) * 128], H[k], start=(k == 0), stop=(k == 7))
                a = ga[it % 2]
                b = gb_[it % 2]
                it += 1
                P.act(a[:, :n], pp[:, :n], AF.Square)
                P.ts("dve", a[:, :n], a[:, :n], 0.044715, 1.0, ALU.mult, ALU.add)
                P.tt("dve", a[:, :n], a[:, :n], pp[:, :n], ALU.mult)
                P.act(b[:, :n], a[:, :n], AF.Sigmoid, scale=1.5957691216057308)
                P.tt("dve", b[:, :n], b[:, :n], pp[:, :n], ALU.mult)
                P.dma("sp", ggT.with_ap(ggT.ap[j * 128:(j + 1) * 128, g:g + n]), b[:, :n], sem="go")
        for j in range(8):
            pp = C.ps[2 + j % 2]
            for k in range(8):
                P.mm(pp[:, :n], win[:, k, 1024 + j * 128:1024 + (j + 1) * 128], H[k], start=(k == 0), stop=(k == 7))
            a = ga[it % 2]
            it += 1
            P.copy("act", a[:, :n], pp[:, :n])
            P.dma("sp", xrT.with_ap(xrT.ap[j * 128:(j + 1) * 128, g:g + n]), a[:, :n], sem="go")


def phase_D(P, dr):
    S = scratch(dr)
    xrT, hf, hb = S["xrT"], S["hf"], S["hb"]
    cw_d = dr.inp("cw", [8, 128, 4])
    cb_d = dr.inp("cb", [8, 128, 1])
    wa_d = dr.inp("lru_w_a", [1, 2, 8, 128, 128])
    wx_d = dr.inp("lru_w_x", [1, 2, 8, 128, 128])
    bab_d = dr.inp("bab", [8, 128, 6])
    psA = psplit(P, "psA", F32, 512, 2)
    psX = psplit(P, "psX", F32, 512, 2)
    ub = [sbt(P, f"ub{i}", [128, TS + 4], F32) for i in range(2)]
    xc = [sbt(P, f"xc{i}", [128, TS], F32) for i in range(2)]
    rt = sbt(P, "rt", [128, TS], F32)
    ig = sbt(P, "ig", [128, TS], F32)
    at = sbt(P, "at", [128, TS], F32)
    om = sbt(P, "om", [128, TS], F32)
    ip = sbt(P, "ip", [128, TS], F32)
    ht = [sbt(P, f"ht{i}", [128, TS], F32) for i in range(2)]
    cw = sbt(P, "cw", [128, 4], F32)
    cb = sbt(P, "cb", [128, 1], F32)
    wa = [sbt(P, f"wa{d}", [128, 128], F32) for d in range(2)]
    wx = [sbt(P, f"wx{d}", [128, 128], F32) for d in range(2)]
    bab = sbt(P, "bab", [128, 6], F32)
    cc = sbt(P, "cc", [128, 2], F32)
    it = 0
    offs = [j - 2 for j in range(4)]
    for nb in range(8):
        rows = slice(nb * 128, (nb + 1) * 128)
        P.dma("sp", cw, cw_d.with_ap(cw_d.ap[nb]), sem="dc")
        P.dma("sp", cb, cb_d.with_ap(cb_d.ap[nb]), sem="dc")
        P.dma("sp", bab, bab_d.with_ap(bab_d.ap[nb]), sem="dc")
        for d in range(2):
            P.dma("sp", wa[d], wa_d.with_ap(wa_d.ap[0, d, nb]), sem="dc")
            P.dma("sp", wx[d], wx_d.with_ap(wx_d.ap[0, d, nb]), sem="dc")
        P.act(cc, bab[:, 4:6], AF.Exp, scale=-1.0)
        P.ts("dve", cc, cc, 1.0, None, ALU.add)
        P.act(cc, cc, AF.Ln)
        P.ts("dve", cc, cc, -8.0, None, ALU.mult)
        for d in range(2):
            dst = hf if d == 0 else hb
            lat = [(NTC, 4 * NTL, c0) for c0 in range(0, 4 * NTL, TS)]
            if d == 1:
                lat = lat[::-1]
            tiles = [(0, NTC, 0)] + lat
            prev = None
            for (seg0, segn, c0) in tiles:
                n = min(TS, segn - c0)
                u = ub[it % 2]
                x = xc[it % 2]
                h = ht[it % 2]
                it += 1
                lo = max(c0 - 2, 0)
                hi = min(c0 + n + 2, segn)
                if lo > c0 - 2:
                    P.memset("pool", u[:, 0:2], 0.0)
                if hi < c0 + n + 2:
                    P.memset("pool", u[:, n + 2:n + 4], 0.0)
                P.dma("sp", u[:, 2 + (lo - c0):2 + (hi - c0)], xrT.with_ap(xrT.ap[rows, seg0 + lo:seg0 + hi]), sem=f"u{it % 2}")
                P.act(x[:, :n], u[:, 2 + offs[0]:2 + offs[0] + n], AF.Identity, bias=cb[:, 0:1], scale=cw[:, 0:1])
                for j in range(1, 4):
                    P.stt("dve", x[:, :n], u[:, 2 + offs[j]:2 + offs[j] + n], cw[:, j:j + 1], x[:, :n], ALU.mult, ALU.add)
                for q0 in range(0, n, 512):
                    qn_ = min(512, n - q0)
                    pa = psA[(q0 // 512) % 2]
                    px = psX[(q0 // 512) % 2]
                    P.mm(pa[:, :qn_], wa[d], x[:, q0:q0 + qn_])
                    P.act(rt[:, q0:q0 + qn_], pa[:, :qn_], AF.Sigmoid, bias=bab[:, d:d + 1])
                    P.mm(px[:, :qn_], wx[d], x[:, q0:q0 + qn_])
                    P.act(ig[:, q0:q0 + qn_], px[:, :qn_], AF.Sigmoid, bias=bab[:, 2 + d:3 + d])
                P.act(at[:, :n], rt[:, :n], AF.Exp, scale=cc[:, d:d + 1])
                P.tt("pool", om[:, :n], at[:, :n], at[:, :n], ALU.mult)
                P.ts("dve", om[:, :n], om[:, :n], -1.0, 1.0, ALU.mult, ALU.add)
                P.act(om[:, :n], om[:, :n], AF.Sqrt)
                P.tt("pool", ip[:, :n], ig[:, :n], x[:, :n], ALU.mult)
                P.tt("dve", ip[:, :n], ip[:, :n], om[:, :n], ALU.mult)
                init = 0.0 if prev is None else prev
                if d == 0:
                    P.scan(h[:, :n], at[:, :n], ip[:, :n], init, ALU.mult, ALU.add)
                    prev = h[:, n - 1:n]
                else:
                    P.scan(h.with_ap(h.ap[:, :n][:, ::-1]), at.with_ap(at.ap[:, :n][:, ::-1]),
                           ip.with_ap(ip.ap[:, :n][:, ::-1]), init, ALU.mult, ALU.add)
                    prev = h[:, 0:1]
                if seg0 > 0:
                    P.dma("sp", dst.with_ap(dst.ap[rows, c0:c0 + n]), h[:, :n], sem="ho")


def phase_E(P, dr, seg):
    S = scratch(dr)
    C = setup_common(P, NTL, 0)
    C.gcol = [NTC + seg * NTL + s for s in range(0, NTL, 512)]
    xin, ggT, hfT, hbT = S["XC"], S["ggT"], S["hf"], S["hb"]
    wout_d = dr.inp("odd_w_out", [1024, 1024])
    mods1 = load_modsF(P, dr, C, 1)
    fw12 = ffn_wF(dr, 1, 1)
    xo = dr.out("out", [1024, 4 * NTL])
    load_Xg(P, C, xin)
    wslab = C.wh[:, :, :].rearrange("p s c -> p (s c)")
    allw = [b for t in C.wslots for b in t.bufs]
    wout = T(wslab[:, 0:8192].rearrange("p (k c) -> p k c", k=8), allw)
    wv = wout_d.ap.rearrange("(k p) c -> p k c", p=128)
    for k in range(8):
        P.dma("pool", wout[:, k, :], T(wv[:, k, :], WDRAM), sem="win")
    mixb = [sbt(P, f"mix{k}", [128, 512], BF16) for k in range(8)]
    ta = [sbt(P, f"ta{i}", [128, 512], F32) for i in range(2)]
    tb = [sbt(P, f"tb{i}", [128, 512], F32) for i in range(2)]
    tg = [sbt(P, f"tg{i}", [128, 512], F32) for i in range(2)]

    def mixf(bi):
        s, n, isc = C.blocks[bi]
        g = C.gcol[bi]
        for k in range(8):
            a, b_, gg_ = ta[k % 2], tb[k % 2], tg[k % 2]
            rows = slice(k * 128, (k + 1) * 128)
            P.dma("sp", a[:, :n], hfT.with_ap(hfT.ap[rows, g - NTC:g - NTC + n]), sem="mx")
            P.dma("sp", b_[:, :n], hbT.with_ap(hbT.ap[rows, g - NTC:g - NTC + n]), sem="mx")
            P.dma("sp", gg_[:, :n], ggT.with_ap(ggT.ap[rows, g:g + n]), sem="mx")
            P.tt("pool", a[:, :n], a[:, :n], b_[:, :n], ALU.add)
            P.tt("dve", mixb[k][:, :n], a[:, :n], gg_[:, :n], ALU.mult)
        return mixb

    mix_out(P, C, wout, mixf, mods1[1])
    P.barrier()
    ffn(P, C, fw12[0], fw12[1], fw12[2], mods1[2])
    store_Xg(P, C, xo, goff=NTC)


PHASES = None


def build_fused(nc):
    with contextlib.ExitStack() as semstack:
        P = Prog(nc, semstack)
        dr = DR(nc)
        plan = [("M", None)] + [("A", s) for s in range(NSEG)] + [("B", None)] + \
               [("C", s) for s in range(NSEG)] + [("D", None)] + [("E", s) for s in range(NSEG)]
        for nm, seg in plan:
            if PHASES is not None and nm not in PHASES:
                continue
            with contextlib.ExitStack() as pstack:
                P.stack = pstack
                if nm == "M":
                    phase_M(P, dr)
                elif nm == "A":
                    phase_A(P, dr, seg)
                elif nm == "B":
                    phase_B(P, dr)
                elif nm == "C":
                    phase_C(P, dr, seg)
                elif nm == "D":
                    phase_D(P, dr)
                else:
                    phase_E(P, dr, seg)
                P.barrier()
                P.emit()


def rope_tables_g():
    pos = np.arange(4 * NTL)
    rows = (pos // 64).astype(np.float32)
    cols = (pos % 64).astype(np.float32)
    inv = (np.float32(10000.0) ** (-np.arange(0, 16, 2, dtype=np.float32) / np.float32(16))).astype(np.float32)
    cosT = np.ones((96, NK), np.float32)
    sinT = np.zeros((96, NK), np.float32)
    for a, pp in enumerate((rows, cols)):
        ang = (pp[None, :] * inv[:, None]).astype(np.float32)
        for half in range(2):
            r0 = 64 + a * 16 + half * 8
            cosT[r0:r0 + 8, NTC:] = np.cos(ang)
            sinT[r0:r0 + 8, NTC:] = np.sin(ang)
    return cosT, sinT


def maps_fused(inp):
    oblk, rm, e32 = consts_A()
    triF, triB, ident = consts_B()
    cosT, sinT = rope_tables_g()
    ukv = inp["mla_w_ukv"][0].reshape(256, 8, 128)
    w_ukk = np.zeros((256, 8, 96), np.float32)
    w_ukk[:, :, :64] = ukv[:, :, :64]
    w_ukk = w_ukk.reshape(256, 768)
    w_ukv = np.ascontiguousarray(ukv[:, :, 64:].reshape(256, 512))
    c96 = np.zeros((96, 3), np.float32)
    c96[:, 0] = inp["mla_q_g"][0]
    c96[:, 1] = inp["mla_k_g"][0]
    c96[:64, 2] = 1.0 / 64
    c96[64:, 2] = 1.0 / 32
    modb = np.ascontiguousarray(inp["mod_b"].reshape(2, 72, 128).transpose(2, 0, 1).reshape(128, 144))
    gb16 = np.ascontiguousarray(np.broadcast_to(inp["mlstm_gate_b"][0][None, :], (128, 16))).astype(np.float32)
    cw = np.ascontiguousarray(inp["odd_conv_w"][0].reshape(4, 8, 128).transpose(1, 2, 0))
    cb = np.ascontiguousarray(inp["odd_conv_b"][0].reshape(8, 128, 1))
    bab = np.stack([inp["lru_b_a"][0, 0], inp["lru_b_a"][0, 1], inp["lru_b_x"][0, 0],
                    inp["lru_b_x"][0, 1], inp["lru_lam"][0, 0], inp["lru_lam"][0, 1]], axis=1)
    bab = np.ascontiguousarray(bab.reshape(8, 128, 6).astype(np.float32))
    common = dict(mod_w=inp["mod_w"], modb=modb, normg0=normg_table(inp["norm_g"][0]), normg1=normg_table(inp["norm_g"][1]),
                  ffn_w_gate=inp["ffn_w_gate"], ffn_w_up=inp["ffn_w_up"], ffn_w_down=inp["ffn_w_down"],
                  w_in=inp["even_w_in"][0], w_uq=inp["mla_w_uq"][0], w_ukk=w_ukk, w_ukv=w_ukv,
                  cqg=fm(inp["mla_cq_g"][0]), ckvg=fm(inp["mla_ckv_g"][0]), c96=c96, oblk=oblk, rm=rm, e32=e32,
                  cosT=cosT, sinT=sinT, gb16=gb16, triF=triF, triB=triB, ident=ident,
                  even_w_out=inp["even_w_out"][0], outg=fm(inp["mlstm_out_g"][0]), odd_w_in=inp["odd_w_in"][0],
                  cw=cw, cb=cb, lru_w_a=inp["lru_w_a"], lru_w_x=inp["lru_w_x"], bab=bab, odd_w_out=inp["odd_w_out"][0])
    maps = []
    for r in range(8):
        b = r // 4
        xT = np.ascontiguousarray(np.concatenate([inp["ctx"][b].T, inp["x"][b].T], axis=1))
        vecs = np.stack([inp["c"][b], inp["c_ctx"]], 0)
        cv = np.ascontiguousarray(vecs.reshape(2, 8, 128).transpose(2, 1, 0).reshape(128, 16))
        mp = dict(common)
        mp["xT"] = xT
        mp["cv"] = cv
        maps.append(mp)
    return maps


def kernel(**inp):
    inp = {k: np.asarray(v) for k, v in inp.items()}
    nc = bass.Bass("TRN2", target_bir_lowering=False)
    build_fused(nc)
    res = run_bass_kernel_spmd(nc, maps_fused(inp), core_ids=list(range(8))).results
    out = np.zeros((2, 4 * NTL, 1024), np.float32)
    for b in range(2):
        out[b] = np.asarray(res[4 * b]["out"], np.float32).T
    return out
```
